# Optimizing a Trainium2 kernel written in Bass

```python
import math
import jax, jax.numpy as jnp
from jax import lax
import numpy as np

D_MODEL = 1024
BATCH = 8
SEQ = 8192
DEPTH = 1
DEC_BATCH = 16
DEC_SEQ = 4096
PAST_LEN = 128

HEAD_DIM = 64
N_GROUPS = 3
HEADS_PER_GROUP = 4
ATTN_WIDTH = N_GROUPS * HEADS_PER_GROUP * HEAD_DIM
ATTN_OUT_WIDTH = HEADS_PER_GROUP * HEAD_DIM
WINDOWS = (128, 512, 2048)
DILATIONS = (1, 4, 16)
ROT_DIM = HEAD_DIM // 4
ROPE_THETA = 500000.0
Q_BLOCK = 128
NEG_INF = -1e30
CONV_WIDTH = 512
CONV_KERNEL = 31
N_BRANCHES = 2
IN_COLS = 3 * ATTN_WIDTH + 2 * CONV_WIDTH + N_BRANCHES * D_MODEL
PEER_HEADS = 8
N_SUBKEYS = 128
N_EXPERTS = N_SUBKEYS * N_SUBKEYS
PEER_KEY_DIM = 256
PEER_TOPK = 16
TOKEN_BLOCK = 128
EPS = 1e-6

kernel_name = "hybrid_dilated_conv_peer_encoder"


def rms_norm(x, g):
    xf = x.astype(jnp.float32)
    y = xf * lax.rsqrt(jnp.mean(xf * xf, axis=-1, keepdims=True) + EPS)
    return (y * g.astype(jnp.float32)).astype(x.dtype)


def partial_rotary(x, pos):
    half = ROT_DIM // 2
    inv = ROPE_THETA ** (-jnp.arange(half, dtype=jnp.float32) * 2.0 / ROT_DIM)
    ang = pos.astype(jnp.float32)[:, None] * inv[None, :]
    cos = jnp.cos(ang)[None, :, None, None, :]
    sin = jnp.sin(ang)[None, :, None, None, :]
    xf = x.astype(jnp.float32)
    x1 = xf[..., :half]
    x2 = xf[..., half:ROT_DIM]
    out = jnp.concatenate([x1 * cos - x2 * sin, x1 * sin + x2 * cos, xf[..., ROT_DIM:]], axis=-1)
    return out.astype(x.dtype)


def dilation_offsets():
    return np.stack([d * np.arange(-(w // (2 * d)), w // (2 * d) + 1)
                     for w, d in zip(WINDOWS, DILATIONS)]).astype(np.int32)


def dilated_attention(q, k, v):
    B, S = q.shape[0], q.shape[1]
    offs = jnp.asarray(dilation_offsets())
    scale = HEAD_DIM ** -0.5
    gather = jax.vmap(lambda t, ix: jnp.take(t, ix, axis=1), in_axes=(2, 0), out_axes=2)

    def block(i):
        start = i * Q_BLOCK
        pos = start + jnp.arange(Q_BLOCK, dtype=jnp.int32)
        idx = pos[None, :, None] + offs[:, None, :]
        valid = (idx >= 0) & (idx < S)
        idxc = jnp.clip(idx, 0, S - 1)
        qb = lax.dynamic_slice_in_dim(q, start, Q_BLOCK, axis=1)
        kb = gather(k, idxc)
        vb = gather(v, idxc)
        s = jnp.einsum('bqghd,bqgjhd->bqghj', qb, kb).astype(jnp.float32) * scale
        mask = jnp.transpose(valid, (1, 0, 2))[None, :, :, None, :]
        s = jnp.where(mask, s, NEG_INF)
        m = jnp.max(s, axis=-1, keepdims=True)
        p = jnp.exp(s - m)
        den = jnp.sum(p, axis=-1)
        o = jnp.einsum('bqghj,bqgjhd->bqghd', p.astype(vb.dtype), vb).astype(jnp.float32) / den[..., None]
        lse = m[..., 0] + jnp.log(den)
        w = jax.nn.softmax(lse, axis=2)
        out = jnp.sum(w[..., None] * o, axis=2)
        return out.astype(q.dtype)

    out = lax.map(block, jnp.arange(S // Q_BLOCK))
    return jnp.transpose(out, (1, 0, 2, 3, 4)).reshape(B, S, ATTN_OUT_WIDTH)


def conv_module(a, b, dw_w, dw_b, ln_g, ln_b, pw_w, pw_b):
    u = a * jax.nn.sigmoid(b)
    u = lax.conv_general_dilated(
        u, dw_w[:, None, :].astype(u.dtype), window_strides=(1,),
        padding=[(CONV_KERNEL // 2, CONV_KERNEL // 2)],
        dimension_numbers=('NWC', 'WIO', 'NWC'),
        feature_group_count=CONV_WIDTH) + dw_b
    uf = u.astype(jnp.float32)
    mu = jnp.mean(uf, axis=-1, keepdims=True)
    var = jnp.mean(jnp.square(uf - mu), axis=-1, keepdims=True)
    un = ((uf - mu) * lax.rsqrt(var + EPS) * ln_g.astype(jnp.float32) + ln_b.astype(jnp.float32)).astype(u.dtype)
    return jax.nn.silu(un) @ pw_w + pw_b


def peer(x, wq, keys, u, v):
    shape = x.shape
    t = x.reshape(-1, D_MODEL)
    n_blocks = t.shape[0] // TOKEN_BLOCK

    def block(xb):
        q = (xb @ wq).reshape(TOKEN_BLOCK, PEER_HEADS, 2, PEER_KEY_DIM // 2)
        s = jnp.einsum('thcd,hckd->thck', q, keys).astype(jnp.float32)
        s_top, i_top = lax.top_k(s, PEER_TOPK)
        cand = (s_top[:, :, 0, :, None] + s_top[:, :, 1, None, :]).reshape(TOKEN_BLOCK, PEER_HEADS, PEER_TOPK * PEER_TOPK)
        cand_id = (i_top[:, :, 0, :, None] * N_SUBKEYS + i_top[:, :, 1, None, :]).reshape(TOKEN_BLOCK, PEER_HEADS, PEER_TOPK * PEER_TOPK)
        best, sel = lax.top_k(cand, PEER_TOPK)
        ids = jnp.take_along_axis(cand_id, sel, axis=-1)
        g = jax.nn.softmax(best, axis=-1)
        u_sel = u[ids]
        v_sel = v[ids]
        act = jax.nn.gelu(jnp.einsum('td,thkd->thk', xb, u_sel).astype(jnp.float32), approximate=False)
        return jnp.einsum('thk,thkd->td', (g * act).astype(xb.dtype), v_sel)

    out = lax.map(block, t.reshape(n_blocks, TOKEN_BLOCK, D_MODEL))
    return out.reshape(shape)


def encoder_layer(x, norm1_g, w_in, b_gate, w_attn_up, conv_dw_w, conv_dw_b, conv_ln_g, conv_ln_b,
                  conv_pw_w, conv_pw_b, w_out, norm2_g, peer_wq, peer_keys, peer_u, peer_v):
    B, S, _ = x.shape
    h = rms_norm(x, norm1_g)
    proj = h @ w_in
    cuts = np.cumsum([ATTN_WIDTH, ATTN_WIDTH, ATTN_WIDTH, CONV_WIDTH, CONV_WIDTH]).tolist()
    q, k, vv, glu_a, glu_b, gates = jnp.split(proj, cuts, axis=-1)
    hs = (B, S, N_GROUPS, HEADS_PER_GROUP, HEAD_DIM)
    pos = jnp.arange(S, dtype=jnp.int32)
    q = partial_rotary(q.reshape(hs), pos)
    k = partial_rotary(k.reshape(hs), pos)
    attn = dilated_attention(q, k, vv.reshape(hs)) @ w_attn_up
    conv = conv_module(glu_a, glu_b, conv_dw_w, conv_dw_b, conv_ln_g, conv_ln_b, conv_pw_w, conv_pw_b)
    g = jax.nn.sigmoid((gates + b_gate).astype(jnp.float32)).astype(x.dtype).reshape(B, S, N_BRANCHES, D_MODEL)
    mixed = g[:, :, 0, :] * attn + g[:, :, 1, :] * conv
    x = x + mixed @ w_out
    x = x + peer(rms_norm(x, norm2_g), peer_wq, peer_keys, peer_u, peer_v)
    return x


def setup_inputs(seed: int = 0) -> dict:
    key = jax.random.key(seed)
    ks = jax.random.split(key, 20)
    f32 = jnp.float32
    nrm = lambda k, shape, s: jax.random.normal(k, shape, f32) * s
    L = DEPTH
    return {
        "x_prompt": nrm(ks[0], (BATCH, SEQ, D_MODEL), 1.0),
        "x_sample": nrm(ks[1], (DEC_BATCH, DEC_SEQ, D_MODEL), 1.0),
        "norm1_g": 1.0 + nrm(ks[2], (L, D_MODEL), 0.02),
        "w_in": nrm(ks[3], (L, D_MODEL, IN_COLS), D_MODEL ** -0.5),
        "b_gate": nrm(ks[4], (L, N_BRANCHES * D_MODEL), 0.02),
        "w_attn_up": nrm(ks[5], (L, ATTN_OUT_WIDTH, D_MODEL), ATTN_OUT_WIDTH ** -0.5),
        "conv_dw_w": nrm(ks[6], (L, CONV_KERNEL, CONV_WIDTH), CONV_KERNEL ** -0.5),
        "conv_dw_b": nrm(ks[7], (L, CONV_WIDTH), 0.02),
        "conv_ln_g": 1.0 + nrm(ks[8], (L, CONV_WIDTH), 0.02),
        "conv_ln_b": nrm(ks[9], (L, CONV_WIDTH), 0.02),
        "conv_pw_w": nrm(ks[10], (L, CONV_WIDTH, D_MODEL), CONV_WIDTH ** -0.5),
        "conv_pw_b": nrm(ks[11], (L, D_MODEL), 0.02),
        "w_out": nrm(ks[12], (L, D_MODEL, D_MODEL), D_MODEL ** -0.5),
        "norm2_g": 1.0 + nrm(ks[13], (L, D_MODEL), 0.02),
        "peer_wq": nrm(ks[14], (L, D_MODEL, PEER_HEADS * PEER_KEY_DIM), D_MODEL ** -0.5),
        "peer_keys": nrm(ks[15], (L, PEER_HEADS, 2, N_SUBKEYS, PEER_KEY_DIM // 2), (PEER_KEY_DIM // 2) ** -0.5),
        "peer_u": nrm(ks[16], (L, N_EXPERTS, D_MODEL), D_MODEL ** -0.5),
        "peer_v": nrm(ks[17], (L, N_EXPERTS, D_MODEL), (PEER_HEADS * PEER_TOPK) ** -0.5),
        "final_g": 1.0 + nrm(ks[18], (D_MODEL,), 0.02),
    }


def reference(x_prompt, x_sample, norm1_g, w_in, b_gate, w_attn_up, conv_dw_w, conv_dw_b, conv_ln_g,
              conv_ln_b, conv_pw_w, conv_pw_b, w_out, norm2_g, peer_wq, peer_keys, peer_u, peer_v, final_g):
    hp = x_prompt
    hsmp = x_sample
    for l in range(DEPTH):
        args = (norm1_g[l], w_in[l], b_gate[l], w_attn_up[l], conv_dw_w[l], conv_dw_b[l], conv_ln_g[l],
                conv_ln_b[l], conv_pw_w[l], conv_pw_b[l], w_out[l], norm2_g[l], peer_wq[l], peer_keys[l],
                peer_u[l], peer_v[l])
        hp = encoder_layer(hp, *args)
        hsmp = encoder_layer(hsmp, *args)
    y_prompt = rms_norm(hp, final_g)
    y_sample = rms_norm(hsmp, final_g)
    return (y_prompt, y_sample)
```

```python
from contextlib import ExitStack
import numpy as np
import concourse.bass as bass
import concourse.mybir as mybir
from concourse.bass_utils import run_bass_kernel_spmd

F32 = mybir.dt.float32
BF16 = mybir.dt.bfloat16
U32 = mybir.dt.uint32
AF = mybir.ActivationFunctionType
ALU = mybir.AluOpType
AX = mybir.AxisListType

D = 1024
NCOL = 5376
PAD = 1024
NEXP = 16384
EPS = 1e-6
DIL = (1, 4, 16)
NEG = -30000.0
N_CORES = 8


class Res:
    __slots__ = ("name", "w", "rd")

    def __init__(self, name):
        self.name = name
        self.w = None
        self.rd = {}


class DSem:
    def __init__(self, h, key):
        self.h = h
        self.key = key
        self.count = 0


class Sched:
    CE = ("pe", "act", "dve", "pool")

    def __init__(self, nc):
        self.nc = nc
        self.E = {"pe": nc.tensor, "act": nc.scalar, "dve": nc.vector, "pool": nc.gpsimd, "sp": nc.sync}
        self.semh = {}
        self.cnt = {}
        self.waited = {e: {} for e in self.E}
        for e in self.CE:
            self.semh[e] = nc.alloc_semaphore("s_" + e)
            self.cnt[e] = 0
        self.dsems = []
        self.bar = nc.alloc_semaphore("s_bar")
        self.barcnt = 0
        self.n_ds = 0
        self.free_ds = []
        self.live_ds = []

    def new_dsem(self, name=None):
        if self.free_ds:
            d = self.free_ds.pop()
            self.live_ds.append(d)
            return d
        self.n_ds += 1
        name = "d%d_%s" % (self.n_ds, name or "")
        d = DSem(self.nc.alloc_semaphore(name), name)
        self.semh[name] = d.h
        self.dsems.append(d)
        self.live_ds.append(d)
        return d

    def recycle(self):
        self.free_ds.extend(self.live_ds)
        self.live_ds = []

    def _wait(self, eng, key, val):
        if self.waited[eng].get(key, 0) >= val:
            return
        self.E[eng].wait_ge(self.semh[key], val)
        self.waited[eng][key] = val

    def _deps(self, eng, reads, writes, is_dma=False):
        deps = {}

        def add(key, val, kind):
            if key == eng and not is_dma and (eng == "pe" or kind != "raw"):
                return
            if deps.get(key, 0) < val:
                deps[key] = val

        for r in reads:
            if r.w is not None:
                add(r.w[0], r.w[1], "raw")
        for r in writes:
            if r.w is not None:
                add(r.w[0], r.w[1], "waw")
            for k, v in r.rd.items():
                add(k, v, "war")
        for k, v in deps.items():
            self._wait(eng, k, v)

    def op(self, eng, fn, reads=(), writes=()):
        self._deps(eng, reads, writes)
        ins = fn(self.E[eng])
        self.cnt[eng] += 1
        c = self.cnt[eng]
        ins.then_inc(self.semh[eng], 1)
        for r in reads:
            r.rd[eng] = c
        for r in writes:
            r.w = (eng, c)
            r.rd = {}
        return ins

    def dma(self, out, in_, reads=(), writes=(), dsem=None, first=True, q="sp"):
        if isinstance(dsem, LazyD):
            dsem = dsem.get()
        self._deps(q, reads, writes, is_dma=True)
        if first and dsem.count > 0:
            self._wait(q, dsem.key, dsem.count)
        ins = self.E[q].dma_start(out=out, in_=in_)
        dsem.count += 16
        ins.then_inc(dsem.h, 16)
        for r in reads:
            r.rd[dsem.key] = dsem.count
        for r in writes:
            r.w = (dsem.key, dsem.count)
            r.rd = {}
        return ins

    def barrier(self, dummy_sb, dummy_dram):
        for e in self.CE:
            if self.cnt[e] > 0:
                self._wait("sp", e, self.cnt[e])
        for d in self.dsems:
            if d.count > 0:
                self._wait("sp", d.key, d.count)
        ins = self.E["sp"].dma_start(out=dummy_sb, in_=dummy_dram)
        self.barcnt += 16
        ins.then_inc(self.bar, 16)
        for e in self.E:
            self.E[e].wait_ge(self.bar, self.barcnt)
            for k in self.CE:
                self.waited[e][k] = self.cnt[k]
            for d in self.dsems:
                self.waited[e][d.key] = d.count


class LazyD:
    def __init__(self, S, name):
        self.S = S
        self.name = name
        self.d = None

    def get(self):
        if self.d is None:
            self.d = self.S.new_dsem(self.name)
        return self.d


class Rot:
    ctr = 0

    def __init__(self, S, st, nc, name, shape, dtype, n):
        self.items = []
        for i in range(n):
            Rot.ctr += 1
            t = st.enter_context(nc.sbuf_tensor("rot%d_%s%d" % (Rot.ctr, name, i), list(shape), dtype))
            self.items.append((t, Res("%s%d" % (name, i)), LazyD(S, name)))
        self.i = 0

    def get(self):
        it = self.items[self.i % len(self.items)]
        self.i += 1
        return it


def build_program(seqs, dbg=False):
    nc = bass.Bass("TRN2", target_bir_lowering=False)
    NT = sum(seqs)
    NTP = NT + 2 * PAD * len(seqs)
    seqinfo = []
    ts, ps_ = 0, 0
    for L in seqs:
        assert L % 2048 == 0
        seqinfo.append((ts, L, ps_))
        ts += L
        ps_ += L + 2 * PAD

    def din(name, shape, dt=F32):
        return nc.dram_tensor(name, list(shape), dt, kind="ExternalInput").ap()

    def dscr(name, shape, dt):
        if dbg:
            return nc.dram_tensor(name, list(shape), dt, kind="ExternalOutput").ap()
        return nc.dram_tensor(name, list(shape), dt).ap()

    x_in = din("x", [NT, D])
    w_in = din("w_in", [D, NCOL])
    g1b_in = din("g1b", [128, D])
    g2b_in = din("g2b", [128, D])
    gfb_in = din("gfb", [128, D])
    bgate_in = din("bgate", [128, 16])
    wup_in = din("w_up", [256, D])
    dww_in = din("dww", [128, 4, 31])
    cpar_in = din("cpar", [128, 3, 4])
    pww_in = din("pw_w", [512, D])
    pwb_in = din("pwb", [128, 8])
    wout_in = din("w_out", [D, D])
    wq_in = din("wq", [D, 2048])
    keysT_in = din("keysT", [128, 16, 128])
    pu_in = din("peer_u", [NEXP, D])
    pv_in = din("peer_v", [NEXP, D])
    ropec_in = din("rope_c", [128, 8192])
    ropes_in = din("rope_s", [128, 8192])
    dummy_in = din("dummy8", [1, 8])
    y_out = nc.dram_tensor("y", [NT, D], F32, kind="ExternalOutput").ap()

    qS = dscr("qS", [768, NT], BF16)
    kS = dscr("kS", [768, NTP], BF16)
    vS = dscr("vS", [NTP, 768], BF16)
    uS = dscr("uS", [512, NTP], BF16)
    gS = dscr("gS", [2048, NT], BF16)
    aS = dscr("aS", [256, NT], BF16)
    x2S = dscr("x2S", [NT, D], F32)
    h2S = dscr("h2S", [D, NT], BF16)
    rS = dscr("rS", [3, 128, NT], F32)
    UT = nc.dram_tensor("UT", [128, 128, 1024], BF16).ap()
    VB = nc.dram_tensor("VB", [NEXP, D], BF16).ap()

    S = Sched(nc)
    top = ExitStack()

    nctr = [0]

    def sb(st, name, shape, dt):
        nctr[0] += 1
        return st.enter_context(nc.sbuf_tensor("sb%d_%s" % (nctr[0], name), list(shape), dt))

    PS = top.enter_context(nc.psum_tensor("PS", [128, 8, 512], F32))
    RB = [Res("bank%d" % i) for i in range(8)]
    ident_bf = sb(top, "ident_bf", [128, 128], BF16)
    ident_f = sb(top, "ident_f", [128, 128], F32)
    iota_f = sb(top, "iota_f", [128, 128], F32)
    dif = sb(top, "dif", [128, 128], F32)
    maskAB = sb(top, "maskAB", [128, 2, 128], BF16)
    pcol = sb(top, "pcol", [128, 1], F32)
    bz = sb(top, "bz", [128, 1], F32)
    epsc = sb(top, "epsc", [128, 1], F32)
    blo = sb(top, "blo", [128, 1], F32)
    bhi = sb(top, "bhi", [128, 1], F32)
    ones65 = sb(top, "ones65", [65, 64], F32)
    onesM = sb(top, "onesM", [128, 128], BF16)
    dummy_sb = sb(top, "dummy_sb", [1, 8], F32)
    RC = Res("consts")

    S.op("pool", lambda e: e.iota(dif[:], pattern=[[1, 128]], base=0, channel_multiplier=-1,
                                  allow_small_or_imprecise_dtypes=True), writes=[RC])
    S.op("pool", lambda e: e.iota(iota_f[:], pattern=[[1, 128]], base=0, channel_multiplier=0,
                                  allow_small_or_imprecise_dtypes=True), writes=[RC])
    S.op("pool", lambda e: e.iota(pcol[:], pattern=[[0, 1]], base=0, channel_multiplier=1,
                                  allow_small_or_imprecise_dtypes=True), writes=[RC])
    S.op("dve", lambda e: e.tensor_single_scalar(ident_bf[:], dif[:], 0.0, op=ALU.is_equal), reads=[RC], writes=[RC])
    S.op("dve", lambda e: e.tensor_single_scalar(ident_f[:], dif[:], 0.0, op=ALU.is_equal), reads=[RC], writes=[RC])
    S.op("dve", lambda e: e.tensor_scalar(maskAB[:, 0, :], dif[:], 0.0, NEG, op0=ALU.is_gt, op1=ALU.mult), reads=[RC], writes=[RC])
    S.op("dve", lambda e: e.tensor_scalar(maskAB[:, 1, :], dif[:], 0.0, NEG, op0=ALU.is_lt, op1=ALU.mult), reads=[RC], writes=[RC])
    S.op("dve", lambda e: e.tensor_scalar(blo[:], pcol[:], 64.0, NEG, op0=ALU.is_lt, op1=ALU.mult), reads=[RC], writes=[RC])
    S.op("dve", lambda e: e.tensor_scalar(bhi[:], pcol[:], 64.0, NEG, op0=ALU.is_ge, op1=ALU.mult), reads=[RC], writes=[RC])
    S.op("dve", lambda e: e.memset(bz[:], 0.0), writes=[RC])
    S.op("dve", lambda e: e.memset(epsc[:], EPS), writes=[RC])
    S.op("dve", lambda e: e.memset(ones65[:], 1.0), writes=[RC])
    S.op("dve", lambda e: e.memset(onesM[:], 1.0 / 512.0), writes=[RC])

    def barrier():
        S.barrier(dummy_sb[:], dummy_in)
        S.recycle()

    barrier()

    def mm(out, lhsT, rhs, start, stop, reads, writes):
        S.op("pe", lambda e: e.matmul(out, lhsT=lhsT, rhs=rhs, start=start, stop=stop), reads=reads, writes=writes)

    with ExitStack() as st:
        zt = sb(st, "zt", [128, 6144], BF16)
        RZ = Res("zt")
        S.op("pool", lambda e: e.memset(zt[:], 0.0), writes=[RZ])
        dz = S.new_dsem("zero")
        for (ts_, L, pb) in seqinfo:
            for off in (pb, pb + PAD + L):
                S.dma(kS[:, off:off + PAD].rearrange("(a p) t -> p a t", p=128),
                      zt[:].rearrange("p (a t) -> p a t", a=6), reads=[RZ], dsem=dz, first=False)
                S.dma(vS[off:off + PAD, :].rearrange("(p a) c -> p (a c)", p=128), zt[:], reads=[RZ], dsem=dz, first=False)
                S.dma(uS[:, off:off + PAD].rearrange("(a p) t -> p a t", p=128),
                      zt[:, 0:4096].rearrange("p (a t) -> p a t", a=4), reads=[RZ], dsem=dz, first=False)
        ust = Rot(S, st, nc, "ust", [128, D], F32, 2)
        vst = Rot(S, st, nc, "vst", [128, D], F32, 2)
        ubf = Rot(S, st, nc, "ubf", [128, D], BF16, 2)
        uts = Rot(S, st, nc, "uts", [128, D], BF16, 2)
        vbf = Rot(S, st, nc, "vbf", [128, D], BF16, 2)
        NI = NEXP // 128
        loaded = {}

        def p0_load(i):
            a = ust.get()
            b = vst.get()
            S.dma(a[0][:], pu_in[i * 128:(i + 1) * 128, :], writes=[a[1]], dsem=a[2])
            S.dma(b[0][:], pv_in[i * 128:(i + 1) * 128, :], writes=[b[1]], dsem=b[2])
            loaded[i] = (a, b)

        p0_load(0)
        for i in range(NI):
            if i + 1 < NI:
                p0_load(i + 1)
            a, b = loaded.pop(i)
            ub = ubf.get()
            S.op("act", lambda e: e.copy(ub[0][:], a[0][:]), reads=[a[1]], writes=[ub[1]])
            bank = 6 + (i % 2)
            ptb = PS[:, bank, :].bitcast(BF16)
            for kc in range(8):
                S.op("pe", lambda e: e.transpose(ptb[:, kc * 128:(kc + 1) * 128], ub[0][:, kc * 128:(kc + 1) * 128], ident_bf[:]),
                     reads=[ub[1]], writes=[RB[bank]])
            us_ = uts.get()
            S.op("dve", lambda e: e.tensor_copy(us_[0][:], ptb), reads=[RB[bank]], writes=[us_[1]])
            S.dma(UT[i], us_[0][:], reads=[us_[1]], dsem=us_[2])
            vb_ = vbf.get()
            S.op("pool", lambda e: e.tensor_copy(vb_[0][:], b[0][:]), reads=[b[1]], writes=[vb_[1]])
            S.dma(VB[i * 128:(i + 1) * 128, :], vb_[0][:], reads=[vb_[1]], dsem=vb_[2])
    barrier()

    tiles = []
    for (ts_, L, pb) in seqinfo:
        for p0 in range(0, L, 512):
            tiles.append((ts_ + p0, p0, pb + PAD + p0))

    def load_w_bf16(st, name, src, kchunks, ncols, eng_cycle=("act", "dve"), rows=128):
        wt = sb(st, name, [rows, kchunks, ncols], BF16)
        R = Res(name)
        with ExitStack() as st2:
            stg = Rot(S, st2, nc, name + "_stg", [rows, ncols], F32, 2)
            for kc in range(kchunks):
                t, r, d = stg.get()
                S.dma(t[:], src[kc * rows:(kc + 1) * rows, :], writes=[r], dsem=d)
                eng = eng_cycle[kc % len(eng_cycle)]
                if eng == "act":
                    S.op("act", lambda e: e.copy(wt[:, kc, :], t[:]), reads=[r], writes=[R])
                else:
                    S.op(eng, lambda e: e.tensor_copy(wt[:, kc, :], t[:]), reads=[r], writes=[R])
            barrier()
        return wt, R

    def load_small(st, name, src, shape):
        t = sb(st, name, shape, F32)
        R = Res(name)
        S.dma(t[:], src, writes=[R], dsem=S.new_dsem(name))
        return t, R

    def rms_block(xt_ap, Rx, gb, Rg, out_bf, Rout, junk, Rjunk, ssq, Rssq, col):
        S.op("act", lambda e: e.activation(out=junk[:], in_=xt_ap, func=AF.Square, accum_out=ssq[:, col:col + 1]),
             reads=[Rx], writes=[Rjunk, Rssq])
        S.op("act", lambda e: e.activation(out=ssq[:, col:col + 1], in_=ssq[:, col:col + 1], func=AF.Sqrt, bias=epsc[:], scale=1.0 / D),
             reads=[Rssq, RC], writes=[Rssq])
        S.op("dve", lambda e: e.reciprocal(ssq[:, col:col + 1], ssq[:, col:col + 1]), reads=[Rssq], writes=[Rssq])
        S.op("dve", lambda e: e.scalar_tensor_tensor(out_bf, xt_ap, ssq[:, col:col + 1], gb[:], op0=ALU.mult, op1=ALU.mult),
             reads=[Rx, Rssq, Rg], writes=[Rout])

    with ExitStack() as st:
        Wb, RW = load_w_bf16(st, "Wb", w_in, 8, NCOL)
        Wr = sb(st, "Wr", [128, 8, 1536], BF16)
        RWr = Res("Wr")
        S.op("pool", lambda e: e.memset(Wr[:], 0.0), writes=[RWr])
        for kc in range(8):
            wv = Wb[:, kc, 0:1536].rearrange("p (h e) -> p h e", e=64)
            rv = Wr[:, kc, :].rearrange("p (h e) -> p h e", e=64)
            S.op("pool", lambda e: e.tensor_single_scalar(rv[:, :, 0:8], wv[:, :, 8:16], -1.0, op=ALU.mult), reads=[RW], writes=[RWr])
            S.op("pool", lambda e: e.tensor_copy(rv[:, :, 8:16], wv[:, :, 0:8]), reads=[RW], writes=[RWr])
        g1b, Rg1 = load_small(st, "g1b", g1b_in, [128, D])
        bgate, Rbg = load_small(st, "bgate", bgate_in, [128, 16])
        xr = Rot(S, st, nc, "xt", [128, D], F32, 3)
        ropc = Rot(S, st, nc, "ropc", [128, 512], F32, 2)
        rops = Rot(S, st, nc, "rops", [128, 512], F32, 2)
        junk = sb(st, "junk", [128, D], BF16)
        Rjunk = Res("junk")
        ssq = sb(st, "ssq", [128, 8], F32)
        Rssq = [Res("ssq%d" % i) for i in range(8)]
        hbr = Rot(S, st, nc, "hb", [128, D], BF16, 2)
        hTs = [sb(st, "hT%d" % i, [128, 8, 512], BF16) for i in range(2)]
        RhT = [Res("hT%d" % i) for i in range(2)]
        t1 = Rot(S, st, nc, "t1", [128, 512], F32, 2)
        t2 = Rot(S, st, nc, "t2", [128, 512], F32, 2)
        qko = Rot(S, st, nc, "qko", [128, 512], BF16, 3)
        sig = Rot(S, st, nc, "sig", [128, 512], F32, 2)
        uo = Rot(S, st, nc, "uo", [128, 512], BF16, 2)
        go = Rot(S, st, nc, "go", [128, 512], BF16, 3)
        vo = Rot(S, st, nc, "vo", [128, 768], BF16, 2)

        xq = {}
        blocks = [(ti, a) for ti in range(len(tiles)) for a in range(4)]
        nxt = [0]

        def p1_prefetch(upto):
            while nxt[0] <= upto and nxt[0] < len(blocks):
                ti, a = blocks[nxt[0]]
                t0 = tiles[ti][0]
                it = xr.get()
                S.dma(it[0][:], x_in[t0 + a * 128:t0 + (a + 1) * 128, :], writes=[it[1]], dsem=it[2])
                xq[nxt[0]] = it
                nxt[0] += 1

        pbank = [0]

        def nbank():
            b = pbank[0] % 4
            pbank[0] += 1
            return b

        for ti, (t0, p0, pt) in enumerate(tiles):
            hT = hTs[ti % 2]
            RH = RhT[ti % 2]
            rc = ropc.get()
            rs = rops.get()
            S.dma(rc[0][:], ropec_in[:, p0:p0 + 512], writes=[rc[1]], dsem=rc[2])
            S.dma(rs[0][:], ropes_in[:, p0:p0 + 512], writes=[rs[1]], dsem=rs[2])
            for a in range(4):
                bi = ti * 4 + a
                p1_prefetch(bi + 2)
                xt_, Rx, _ = xq.pop(bi)
                hb, Rhb, _ = hbr.get()
                col = bi % 8
                rms_block(xt_[:], Rx, g1b, Rg1, hb[:], Rhb, junk, Rjunk, ssq, Rssq[col], col)
                bank = 6 + (bi % 2)
                ptb = PS[:, bank, :].bitcast(BF16)
                for kc in range(8):
                    S.op("pe", lambda e: e.transpose(ptb[:, kc * 128:(kc + 1) * 128], hb[:, kc * 128:(kc + 1) * 128], ident_bf[:]),
                         reads=[Rhb], writes=[RB[bank]])
                S.op("act", lambda e: e.copy(hT[:, :, a * 128:(a + 1) * 128], ptb.rearrange("p (k t) -> p k t", k=8)),
                     reads=[RB[bank]], writes=[RH])
            for cc in range(12):
                bm = nbank()
                br = nbank()
                for kc in range(8):
                    mm(PS[:, bm, :], Wb[:, kc, cc * 128:(cc + 1) * 128], hT[:, kc, :], kc == 0, kc == 7, [RW, RH], [RB[bm]])
                for kc in range(8):
                    mm(PS[:, br, :], Wr[:, kc, cc * 128:(cc + 1) * 128], hT[:, kc, :], kc == 0, kc == 7, [RWr, RH], [RB[br]])
                a1 = t1.get()
                a2 = t2.get()
                o = qko.get()
                S.op("dve", lambda e: e.tensor_tensor(a1[0][:], PS[:, bm, :], rc[0][:], op=ALU.mult), reads=[RB[bm], rc[1]], writes=[a1[1]])
                S.op("dve", lambda e: e.tensor_tensor(a2[0][:], PS[:, br, :], rs[0][:], op=ALU.mult), reads=[RB[br], rs[1]], writes=[a2[1]])
                S.op("pool", lambda e: e.tensor_tensor(o[0][:], a1[0][:], a2[0][:], op=ALU.add), reads=[a1[1], a2[1]], writes=[o[1]])
                if cc < 6:
                    S.dma(qS[cc * 128:(cc + 1) * 128, t0:t0 + 512], o[0][:], reads=[o[1]], dsem=o[2])
                else:
                    S.dma(kS[(cc - 6) * 128:(cc - 5) * 128, pt:pt + 512], o[0][:], reads=[o[1]], dsem=o[2])
            for c in range(4):
                ba = nbank()
                bb = nbank()
                for kc in range(8):
                    mm(PS[:, ba, :], Wb[:, kc, 2304 + c * 128:2304 + (c + 1) * 128], hT[:, kc, :], kc == 0, kc == 7, [RW, RH], [RB[ba]])
                for kc in range(8):
                    mm(PS[:, bb, :], Wb[:, kc, 2816 + c * 128:2816 + (c + 1) * 128], hT[:, kc, :], kc == 0, kc == 7, [RW, RH], [RB[bb]])
                sg = sig.get()
                o = uo.get()
                S.op("act", lambda e: e.activation(out=sg[0][:], in_=PS[:, bb, :], func=AF.Sigmoid), reads=[RB[bb]], writes=[sg[1]])
                S.op("dve", lambda e: e.tensor_tensor(o[0][:], PS[:, ba, :], sg[0][:], op=ALU.mult), reads=[RB[ba], sg[1]], writes=[o[1]])
                S.dma(uS[c * 128:(c + 1) * 128, pt:pt + 512], o[0][:], reads=[o[1]], dsem=o[2])
            for gc in range(16):
                bg = nbank()
                for kc in range(8):
                    mm(PS[:, bg, :], Wb[:, kc, 3328 + gc * 128:3328 + (gc + 1) * 128], hT[:, kc, :], kc == 0, kc == 7, [RW, RH], [RB[bg]])
                o = go.get()
                S.op("act", lambda e: e.activation(out=o[0][:], in_=PS[:, bg, :], func=AF.Sigmoid, bias=bgate[:, gc:gc + 1]),
                     reads=[RB[bg], Rbg], writes=[o[1]])
                S.dma(gS[gc * 128:(gc + 1) * 128, t0:t0 + 512], o[0][:], reads=[o[1]], dsem=o[2])
            for a in range(4):
                for kc in range(8):
                    mm(PS[:, 4, :], hT[:, kc, a * 128:(a + 1) * 128], Wb[:, kc, 1536:2048], kc == 0, kc == 7, [RW, RH], [RB[4]])
                for kc in range(8):
                    mm(PS[:, 5, 0:256], hT[:, kc, a * 128:(a + 1) * 128], Wb[:, kc, 2048:2304], kc == 0, kc == 7, [RW, RH], [RB[5]])
                o = vo.get()
                S.op("act", lambda e: e.copy(o[0][:, 0:512], PS[:, 4, :]), reads=[RB[4]], writes=[o[1]])
                S.op("dve", lambda e: e.tensor_copy(o[0][:, 512:768], PS[:, 5, 0:256]), reads=[RB[5]], writes=[o[1]])
                S.dma(vS[pt + a * 128:pt + (a + 1) * 128, :], o[0][:], reads=[o[1]], dsem=o[2])
    barrier()

    with ExitStack() as st:
        acc = sb(st, "acc", [65, 4, 2048], F32)
        Racc = Res("acc")
        Vaug = [sb(st, "Vaug%d" % i, [128, 32, 4, 65], BF16) for i in range(2)]
        RV = [Res("Vaug%d" % i) for i in range(2)]
        dV = [S.new_dsem("Vaug") for _ in range(2)]
        for i in range(2):
            S.op("pool", lambda e: e.memset(Vaug[i][:], 1.0), writes=[RV[i]])
        Qr = Rot(S, st, nc, "Qt", [64, 2048], BF16, 2)
        Kr = Rot(S, st, nc, "Kt", [64, 4096], BF16, 2)
        Pr = Rot(S, st, nc, "Pt", [128, 2, 128], BF16, 4)
        rec = Rot(S, st, nc, "rec", [64, 512], F32, 2)
        ao = sb(st, "ao", [64, 4, 2048], BF16)
        Rao = Res("ao")
        dao = S.new_dsem("ao")
        sslot = [0]
        oslot = [0]
        vcount = 0
        for (ts_, L, pb) in seqinfo:
            for s0 in range(0, L, 2048):
                t0g = ts_ + s0
                po = pb + PAD + s0
                first = s0 == 0
                last = s0 + 2048 == L
                for g in range(3):
                    d = DIL[g]
                    halo = 64 * d
                    nq = 16 // d
                    gb = vcount % 2
                    vcount += 1
                    fst = True
                    for r in range(d):
                        for j in range(nq + 1):
                            tok = po + (128 * j - 64) * d + r
                            S.dma(Vaug[gb][:, r * (nq + 1) + j, :, 0:64],
                                  vS[tok:tok + 127 * d + 1:d, g * 256:(g + 1) * 256].rearrange("p (h e) -> p h e", h=4),
                                  writes=[RV[gb]], dsem=dV[gb], first=fst)
                            fst = False
                    for hs in range(4):
                        hr = (g * 4 + hs) * 64
                        Qt, RQ, dQ = Qr.get()
                        Kt, RK, dK = Kr.get()
                        S.dma(Qt[:], qS[hr:hr + 64, t0g:t0g + 2048], writes=[RQ], dsem=dQ)
                        S.dma(Kt[:, 0:2048 + 2 * halo], kS[hr:hr + 64, po - halo:po + 2048 + halo], writes=[RK], dsem=dK)
                        for r in range(d):
                            for qb in range(nq):
                                c0 = 128 * qb * d + r
                                qv = Qt[:, c0:c0 + 127 * d + 1:d]
                                kA = Kt[:, c0:c0 + 127 * d + 1:d]
                                kB = Kt[:, c0 + 128 * d:c0 + 255 * d + 1:d]
                                sbk = sslot[0] % 4
                                sslot[0] += 1
                                pst = PS[:, sbk, 0:256].rearrange("p (a q) -> p a q", a=2)
                                mm(pst[:, 0, :], kA, qv, True, False, [RK, RQ], [RB[sbk]])
                                mm(pst[:, 0, :], ident_bf[:], maskAB[:, 0, :], False, True, [RC], [RB[sbk]])
                                mm(pst[:, 1, :], kB, qv, True, False, [RK, RQ], [RB[sbk]])
                                mm(pst[:, 1, :], ident_bf[:], maskAB[:, 1, :], False, True, [RC], [RB[sbk]])
                                bA = blo if (first and qb == 0) else bz
                                bB = bhi if (last and qb == nq - 1) else bz
                                Pt, RP, _ = Pr.get()
                                if bA is bB:
                                    S.op("act", lambda e: e.activation(out=Pt[:], in_=pst, func=AF.Exp, bias=bz[:], scale=0.125),
                                         reads=[RB[sbk], RC], writes=[RP])
                                else:
                                    S.op("act", lambda e: e.activation(out=Pt[:, 0, :], in_=pst[:, 0, :], func=AF.Exp, bias=bA[:], scale=0.125),
                                         reads=[RB[sbk], RC], writes=[RP])
                                    S.op("act", lambda e: e.activation(out=Pt[:, 1, :], in_=pst[:, 1, :], func=AF.Exp, bias=bB[:], scale=0.125),
                                         reads=[RB[sbk], RC], writes=[RP])
                                obk = 4 + (oslot[0] % 4)
                                oslot[0] += 1
                                pov = PS[0:65, obk, 0:128]
                                blk = r * (nq + 1) + qb
                                mm(pov, Vaug[gb][:, blk, hs, :], Pt[:, 0, :], True, False, [RV[gb], RP], [RB[obk]])
                                mm(pov, Vaug[gb][:, blk + 1, hs, :], Pt[:, 1, :], False, True, [RV[gb], RP], [RB[obk]])
                                av = acc[:, hs, c0:c0 + 127 * d + 1:d]
                                if g == 0:
                                    S.op("dve", lambda e: e.tensor_copy(av, pov), reads=[RB[obk]], writes=[Racc])
                                else:
                                    S.op("dve", lambda e: e.tensor_tensor(av, av, pov, op=ALU.add), reads=[RB[obk], Racc], writes=[Racc])
                for hs in range(4):
                    for ch in range(4):
                        bk = nbank()
                        mm(PS[0:64, bk, :], ones65[64:65, :], acc[64:65, hs, ch * 512:(ch + 1) * 512], True, True, [Racc, RC], [RB[bk]])
                        rc_ = rec.get()
                        S.op("dve", lambda e: e.reciprocal(rc_[0][:], PS[0:64, bk, :]), reads=[RB[bk]], writes=[rc_[1]])
                        S.op("pool", lambda e: e.tensor_tensor(ao[:, hs, ch * 512:(ch + 1) * 512], acc[0:64, hs, ch * 512:(ch + 1) * 512],
                                                              rc_[0][:], op=ALU.mult), reads=[Racc, rc_[1]], writes=[Rao])
                S.dma(aS[:, t0g:t0g + 2048].rearrange("(h e) t -> e h t", h=4), ao[:], reads=[Rao], dsem=dao)
    barrier()

    with ExitStack() as st:
        pww, Rpww = load_w_bf16(st, "pww", pww_in, 4, D)
        wup, Rwup = load_w_bf16(st, "wup", wup_in, 4, D, rows=64)
        wout, Rwout = load_w_bf16(st, "wout", wout_in, 8, D)
        dww, Rdww = load_small(st, "dww", dww_in, [128, 4, 31])
        cpar, Rcp = load_small(st, "cpar", cpar_in, [128, 3, 4])
        pwb, Rpwb = load_small(st, "pwb", pwb_in, [128, 8])
        Dg = sb(st, "Dg", [128, 124, 128], BF16)
        RDg = Res("Dg")
        for c in range(4):
            for k in range(31):
                eng = "dve" if (k % 2 == 0) else "pool"
                S.op(eng, lambda e: e.tensor_scalar(Dg[:, c * 31 + k, :], ident_f[:], dww[:, c, k:k + 1], None, op0=ALU.mult),
                     reads=[Rdww, RC], writes=[RDg])
        Ur = Rot(S, st, nc, "U", [128, 4, 542], BF16, 2)
        Ar = Rot(S, st, nc, "At", [64, 4, 512], BF16, 2)
        Gr = Rot(S, st, nc, "G", [128, 2, 512], BF16, 3)
        xr = Rot(S, st, nc, "x3", [128, D], F32, 2)
        uc = sb(st, "uc", [128, 4, 512], BF16)
        sq = sb(st, "sq", [128, 4, 512], BF16)
        Ruc = Res("uc")
        Rsq = Res("sq")
        msq = sb(st, "msq", [128, 512], F32)
        rstd = sb(st, "rstd", [128, 512], F32)
        nmr = sb(st, "nmr", [128, 512], F32)
        Rst = Res("stats")
        tt = Rot(S, st, nc, "tt", [128, 512], F32, 2)
        sl = sb(st, "sl", [128, 4, 512], BF16)
        Rsl = Res("sl")
        m1 = Rot(S, st, nc, "m1", [128, 512], F32, 2)
        m2 = Rot(S, st, nc, "m2", [128, 512], F32, 2)
        mx = sb(st, "mx", [128, 8, 512], BF16)
        Rmx = Res("mx")
        x2r = Rot(S, st, nc, "x2o", [128, D], F32, 2)

        for ti, (t0, p0, pt) in enumerate(tiles):
            U, RU, dU = Ur.get()
            S.dma(U[:], uS[:, pt - 15:pt + 527].rearrange("(c p) t -> p c t", p=128), writes=[RU], dsem=dU)
            At, RA, dA = Ar.get()
            S.dma(At[:], aS[:, t0:t0 + 512].rearrange("(h e) t -> e h t", h=4), writes=[RA], dsem=dA)
            for c in range(4):
                bk = nbank()
                for k in range(31):
                    mm(PS[:, bk, :], Dg[:, c * 31 + k, :], U[:, c, k:k + 512], k == 0, k == 30, [RDg, RU], [RB[bk]])
                S.op("act", lambda e: e.activation(out=uc[:, c, :], in_=PS[:, bk, :], func=AF.Identity, bias=cpar[:, 0, c:c + 1]),
                     reads=[RB[bk], Rcp], writes=[Ruc])
                S.op("act", lambda e: e.activation(out=sq[:, c, :], in_=PS[:, bk, :], func=AF.Square, bias=cpar[:, 0, c:c + 1]),
                     reads=[RB[bk], Rcp], writes=[Rsq])
            bme = nbank()
            bex = nbank()
            for c in range(4):
                mm(PS[:, bme, :], onesM[:], uc[:, c, :], c == 0, c == 3, [RC, Ruc], [RB[bme]])
            for c in range(4):
                mm(PS[:, bex, :], onesM[:], sq[:, c, :], c == 0, c == 3, [RC, Rsq], [RB[bex]])
            S.op("act", lambda e: e.activation(out=msq[:], in_=PS[:, bme, :], func=AF.Square), reads=[RB[bme]], writes=[Rst])
            S.op("dve", lambda e: e.tensor_tensor(rstd[:], PS[:, bex, :], msq[:], op=ALU.subtract), reads=[RB[bex], Rst], writes=[Rst])
            S.op("act", lambda e: e.activation(out=rstd[:], in_=rstd[:], func=AF.Sqrt, bias=epsc[:]), reads=[Rst, RC], writes=[Rst])
            S.op("dve", lambda e: e.reciprocal(rstd[:], rstd[:]), reads=[Rst], writes=[Rst])
            S.op("dve", lambda e: e.tensor_tensor(nmr[:], PS[:, bme, :], rstd[:], op=ALU.mult), reads=[RB[bme], Rst], writes=[Rst])
            for c in range(4):
                t_, Rt_, _ = tt.get()
                S.op("dve", lambda e: e.tensor_tensor(t_[:], uc[:, c, :], rstd[:], op=ALU.mult), reads=[Ruc, Rst], writes=[Rt_])
                S.op("pool", lambda e: e.tensor_tensor(t_[:], t_[:], nmr[:], op=ALU.subtract), reads=[Rt_, Rst], writes=[Rt_])
                S.op("act", lambda e: e.activation(out=sl[:, c, :], in_=t_[:], func=AF.Silu, bias=cpar[:, 2, c:c + 1], scale=cpar[:, 1, c:c + 1]),
                     reads=[Rt_, Rcp], writes=[Rsl])
            for dc in range(8):
                G, RG, dG = Gr.get()
                S.dma(G[:], gS[:, t0:t0 + 512].rearrange("(b c p) t -> p b c t", b=2, p=128)[:, :, dc, :], writes=[RG], dsem=dG)
                bcv = nbank()
                bat = nbank()
                for c in range(4):
                    mm(PS[:, bcv, :], pww[:, c, dc * 128:(dc + 1) * 128], sl[:, c, :], c == 0, c == 3, [Rpww, Rsl], [RB[bcv]])
                for hs in range(4):
                    mm(PS[:, bat, :], wup[:, hs, dc * 128:(dc + 1) * 128], At[:, hs, :], hs == 0, hs == 3, [Rwup, RA], [RB[bat]])
                a1 = m1.get()
                a2 = m2.get()
                S.op("dve", lambda e: e.scalar_tensor_tensor(a1[0][:], PS[:, bcv, :], pwb[:, dc:dc + 1], G[:, 1, :], op0=ALU.add, op1=ALU.mult),
                     reads=[RB[bcv], Rpwb, RG], writes=[a1[1]])
                S.op("dve", lambda e: e.tensor_tensor(a2[0][:], PS[:, bat, :], G[:, 0, :], op=ALU.mult), reads=[RB[bat], RG], writes=[a2[1]])
                S.op("pool", lambda e: e.tensor_tensor(mx[:, dc, :], a1[0][:], a2[0][:], op=ALU.add), reads=[a1[1], a2[1]], writes=[Rmx])
            for a in range(4):
                xt_, Rx, dX = xr.get()
                S.dma(xt_[:], x_in[t0 + a * 128:t0 + (a + 1) * 128, :], writes=[Rx], dsem=dX)
                b0 = 4 + 2 * (a % 2)
                for n in range(2):
                    for kc in range(8):
                        mm(PS[:, b0 + n, :], mx[:, kc, a * 128:(a + 1) * 128], wout[:, kc, n * 512:(n + 1) * 512], kc == 0, kc == 7,
                           [Rmx, Rwout], [RB[b0 + n]])
                o, Ro, dO = x2r.get()
                S.op("dve", lambda e: e.tensor_tensor(o[:].rearrange("p (n f) -> p n f", n=2), PS[:, b0:b0 + 2, :],
                                                     xt_[:].rearrange("p (n f) -> p n f", n=2), op=ALU.add),
                     reads=[RB[b0], RB[b0 + 1], Rx], writes=[Ro])
                S.dma(x2S[t0 + a * 128:t0 + (a + 1) * 128, :], o[:], reads=[Ro], dsem=dO)
    barrier()

    with ExitStack() as st:
        wq, Rwq = load_w_bf16(st, "wq", wq_in, 8, 2048)
        kst = sb(st, "kst", [128, 16, 128], F32)
        Rks = Res("kst")
        S.dma(kst[:], keysT_in, writes=[Rks], dsem=S.new_dsem("kst"))
        keysT = sb(st, "keysT", [128, 16, 128], BF16)
        RkT = Res("keysT")
        S.op("dve", lambda e: e.tensor_copy(keysT[:], kst[:]), reads=[Rks], writes=[RkT])
        g2b, Rg2 = load_small(st, "g2b", g2b_in, [128, D])
        xr = Rot(S, st, nc, "x2i", [128, D], F32, 3)
        junk = sb(st, "junk2", [128, D], BF16)
        Rjunk = Res("junk2")
        ssq = sb(st, "ssq2", [128, 8], F32)
        Rssq = [Res("ssq2_%d" % i) for i in range(8)]
        hbr = Rot(S, st, nc, "h2b", [128, D], BF16, 2)
        hTr = Rot(S, st, nc, "h2T", [128, 8, 512], BF16, 2)
        qpT = sb(st, "qpT", [128, 16, 512], BF16)
        RqpT = Res("qpT")
        class BS:
            pass

        BSETS = []
        for bi_ in range(2):
            B = BS()
            B.Ssb = sb(st, "Ssb%d" % bi_, [128, 16, 128], F32)
            B.RS_ = Res("Ssb")
            B.wk = sb(st, "wk%d" % bi_, [128, 16, 128], F32)
            B.Rwk = Res("wk")
            B.tops = sb(st, "tops%d" % bi_, [128, 16, 16], F32)
            B.Rtops = Res("tops")
            B.tidx = sb(st, "tidx%d" % bi_, [128, 16, 16], U32)
            B.Rtidx = Res("tidx")
            B.tif = sb(st, "tif%d" % bi_, [128, 16, 16], F32)
            B.Rtif = Res("tif")
            B.best = sb(st, "best%d" % bi_, [128, 8, 16], F32)
            B.Rbest = Res("best")
            B.bidx = sb(st, "bidx%d" % bi_, [128, 8, 16], U32)
            B.Rbidx = Res("bidx")
            B.ab_u = sb(st, "ab_u%d" % bi_, [128, 2, 8, 16], U32)
            B.ab_f = sb(st, "ab_f%d" % bi_, [128, 2, 8, 16], F32)
            B.Rab = Res("ab")
            B.ge = sb(st, "ge%d" % bi_, [128, 8, 16], F32)
            B.gs = sb(st, "gs%d" % bi_, [128, 8], F32)
            B.Rge = Res("ge")
            B.E0 = sb(st, "E0%d" % bi_, [128, 8, 16, 16], F32)
            B.RE0 = Res("E0")
            B.Rt = sb(st, "Rt%d" % bi_, [128, 3, 128], F32)
            B.RRt = Res("Rt")
            BSETS.append(B)
        RTr = Rot(S, st, nc, "RT", [128, 3, 128], F32, 2)

        def routing(a, B, t0):
            Ssb, RS_, wk, Rwk = B.Ssb, B.RS_, B.wk, B.Rwk
            tops, Rtops, tidx, Rtidx, tif, Rtif = B.tops, B.Rtops, B.tidx, B.Rtidx, B.tif, B.Rtif
            best, Rbest, bidx, Rbidx = B.best, B.Rbest, B.bidx, B.Rbidx
            ab_u, ab_f, Rab, ge, gs, Rge = B.ab_u, B.ab_f, B.Rab, B.ge, B.gs, B.Rge
            Rt, RRt = B.Rt, B.RRt
            cand = wk[:].rearrange("p (h c) k -> p h (c k)", c=2)
            Rcand = Rwk
            wk2 = Ssb[:].rearrange("p (h c) k -> p h (c k)", c=2)
            Rwk2 = RS_
            for hc in range(16):
                b = hc // 4
                mm(PS[:, b, (hc % 4) * 128:(hc % 4 + 1) * 128], qpT[:, hc, a * 128:(a + 1) * 128], keysT[:, hc, :], True, True,
                   [RqpT, RkT], [RB[b]])
            S.op("act", lambda e: e.copy(Ssb[:].rearrange("p (b q) k -> p b (q k)", b=4), PS[:, 0:4, :]),
                 reads=[RB[0], RB[1], RB[2], RB[3]], writes=[RS_])
            yield
            for hc in range(16):
                S.op("dve", lambda e: e.max(out=tops[:, hc, 0:8], in_=Ssb[:, hc, :]), reads=[RS_], writes=[Rtops])
            yield
            for hc in range(16):
                S.op("dve", lambda e: e.max_index(out=tidx[:, hc, 0:8], in_max=tops[:, hc, 0:8], in_values=Ssb[:, hc, :]),
                     reads=[RS_, Rtops], writes=[Rtidx])
            for hc in range(16):
                S.op("dve", lambda e: e.match_replace(out=wk[:, hc, :], in_to_replace=tops[:, hc, 0:8], in_values=Ssb[:, hc, :], imm_value=-1e30),
                     reads=[RS_, Rtops], writes=[Rwk])
            yield
            for hc in range(16):
                S.op("dve", lambda e: e.max(out=tops[:, hc, 8:16], in_=wk[:, hc, :]), reads=[Rwk], writes=[Rtops])
            yield
            for hc in range(16):
                S.op("dve", lambda e: e.max_index(out=tidx[:, hc, 8:16], in_max=tops[:, hc, 8:16], in_values=wk[:, hc, :]),
                     reads=[Rwk, Rtops], writes=[Rtidx])
            yield
            S.op("dve", lambda e: e.tensor_copy(tif[:], tidx[:]), reads=[Rtidx], writes=[Rtif])
            tops4 = tops[:].rearrange("p (h c) k -> p h c k", c=2)
            tif4 = tif[:].rearrange("p (h c) k -> p h c k", c=2)
            S.op("dve", lambda e: e.tensor_tensor(cand.rearrange("p h (a b) -> p h a b", a=16),
                                                 tops4[:, :, 0, :].unsqueeze(3).broadcast_to([128, 8, 16, 16]),
                                                 tops4[:, :, 1, :].unsqueeze(2).broadcast_to([128, 8, 16, 16]), op=ALU.add),
                 reads=[Rtops], writes=[Rcand])
            yield
            for h in range(8):
                S.op("dve", lambda e: e.max(out=best[:, h, 0:8], in_=cand[:, h, :]), reads=[Rcand], writes=[Rbest])
            yield
            for h in range(8):
                S.op("dve", lambda e: e.max_index(out=bidx[:, h, 0:8], in_max=best[:, h, 0:8], in_values=cand[:, h, :]),
                     reads=[Rcand, Rbest], writes=[Rbidx])
            for h in range(8):
                S.op("dve", lambda e: e.match_replace(out=wk2[:, h, :], in_to_replace=best[:, h, 0:8], in_values=cand[:, h, :], imm_value=-1e30),
                     reads=[Rcand, Rbest], writes=[Rwk2])
            yield
            for h in range(8):
                S.op("dve", lambda e: e.max(out=best[:, h, 8:16], in_=wk2[:, h, :]), reads=[Rwk2], writes=[Rbest])
            yield
            for h in range(8):
                S.op("dve", lambda e: e.max_index(out=bidx[:, h, 8:16], in_max=best[:, h, 8:16], in_values=wk2[:, h, :]),
                     reads=[Rwk2, Rbest], writes=[Rbidx])
            S.op("dve", lambda e: e.tensor_tensor(ge[:], best[:], best[:, :, 0:1].broadcast_to([128, 8, 16]), op=ALU.subtract),
                 reads=[Rbest], writes=[Rge])
            yield
            S.op("act", lambda e: e.activation(out=ge[:], in_=ge[:], func=AF.Exp), reads=[Rge], writes=[Rge])
            S.op("dve", lambda e: e.tensor_single_scalar(ab_u[:, 0, :, :], bidx[:], 4, op=ALU.logical_shift_right), reads=[Rbidx], writes=[Rab])
            S.op("dve", lambda e: e.tensor_single_scalar(ab_u[:, 1, :, :], bidx[:], 15, op=ALU.bitwise_and), reads=[Rbidx], writes=[Rab])
            yield
            S.op("dve", lambda e: e.tensor_copy(ab_f[:], ab_u[:]), reads=[Rab], writes=[Rab])
            S.op("dve", lambda e: e.reduce_sum(gs[:], ge[:], axis=AX.X), reads=[Rge], writes=[Rge])
            yield
            S.op("dve", lambda e: e.reciprocal(gs[:], gs[:]), reads=[Rge], writes=[Rge])
            yield
            S.op("dve", lambda e: e.tensor_tensor(Rt[:, 2, :].rearrange("p (h k) -> p h k", h=8), ge[:],
                                                 gs[:].unsqueeze(2).broadcast_to([128, 8, 16]), op=ALU.mult),
                 reads=[Rge], writes=[RRt])
            io16 = iota_f[:, 0:16].unsqueeze(1).unsqueeze(1).broadcast_to([128, 8, 16, 16])
            for c in range(2):
                EE, REE = B.E0, B.RE0
                S.op("dve", lambda e: e.tensor_tensor(EE[:], io16, ab_f[:, c, :, :].unsqueeze(3).broadcast_to([128, 8, 16, 16]), op=ALU.is_equal),
                     reads=[Rab, RC], writes=[REE])
                yield
                S.op("pool", lambda e: e.tensor_tensor(EE[:], EE[:], tif4[:, :, c, :].unsqueeze(2).broadcast_to([128, 8, 16, 16]), op=ALU.mult),
                     reads=[REE, Rtif], writes=[REE])
                S.op("dve", lambda e: e.reduce_sum(Rt[:, c, :].rearrange("p (h k) -> p h k", h=8), EE[:], axis=AX.X),
                     reads=[REE], writes=[RRt])
                yield
            for c in range(3):
                S.op("pe", lambda e: e.transpose(PS[:, 6, c * 128:(c + 1) * 128], Rt[:, c, :], ident_f[:]), reads=[RRt, RC], writes=[RB[6]])
            RT_, RRT, dRT = RTr.get()
            S.op("act", lambda e: e.copy(RT_[:], PS[:, 6, 0:384].rearrange("p (c t) -> p c t", c=3)), reads=[RB[6]], writes=[RRT])
            tb = t0 + a * 128
            S.dma(rS[:, :, tb:tb + 128].rearrange("c p t -> p c t"), RT_[:], reads=[RRT], dsem=dRT)

        for ti, (t0, p0, pt) in enumerate(tiles):
            hT, RH, dH = hTr.get()
            for a in range(4):
                bi = ti * 4 + a
                xt_, Rx, dX = xr.get()
                S.dma(xt_[:], x2S[t0 + a * 128:t0 + (a + 1) * 128, :], writes=[Rx], dsem=dX)
                hb, Rhb, _ = hbr.get()
                col = bi % 8
                rms_block(xt_[:], Rx, g2b, Rg2, hb[:], Rhb, junk, Rjunk, ssq, Rssq[col], col)
                bank = 6 + (bi % 2)
                ptb = PS[:, bank, :].bitcast(BF16)
                for kc in range(8):
                    S.op("pe", lambda e: e.transpose(ptb[:, kc * 128:(kc + 1) * 128], hb[:, kc * 128:(kc + 1) * 128], ident_bf[:]),
                         reads=[Rhb], writes=[RB[bank]])
                S.op("act", lambda e: e.copy(hT[:, :, a * 128:(a + 1) * 128], ptb.rearrange("p (k t) -> p k t", k=8)),
                     reads=[RB[bank]], writes=[RH])
            S.dma(h2S[:, t0:t0 + 512].rearrange("(k p) t -> p k t", p=128), hT[:], reads=[RH], dsem=dH)
            for hc in range(16):
                bk = 4 + (hc % 2)
                for kc in range(8):
                    mm(PS[:, bk, :], wq[:, kc, hc * 128:(hc + 1) * 128], hT[:, kc, :], kc == 0, kc == 7, [Rwq, RH], [RB[bk]])
                if hc % 2 == 0:
                    S.op("act", lambda e: e.copy(qpT[:, hc, :], PS[:, bk, :]), reads=[RB[bk]], writes=[RqpT])
                else:
                    S.op("dve", lambda e: e.tensor_copy(qpT[:, hc, :], PS[:, bk, :]), reads=[RB[bk]], writes=[RqpT])
            for a0 in (0, 2):
                gens = [routing(a0, BSETS[0], t0), routing(a0 + 1, BSETS[1], t0)]
                alive = [True, True]
                while any(alive):
                    for gi in range(2):
                        if alive[gi]:
                            try:
                                next(gens[gi])
                            except StopIteration:
                                alive[gi] = False
    barrier()

    TT = 256
    with ExitStack() as st:
        gfb, Rgf = load_small(st, "gfb", gfb_in, [128, D])
        WTs = [sb(st, "WT%d" % i, [128, TT, 128], BF16) for i in range(2)]
        RWTs = [Res("WT%d" % i) for i in range(2)]
        rtr = Rot(S, st, nc, "rt4", [128, 3, TT], F32, 2)
        h2r = Rot(S, st, nc, "h2T4", [128, 8, TT], BF16, 2)
        x2r = Rot(S, st, nc, "x24", [128, D], F32, 1)
        TB = 16
        Lr = Rot(S, st, nc, "Lb", [128, TB, 128], BF16, 2)
        Rr = Rot(S, st, nc, "Rb", [128, TB, 128], BF16, 2)
        Ubr = Rot(S, st, nc, "Ub", [128, 2, D], BF16, 3)
        Vbr = Rot(S, st, nc, "Vb", [128, 2, D], BF16, 3)
        Asb = Rot(S, st, nc, "Asb", [128, 2, TT], BF16, 2)
        WAr = Rot(S, st, nc, "WA", [128, 2, TT], BF16, 2)
        yo = Rot(S, st, nc, "yo", [128, D], F32, 2)
        junk = sb(st, "junk4", [128, D], BF16)
        Rjunk = Res("junk4")
        ssq = sb(st, "ssq4", [128, 8], F32)
        Rssq = [Res("ssq4_%d" % i) for i in range(8)]
        ntile = NT // TT
        NP = 64
        NTB = TT // TB
        PER = NP // NTB
        pend = {}

        def p4_load(ti):
            tb = ti * TT
            r_ = rtr.get()
            S.dma(r_[0][:], rS[:, :, tb:tb + TT].rearrange("c p t -> p c t"), writes=[r_[1]], dsem=r_[2])
            h_ = h2r.get()
            S.dma(h_[0][:], h2S[:, tb:tb + TT].rearrange("(k p) t -> p k t", p=128), writes=[h_[1]], dsem=h_[2])
            pend[ti] = (r_, h_)

        chunkq = {}
        cnext = [0]
        total = ntile * NP

        def p4_chunk_prefetch(upto):
            while cnext[0] <= upto and cnext[0] < total:
                ip = cnext[0] % NP
                u_ = Ubr.get()
                v_ = Vbr.get()
                S.dma(u_[0][:], UT[2 * ip:2 * ip + 2].rearrange("i p f -> p i f"), writes=[u_[1]], dsem=u_[2])
                S.dma(v_[0][:], VB[2 * ip * 128:(2 * ip + 2) * 128, :].rearrange("(i p) f -> p i f", p=128), writes=[v_[1]], dsem=v_[2])
                chunkq[cnext[0]] = (u_, v_)
                cnext[0] += 1

        io = iota_f[:].unsqueeze(1).broadcast_to([128, TB, 128])
        evn = [0]

        def onehot(ti, tb):
            rt_, Rrt, _ = pend[ti][0]
            Lb, RL, _ = Lr.get()
            Rb, RR, _ = Rr.get()
            S.op("dve", lambda e: e.tensor_tensor(Lb[:], io, rt_[:, 0, tb * TB:(tb + 1) * TB].unsqueeze(2).broadcast_to([128, TB, 128]), op=ALU.is_equal),
                 reads=[Rrt, RC], writes=[RL])
            S.op("dve", lambda e: e.tensor_tensor(Lb[:], Lb[:], rt_[:, 2, tb * TB:(tb + 1) * TB].unsqueeze(2).broadcast_to([128, TB, 128]), op=ALU.mult),
                 reads=[Rrt, RL], writes=[RL])
            S.op("dve", lambda e: e.tensor_tensor(Rb[:], io, rt_[:, 1, tb * TB:(tb + 1) * TB].unsqueeze(2).broadcast_to([128, TB, 128]), op=ALU.is_equal),
                 reads=[Rrt, RC], writes=[RR])
            return (Lb, RL, Rb, RR)

        def wmm(ti, tb, LR):
            Lb, RL, Rb, RR = LR
            WT = WTs[ti % 2]
            RWT = RWTs[ti % 2]
            for q4 in range(TB // 4):
                bk = 6 + (evn[0] % 2)
                evn[0] += 1
                pw = PS[:, bk, :].rearrange("p (t i) -> p t i", t=4)
                for tq in range(4):
                    tl = q4 * 4 + tq
                    mm(pw[:, tq, :], Rb[:, tl, :], Lb[:, tl, :], True, True, [RL, RR], [RB[bk]])
                tg = tb * TB + q4 * 4
                S.op("act", lambda e: e.copy(WT[:, tg:tg + 4, :], pw), reads=[RB[bk]], writes=[RWT])

        def u_mm(ci):
            ti, ip = divmod(ci, NP)
            h2T, Rh2, _ = pend[ti][1]
            (Ub, RU, _), _v = chunkq[ci]
            bk = 4 + (ci % 2)
            pa = PS[:, bk, :].rearrange("p (i t) -> p i t", i=2)
            for ii in range(2):
                for kc in range(8):
                    mm(pa[:, ii, :], Ub[:, ii, kc * 128:(kc + 1) * 128], h2T[:, kc, :], kc == 0, kc == 7, [RU, Rh2], [RB[bk]])

        p4_load(0)
        p4_chunk_prefetch(2)
        for tb in range(NTB):
            wmm(0, tb, onehot(0, tb))
        u_mm(0)
        pendLR = None
        for ti in range(ntile):
            if ti + 1 < ntile:
                p4_load(ti + 1)
            WT = WTs[ti % 2]
            RWT = RWTs[ti % 2]
            for ip in range(NP):
                ci = ti * NP + ip
                p4_chunk_prefetch(ci + 2)
                if ti + 1 < ntile and ip % PER == 0:
                    if pendLR is not None:
                        wmm(ti + 1, ip // PER - 1, pendLR)
                    pendLR = onehot(ti + 1, ip // PER)
                if ci + 1 < total:
                    u_mm(ci + 1)
                _u, (Vb, RVb, _) = chunkq.pop(ci)
                bk = 4 + (ci % 2)
                pa = PS[:, bk, :].rearrange("p (i t) -> p i t", i=2)
                A_, RA_, _ = Asb.get()
                S.op("act", lambda e: e.activation(out=A_[:], in_=pa, func=AF.Gelu), reads=[RB[bk]], writes=[RA_])
                WA, RWA, _ = WAr.get()
                S.op("pool", lambda e: e.tensor_tensor(WA[:], A_[:], WT[:, :, 2 * ip:2 * ip + 2].rearrange("p t i -> p i t"), op=ALU.mult),
                     reads=[RA_, RWT], writes=[RWA])
                for ii in range(2):
                    for th in range(2):
                        for n in range(2):
                            bo = th * 2 + n
                            mm(PS[:, bo, :], WA[:, ii, th * 128:(th + 1) * 128], Vb[:, ii, n * 512:(n + 1) * 512],
                               ip == 0 and ii == 0, ip == NP - 1 and ii == 1, [RWA, RVb], [RB[bo]])
            if pendLR is not None:
                wmm(ti + 1, NTB - 1, pendLR)
                pendLR = None
            for th in range(2):
                tb_ = ti * TT + th * 128
                o, Ro, dX = x2r.get()
                S.dma(o[:], x2S[tb_:tb_ + 128, :], writes=[Ro], dsem=dX)
                S.op("dve", lambda e: e.tensor_tensor(o[:].rearrange("p (n f) -> p n f", n=2), PS[:, 2 * th:2 * th + 2, :],
                                                     o[:].rearrange("p (n f) -> p n f", n=2), op=ALU.add),
                     reads=[RB[2 * th], RB[2 * th + 1], Ro], writes=[Ro])
                y_, Ry, dY = yo.get()
                col = (ti * 2 + th) % 8
                S.op("act", lambda e: e.activation(out=junk[:], in_=o[:], func=AF.Square, accum_out=ssq[:, col:col + 1]),
                     reads=[Ro], writes=[Rjunk, Rssq[col]])
                S.op("act", lambda e: e.activation(out=ssq[:, col:col + 1], in_=ssq[:, col:col + 1], func=AF.Sqrt, bias=epsc[:], scale=1.0 / D),
                     reads=[Rssq[col], RC], writes=[Rssq[col]])
                S.op("dve", lambda e: e.reciprocal(ssq[:, col:col + 1], ssq[:, col:col + 1]), reads=[Rssq[col]], writes=[Rssq[col]])
                S.op("dve", lambda e: e.scalar_tensor_tensor(y_[:], o[:], ssq[:, col:col + 1], gfb[:], op0=ALU.mult, op1=ALU.mult),
                     reads=[Ro, Rssq[col], Rgf], writes=[Ry])
                S.dma(y_out[tb_:tb_ + 128, :], y_[:], reads=[Ry], dsem=dY)
            pend.pop(ti)
    barrier()
    top.close()
    return nc


def rope_tables(smax=8192):
    half = 8
    inv = (np.float32(500000.0) ** (-np.arange(half, dtype=np.float32) * np.float32(2.0) / np.float32(16))).astype(np.float32)
    pos = np.arange(smax, dtype=np.float32)
    ang = (pos[:, None] * inv[None, :]).astype(np.float32)
    cos = np.cos(ang).astype(np.float32).T
    sin = np.sin(ang).astype(np.float32).T
    C = np.ones((128, smax), np.float32)
    Sn = np.zeros((128, smax), np.float32)
    for hh in range(2):
        b = hh * 64
        C[b:b + 8] = cos
        C[b + 8:b + 16] = cos
        Sn[b:b + 8] = sin
        Sn[b + 8:b + 16] = sin
    return C, Sn


def make_shared(inp):
    f = lambda a: np.ascontiguousarray(np.asarray(a, dtype=np.float32))
    C, Sn = rope_tables()
    sh = {
        "w_in": f(inp["w_in"][0]),
        "g1b": f(np.broadcast_to(inp["norm1_g"][0][None, :], (128, D))),
        "g2b": f(np.broadcast_to(inp["norm2_g"][0][None, :], (128, D))),
        "gfb": f(np.broadcast_to(np.asarray(inp["final_g"])[None, :], (128, D))),
        "bgate": f(np.asarray(inp["b_gate"][0]).reshape(16, 128).T),
        "w_up": f(inp["w_attn_up"][0]),
        "dww": f(np.asarray(inp["conv_dw_w"][0]).reshape(31, 4, 128).transpose(2, 1, 0)),
        "cpar": f(np.stack([np.asarray(inp["conv_dw_b"][0]).reshape(4, 128).T,
                            np.asarray(inp["conv_ln_g"][0]).reshape(4, 128).T,
                            np.asarray(inp["conv_ln_b"][0]).reshape(4, 128).T], axis=1)),
        "pw_w": f(inp["conv_pw_w"][0]),
        "pwb": f(np.asarray(inp["conv_pw_b"][0]).reshape(8, 128).T),
        "w_out": f(inp["w_out"][0]),
        "wq": f(inp["peer_wq"][0]),
        "keysT": f(np.asarray(inp["peer_keys"][0]).reshape(16, 128, 128).transpose(2, 0, 1)),
        "peer_u": f(inp["peer_u"][0]),
        "peer_v": f(inp["peer_v"][0]),
        "rope_c": C,
        "rope_s": Sn,
        "dummy8": np.zeros((1, 8), np.float32),
    }
    return sh


def kernel(**inputs):
    xp = np.asarray(inputs["x_prompt"], dtype=np.float32)
    xs = np.asarray(inputs["x_sample"], dtype=np.float32)
    seqs = [xp.shape[1], xs.shape[1], xs.shape[1]]
    nc = build_program(seqs)
    sh = make_shared(inputs)
    in_maps = []
    for c in range(N_CORES):
        m = dict(sh)
        m["x"] = np.ascontiguousarray(np.concatenate([xp[c], xs[2 * c], xs[2 * c + 1]], axis=0))
        in_maps.append(m)
    res = run_bass_kernel_spmd(nc, in_maps, core_ids=list(range(N_CORES)))
    yp = np.empty_like(xp)
    ys = np.empty_like(xs)
    Lp, Ls = seqs[0], seqs[1]
    for c in range(N_CORES):
        y = np.asarray(res.results[c]["y"])
        yp[c] = y[0:Lp]
        ys[2 * c] = y[Lp:Lp + Ls]
        ys[2 * c + 1] = y[Lp + Ls:Lp + 2 * Ls]
    return (yp, ys)
```

```python
from contextlib import ExitStack
import numpy as np
import concourse.bass as bass
import concourse.mybir as mybir
from concourse.bass_utils import run_bass_kernel_spmd

F32 = mybir.dt.float32
BF16 = mybir.dt.bfloat16
U32 = mybir.dt.uint32
AF = mybir.ActivationFunctionType
ALU = mybir.AluOpType
AX = mybir.AxisListType

D = 1024
NCOL = 5376
PAD = 1024
NEXP = 16384
EPS = 1e-6
DIL = (1, 4, 16)
NEG = -30000.0
N_CORES = 8


class Res:
    __slots__ = ("name", "w", "rd")

    def __init__(self, name):
        self.name = name
        self.w = None
        self.rd = {}


class DSem:
    def __init__(self, h, key):
        self.h = h
        self.key = key
        self.count = 0


class Sched:
    CE = ("pe", "act", "dve", "pool")

    def __init__(self, nc):
        self.nc = nc
        self.E = {"pe": nc.tensor, "act": nc.scalar, "dve": nc.vector, "pool": nc.gpsimd, "sp": nc.sync}
        self.semh = {}
        self.cnt = {}
        self.waited = {e: {} for e in self.E}
        for e in self.CE:
            self.semh[e] = nc.alloc_semaphore("s_" + e)
            self.cnt[e] = 0
        self.dsems = []
        self.bar = nc.alloc_semaphore("s_bar")
        self.barcnt = 0
        self.n_ds = 0
        self.free_ds = []
        self.live_ds = []

    def new_dsem(self, name=None):
        if self.free_ds:
            d = self.free_ds.pop()
            self.live_ds.append(d)
            return d
        self.n_ds += 1
        name = "d%d_%s" % (self.n_ds, name or "")
        d = DSem(self.nc.alloc_semaphore(name), name)
        self.semh[name] = d.h
        self.dsems.append(d)
        self.live_ds.append(d)
        return d

    def recycle(self):
        self.free_ds.extend(self.live_ds)
        self.live_ds = []

    def _wait(self, eng, key, val):
        if self.waited[eng].get(key, 0) >= val:
            return
        self.E[eng].wait_ge(self.semh[key], val)
        self.waited[eng][key] = val

    def _deps(self, eng, reads, writes, is_dma=False):
        deps = {}

        def add(key, val, kind):
            if key == eng and not is_dma and (eng == "pe" or kind != "raw"):
                return
            if deps.get(key, 0) < val:
                deps[key] = val

        for r in reads:
            if r.w is not None:
                add(r.w[0], r.w[1], "raw")
        for r in writes:
            if r.w is not None:
                add(r.w[0], r.w[1], "waw")
            for k, v in r.rd.items():
                add(k, v, "war")
        for k, v in deps.items():
            self._wait(eng, k, v)

    def op(self, eng, fn, reads=(), writes=()):
        self._deps(eng, reads, writes)
        ins = fn(self.E[eng])
        self.cnt[eng] += 1
        c = self.cnt[eng]
        ins.then_inc(self.semh[eng], 1)
        for r in reads:
            r.rd[eng] = c
        for r in writes:
            r.w = (eng, c)
            r.rd = {}
        return ins

    def dma(self, out, in_, reads=(), writes=(), dsem=None, first=True, q="sp"):
        if isinstance(dsem, LazyD):
            dsem = dsem.get()
        self._deps(q, reads, writes, is_dma=True)
        if first and dsem.count > 0:
            self._wait(q, dsem.key, dsem.count)
        ins = self.E[q].dma_start(out=out, in_=in_)
        dsem.count += 16
        ins.then_inc(dsem.h, 16)
        for r in reads:
            r.rd[dsem.key] = dsem.count
        for r in writes:
            r.w = (dsem.key, dsem.count)
            r.rd = {}
        return ins

    def barrier(self, dummy_sb, dummy_dram):
        for e in self.CE:
            if self.cnt[e] > 0:
                self._wait("sp", e, self.cnt[e])
        for d in self.dsems:
            if d.count > 0:
                self._wait("sp", d.key, d.count)
        ins = self.E["sp"].dma_start(out=dummy_sb, in_=dummy_dram)
        self.barcnt += 16
        ins.then_inc(self.bar, 16)
        for e in self.E:
            self.E[e].wait_ge(self.bar, self.barcnt)
            for k in self.CE:
                self.waited[e][k] = self.cnt[k]
            for d in self.dsems:
                self.waited[e][d.key] = d.count


class LazyD:
    def __init__(self, S, name):
        self.S = S
        self.name = name
        self.d = None

    def get(self):
        if self.d is None:
            self.d = self.S.new_dsem(self.name)
        return self.d


class Rot:
    ctr = 0

    def __init__(self, S, st, nc, name, shape, dtype, n):
        self.items = []
        for i in range(n):
            Rot.ctr += 1
            t = st.enter_context(nc.sbuf_tensor("rot%d_%s%d" % (Rot.ctr, name, i), list(shape), dtype))
            self.items.append((t, Res("%s%d" % (name, i)), LazyD(S, name)))
        self.i = 0

    def get(self):
        it = self.items[self.i % len(self.items)]
        self.i += 1
        return it


def build_program(seqs, dbg=False):
    nc = bass.Bass("TRN2", target_bir_lowering=False)
    NT = sum(seqs)
    NTP = NT + 2 * PAD * len(seqs)
    seqinfo = []
    ts, ps_ = 0, 0
    for L in seqs:
        assert L % 2048 == 0
        seqinfo.append((ts, L, ps_))
        ts += L
        ps_ += L + 2 * PAD

    def din(name, shape, dt=F32):
        return nc.dram_tensor(name, list(shape), dt, kind="ExternalInput").ap()

    def dscr(name, shape, dt):
        if dbg:
            return nc.dram_tensor(name, list(shape), dt, kind="ExternalOutput").ap()
        return nc.dram_tensor(name, list(shape), dt).ap()

    x_in = din("x", [NT, D])
    w_in = din("w_in", [D, NCOL])
    g1b_in = din("g1b", [128, D])
    g2b_in = din("g2b", [128, D])
    gfb_in = din("gfb", [128, D])
    bgate_in = din("bgate", [128, 16])
    wup_in = din("w_up", [256, D])
    dww_in = din("dww", [128, 4, 31])
    cpar_in = din("cpar", [128, 3, 4])
    pww_in = din("pw_w", [512, D])
    pwb_in = din("pwb", [128, 8])
    wout_in = din("w_out", [D, D])
    wq_in = din("wq", [D, 2048])
    keysT_in = din("keysT", [128, 16, 128])
    pu_in = din("peer_u", [NEXP, D])
    pv_in = din("peer_v", [NEXP, D])
    ropec_in = din("rope_c", [128, 8192])
    ropes_in = din("rope_s", [128, 8192])
    dummy_in = din("dummy8", [1, 8])
    y_out = nc.dram_tensor("y", [NT, D], F32, kind="ExternalOutput").ap()

    qS = dscr("qS", [768, NT], BF16)
    kS = dscr("kS", [768, NTP], BF16)
    vS = dscr("vS", [NTP, 768], BF16)
    uS = dscr("uS", [512, NTP], BF16)
    gS = dscr("gS", [2048, NT], BF16)
    aS = dscr("aS", [256, NT], BF16)
    x2S = dscr("x2S", [NT, D], F32)
    h2S = dscr("h2S", [D, NT], BF16)
    rS = dscr("rS", [3, 128, NT], F32)
    UT = nc.dram_tensor("UT", [128, 128, 1024], BF16).ap()
    VB = nc.dram_tensor("VB", [NEXP, D], BF16).ap()

    S = Sched(nc)
    top = ExitStack()

    nctr = [0]

    def sb(st, name, shape, dt):
        nctr[0] += 1
        return st.enter_context(nc.sbuf_tensor("sb%d_%s" % (nctr[0], name), list(shape), dt))

    PS = top.enter_context(nc.psum_tensor("PS", [128, 8, 512], F32))
    RB = [Res("bank%d" % i) for i in range(8)]
    ident_bf = sb(top, "ident_bf", [128, 128], BF16)
    ident_f = sb(top, "ident_f", [128, 128], F32)
    iota_f = sb(top, "iota_f", [128, 128], F32)
    dif = sb(top, "dif", [128, 128], F32)
    maskAB = sb(top, "maskAB", [128, 2, 128], BF16)
    pcol = sb(top, "pcol", [128, 1], F32)
    bz = sb(top, "bz", [128, 1], F32)
    epsc = sb(top, "epsc", [128, 1], F32)
    blo = sb(top, "blo", [128, 1], F32)
    bhi = sb(top, "bhi", [128, 1], F32)
    ones65 = sb(top, "ones65", [65, 64], F32)
    onesM = sb(top, "onesM", [128, 128], BF16)
    dummy_sb = sb(top, "dummy_sb", [1, 8], F32)
    RC = Res("consts")

    S.op("pool", lambda e: e.iota(dif[:], pattern=[[1, 128]], base=0, channel_multiplier=-1,
                                  allow_small_or_imprecise_dtypes=True), writes=[RC])
    S.op("pool", lambda e: e.iota(iota_f[:], pattern=[[1, 128]], base=0, channel_multiplier=0,
                                  allow_small_or_imprecise_dtypes=True), writes=[RC])
    S.op("pool", lambda e: e.iota(pcol[:], pattern=[[0, 1]], base=0, channel_multiplier=1,
                                  allow_small_or_imprecise_dtypes=True), writes=[RC])
    S.op("dve", lambda e: e.tensor_single_scalar(ident_bf[:], dif[:], 0.0, op=ALU.is_equal), reads=[RC], writes=[RC])
    S.op("dve", lambda e: e.tensor_single_scalar(ident_f[:], dif[:], 0.0, op=ALU.is_equal), reads=[RC], writes=[RC])
    S.op("dve", lambda e: e.tensor_scalar(maskAB[:, 0, :], dif[:], 0.0, NEG, op0=ALU.is_gt, op1=ALU.mult), reads=[RC], writes=[RC])
    S.op("dve", lambda e: e.tensor_scalar(maskAB[:, 1, :], dif[:], 0.0, NEG, op0=ALU.is_lt, op1=ALU.mult), reads=[RC], writes=[RC])
    S.op("dve", lambda e: e.tensor_scalar(blo[:], pcol[:], 64.0, NEG, op0=ALU.is_lt, op1=ALU.mult), reads=[RC], writes=[RC])
    S.op("dve", lambda e: e.tensor_scalar(bhi[:], pcol[:], 64.0, NEG, op0=ALU.is_ge, op1=ALU.mult), reads=[RC], writes=[RC])
    S.op("dve", lambda e: e.memset(bz[:], 0.0), writes=[RC])
    S.op("dve", lambda e: e.memset(epsc[:], EPS), writes=[RC])
    S.op("dve", lambda e: e.memset(ones65[:], 1.0), writes=[RC])
    S.op("dve", lambda e: e.memset(onesM[:], 1.0 / 512.0), writes=[RC])

    def barrier():
        S.barrier(dummy_sb[:], dummy_in)
        S.recycle()

    barrier()

    def mm(out, lhsT, rhs, start, stop, reads, writes):
        S.op("pe", lambda e: e.matmul(out, lhsT=lhsT, rhs=rhs, start=start, stop=stop), reads=reads, writes=writes)

    with ExitStack() as st:
        zt = sb(st, "zt", [128, 6144], BF16)
        RZ = Res("zt")
        S.op("pool", lambda e: e.memset(zt[:], 0.0), writes=[RZ])
        dz = S.new_dsem("zero")
        for (ts_, L, pb) in seqinfo:
            for off in (pb, pb + PAD + L):
                S.dma(kS[:, off:off + PAD].rearrange("(a p) t -> p a t", p=128),
                      zt[:].rearrange("p (a t) -> p a t", a=6), reads=[RZ], dsem=dz, first=False)
                S.dma(vS[off:off + PAD, :].rearrange("(p a) c -> p (a c)", p=128), zt[:], reads=[RZ], dsem=dz, first=False)
                S.dma(uS[:, off:off + PAD].rearrange("(a p) t -> p a t", p=128),
                      zt[:, 0:4096].rearrange("p (a t) -> p a t", a=4), reads=[RZ], dsem=dz, first=False)
    barrier()

    tiles = []
    for (ts_, L, pb) in seqinfo:
        for p0 in range(0, L, 512):
            tiles.append((ts_ + p0, p0, pb + PAD + p0))

    def load_w_bf16(st, name, src, kchunks, ncols, eng_cycle=("act", "dve"), rows=128):
        wt = sb(st, name, [rows, kchunks, ncols], BF16)
        R = Res(name)
        with ExitStack() as st2:
            stg = Rot(S, st2, nc, name + "_stg", [rows, ncols], F32, 2)
            for kc in range(kchunks):
                t, r, d = stg.get()
                S.dma(t[:], src[kc * rows:(kc + 1) * rows, :], writes=[r], dsem=d)
                eng = eng_cycle[kc % len(eng_cycle)]
                if eng == "act":
                    S.op("act", lambda e: e.copy(wt[:, kc, :], t[:]), reads=[r], writes=[R])
                else:
                    S.op(eng, lambda e: e.tensor_copy(wt[:, kc, :], t[:]), reads=[r], writes=[R])
            barrier()
        return wt, R

    def load_small(st, name, src, shape):
        t = sb(st, name, shape, F32)
        R = Res(name)
        S.dma(t[:], src, writes=[R], dsem=S.new_dsem(name))
        return t, R

    def rms_block(xt_ap, Rx, gb, Rg, out_bf, Rout, junk, Rjunk, ssq, Rssq, col):
        S.op("act", lambda e: e.activation(out=junk[:], in_=xt_ap, func=AF.Square, accum_out=ssq[:, col:col + 1]),
             reads=[Rx], writes=[Rjunk, Rssq])
        S.op("act", lambda e: e.activation(out=ssq[:, col:col + 1], in_=ssq[:, col:col + 1], func=AF.Sqrt, bias=epsc[:], scale=1.0 / D),
             reads=[Rssq, RC], writes=[Rssq])
        S.op("dve", lambda e: e.reciprocal(ssq[:, col:col + 1], ssq[:, col:col + 1]), reads=[Rssq], writes=[Rssq])
        S.op("dve", lambda e: e.scalar_tensor_tensor(out_bf, xt_ap, ssq[:, col:col + 1], gb[:], op0=ALU.mult, op1=ALU.mult),
             reads=[Rx, Rssq, Rg], writes=[Rout])

    with ExitStack() as st:
        Wb, RW = load_w_bf16(st, "Wb", w_in, 8, NCOL)
        Wr = sb(st, "Wr", [128, 8, 1536], BF16)
        RWr = Res("Wr")
        S.op("pool", lambda e: e.memset(Wr[:], 0.0), writes=[RWr])
        for kc in range(8):
            wv = Wb[:, kc, 0:1536].rearrange("p (h e) -> p h e", e=64)
            rv = Wr[:, kc, :].rearrange("p (h e) -> p h e", e=64)
            S.op("pool", lambda e: e.tensor_single_scalar(rv[:, :, 0:8], wv[:, :, 8:16], -1.0, op=ALU.mult), reads=[RW], writes=[RWr])
            S.op("pool", lambda e: e.tensor_copy(rv[:, :, 8:16], wv[:, :, 0:8]), reads=[RW], writes=[RWr])
        g1b, Rg1 = load_small(st, "g1b", g1b_in, [128, D])
        bgate, Rbg = load_small(st, "bgate", bgate_in, [128, 16])
        xr = Rot(S, st, nc, "xt", [128, D], F32, 3)
        ropc = Rot(S, st, nc, "ropc", [128, 512], F32, 2)
        rops = Rot(S, st, nc, "rops", [128, 512], F32, 2)
        junk = sb(st, "junk", [128, D], BF16)
        Rjunk = Res("junk")
        ssq = sb(st, "ssq", [128, 8], F32)
        Rssq = [Res("ssq%d" % i) for i in range(8)]
        hbr = Rot(S, st, nc, "hb", [128, D], BF16, 2)
        hTs = [sb(st, "hT%d" % i, [128, 8, 512], BF16) for i in range(2)]
        RhT = [Res("hT%d" % i) for i in range(2)]
        t1 = Rot(S, st, nc, "t1", [128, 512], F32, 2)
        t2 = Rot(S, st, nc, "t2", [128, 512], F32, 2)
        qko = Rot(S, st, nc, "qko", [128, 512], BF16, 3)
        sig = Rot(S, st, nc, "sig", [128, 512], F32, 2)
        uo = Rot(S, st, nc, "uo", [128, 512], BF16, 2)
        go = Rot(S, st, nc, "go", [128, 512], BF16, 3)
        vo = Rot(S, st, nc, "vo", [128, 768], BF16, 2)

        xq = {}
        blocks = [(ti, a) for ti in range(len(tiles)) for a in range(4)]
        nxt = [0]

        def p1_prefetch(upto):
            while nxt[0] <= upto and nxt[0] < len(blocks):
                ti, a = blocks[nxt[0]]
                t0 = tiles[ti][0]
                it = xr.get()
                S.dma(it[0][:], x_in[t0 + a * 128:t0 + (a + 1) * 128, :], writes=[it[1]], dsem=it[2])
                xq[nxt[0]] = it
                nxt[0] += 1

        pbank = [0]

        def nbank():
            b = pbank[0] % 4
            pbank[0] += 1
            return b

        for ti, (t0, p0, pt) in enumerate(tiles):
            hT = hTs[ti % 2]
            RH = RhT[ti % 2]
            rc = ropc.get()
            rs = rops.get()
            S.dma(rc[0][:], ropec_in[:, p0:p0 + 512], writes=[rc[1]], dsem=rc[2])
            S.dma(rs[0][:], ropes_in[:, p0:p0 + 512], writes=[rs[1]], dsem=rs[2])
            for a in range(4):
                bi = ti * 4 + a
                p1_prefetch(bi + 2)
                xt_, Rx, _ = xq.pop(bi)
                hb, Rhb, _ = hbr.get()
                col = bi % 8
                rms_block(xt_[:], Rx, g1b, Rg1, hb[:], Rhb, junk, Rjunk, ssq, Rssq[col], col)
                bank = 6 + (bi % 2)
                ptb = PS[:, bank, :].bitcast(BF16)
                for kc in range(8):
                    S.op("pe", lambda e: e.transpose(ptb[:, kc * 128:(kc + 1) * 128], hb[:, kc * 128:(kc + 1) * 128], ident_bf[:]),
                         reads=[Rhb], writes=[RB[bank]])
                S.op("act", lambda e: e.copy(hT[:, :, a * 128:(a + 1) * 128], ptb.rearrange("p (k t) -> p k t", k=8)),
                     reads=[RB[bank]], writes=[RH])
            for cc in range(12):
                bm = nbank()
                br = nbank()
                for kc in range(8):
                    mm(PS[:, bm, :], Wb[:, kc, cc * 128:(cc + 1) * 128], hT[:, kc, :], kc == 0, kc == 7, [RW, RH], [RB[bm]])
                for kc in range(8):
                    mm(PS[:, br, :], Wr[:, kc, cc * 128:(cc + 1) * 128], hT[:, kc, :], kc == 0, kc == 7, [RWr, RH], [RB[br]])
                a1 = t1.get()
                a2 = t2.get()
                o = qko.get()
                S.op("dve", lambda e: e.tensor_tensor(a1[0][:], PS[:, bm, :], rc[0][:], op=ALU.mult), reads=[RB[bm], rc[1]], writes=[a1[1]])
                S.op("dve", lambda e: e.tensor_tensor(a2[0][:], PS[:, br, :], rs[0][:], op=ALU.mult), reads=[RB[br], rs[1]], writes=[a2[1]])
                S.op("pool", lambda e: e.tensor_tensor(o[0][:], a1[0][:], a2[0][:], op=ALU.add), reads=[a1[1], a2[1]], writes=[o[1]])
                if cc < 6:
                    S.dma(qS[cc * 128:(cc + 1) * 128, t0:t0 + 512], o[0][:], reads=[o[1]], dsem=o[2])
                else:
                    S.dma(kS[(cc - 6) * 128:(cc - 5) * 128, pt:pt + 512], o[0][:], reads=[o[1]], dsem=o[2])
            for c in range(4):
                ba = nbank()
                bb = nbank()
                for kc in range(8):
                    mm(PS[:, ba, :], Wb[:, kc, 2304 + c * 128:2304 + (c + 1) * 128], hT[:, kc, :], kc == 0, kc == 7, [RW, RH], [RB[ba]])
                for kc in range(8):
                    mm(PS[:, bb, :], Wb[:, kc, 2816 + c * 128:2816 + (c + 1) * 128], hT[:, kc, :], kc == 0, kc == 7, [RW, RH], [RB[bb]])
                sg = sig.get()
                o = uo.get()
                S.op("act", lambda e: e.activation(out=sg[0][:], in_=PS[:, bb, :], func=AF.Sigmoid), reads=[RB[bb]], writes=[sg[1]])
                S.op("dve", lambda e: e.tensor_tensor(o[0][:], PS[:, ba, :], sg[0][:], op=ALU.mult), reads=[RB[ba], sg[1]], writes=[o[1]])
                S.dma(uS[c * 128:(c + 1) * 128, pt:pt + 512], o[0][:], reads=[o[1]], dsem=o[2])
            for gc in range(16):
                bg = nbank()
                for kc in range(8):
                    mm(PS[:, bg, :], Wb[:, kc, 3328 + gc * 128:3328 + (gc + 1) * 128], hT[:, kc, :], kc == 0, kc == 7, [RW, RH], [RB[bg]])
                o = go.get()
                S.op("act", lambda e: e.activation(out=o[0][:], in_=PS[:, bg, :], func=AF.Sigmoid, bias=bgate[:, gc:gc + 1]),
                     reads=[RB[bg], Rbg], writes=[o[1]])
                S.dma(gS[gc * 128:(gc + 1) * 128, t0:t0 + 512], o[0][:], reads=[o[1]], dsem=o[2])
            for a in range(4):
                for kc in range(8):
                    mm(PS[:, 4, :], hT[:, kc, a * 128:(a + 1) * 128], Wb[:, kc, 1536:2048], kc == 0, kc == 7, [RW, RH], [RB[4]])
                for kc in range(8):
                    mm(PS[:, 5, 0:256], hT[:, kc, a * 128:(a + 1) * 128], Wb[:, kc, 2048:2304], kc == 0, kc == 7, [RW, RH], [RB[5]])
                o = vo.get()
                S.op("act", lambda e: e.copy(o[0][:, 0:512], PS[:, 4, :]), reads=[RB[4]], writes=[o[1]])
                S.op("dve", lambda e: e.tensor_copy(o[0][:, 512:768], PS[:, 5, 0:256]), reads=[RB[5]], writes=[o[1]])
                S.dma(vS[pt + a * 128:pt + (a + 1) * 128, :], o[0][:], reads=[o[1]], dsem=o[2])
    barrier()

    with ExitStack() as st:
        acc = sb(st, "acc", [65, 4, 2048], F32)
        Racc = Res("acc")
        Vaug = [sb(st, "Vaug%d" % i, [128, 32, 4, 65], BF16) for i in range(2)]
        RV = [Res("Vaug%d" % i) for i in range(2)]
        dV = [S.new_dsem("Vaug") for _ in range(2)]
        for i in range(2):
            S.op("pool", lambda e: e.memset(Vaug[i][:], 1.0), writes=[RV[i]])
        Qr = Rot(S, st, nc, "Qt", [64, 2048], BF16, 2)
        Kr = Rot(S, st, nc, "Kt", [64, 4096], BF16, 2)
        Pr = Rot(S, st, nc, "Pt", [128, 2, 128], BF16, 4)
        rec = Rot(S, st, nc, "rec", [64, 512], F32, 2)
        ao = sb(st, "ao", [64, 4, 2048], BF16)
        Rao = Res("ao")
        dao = S.new_dsem("ao")
        sslot = [0]
        oslot = [0]
        vcount = 0
        for (ts_, L, pb) in seqinfo:
            for s0 in range(0, L, 2048):
                t0g = ts_ + s0
                po = pb + PAD + s0
                first = s0 == 0
                last = s0 + 2048 == L
                for g in range(3):
                    d = DIL[g]
                    halo = 64 * d
                    nq = 16 // d
                    gb = vcount % 2
                    vcount += 1
                    fst = True
                    for r in range(d):
                        for j in range(nq + 1):
                            tok = po + (128 * j - 64) * d + r
                            S.dma(Vaug[gb][:, r * (nq + 1) + j, :, 0:64],
                                  vS[tok:tok + 127 * d + 1:d, g * 256:(g + 1) * 256].rearrange("p (h e) -> p h e", h=4),
                                  writes=[RV[gb]], dsem=dV[gb], first=fst)
                            fst = False
                    for hs in range(4):
                        hr = (g * 4 + hs) * 64
                        Qt, RQ, dQ = Qr.get()
                        Kt, RK, dK = Kr.get()
                        S.dma(Qt[:], qS[hr:hr + 64, t0g:t0g + 2048], writes=[RQ], dsem=dQ)
                        S.dma(Kt[:, 0:2048 + 2 * halo], kS[hr:hr + 64, po - halo:po + 2048 + halo], writes=[RK], dsem=dK)
                        for r in range(d):
                            for qb in range(nq):
                                c0 = 128 * qb * d + r
                                qv = Qt[:, c0:c0 + 127 * d + 1:d]
                                kA = Kt[:, c0:c0 + 127 * d + 1:d]
                                kB = Kt[:, c0 + 128 * d:c0 + 255 * d + 1:d]
                                sbk = sslot[0] % 4
                                sslot[0] += 1
                                pst = PS[:, sbk, 0:256].rearrange("p (a q) -> p a q", a=2)
                                mm(pst[:, 0, :], kA, qv, True, False, [RK, RQ], [RB[sbk]])
                                mm(pst[:, 0, :], ident_bf[:], maskAB[:, 0, :], False, True, [RC], [RB[sbk]])
                                mm(pst[:, 1, :], kB, qv, True, False, [RK, RQ], [RB[sbk]])
                                mm(pst[:, 1, :], ident_bf[:], maskAB[:, 1, :], False, True, [RC], [RB[sbk]])
                                bA = blo if (first and qb == 0) else bz
                                bB = bhi if (last and qb == nq - 1) else bz
                                Pt, RP, _ = Pr.get()
                                if bA is bB:
                                    S.op("act", lambda e: e.activation(out=Pt[:], in_=pst, func=AF.Exp, bias=bz[:], scale=0.125),
                                         reads=[RB[sbk], RC], writes=[RP])
                                else:
                                    S.op("act", lambda e: e.activation(out=Pt[:, 0, :], in_=pst[:, 0, :], func=AF.Exp, bias=bA[:], scale=0.125),
                                         reads=[RB[sbk], RC], writes=[RP])
                                    S.op("act", lambda e: e.activation(out=Pt[:, 1, :], in_=pst[:, 1, :], func=AF.Exp, bias=bB[:], scale=0.125),
                                         reads=[RB[sbk], RC], writes=[RP])
                                obk = 4 + (oslot[0] % 4)
                                oslot[0] += 1
                                pov = PS[0:65, obk, 0:128]
                                blk = r * (nq + 1) + qb
                                mm(pov, Vaug[gb][:, blk, hs, :], Pt[:, 0, :], True, False, [RV[gb], RP], [RB[obk]])
                                mm(pov, Vaug[gb][:, blk + 1, hs, :], Pt[:, 1, :], False, True, [RV[gb], RP], [RB[obk]])
                                av = acc[:, hs, c0:c0 + 127 * d + 1:d]
                                if g == 0:
                                    S.op("dve", lambda e: e.tensor_copy(av, pov), reads=[RB[obk]], writes=[Racc])
                                else:
                                    S.op("dve", lambda e: e.tensor_tensor(av, av, pov, op=ALU.add), reads=[RB[obk], Racc], writes=[Racc])
                for hs in range(4):
                    for ch in range(4):
                        bk = nbank()
                        mm(PS[0:64, bk, :], ones65[64:65, :], acc[64:65, hs, ch * 512:(ch + 1) * 512], True, True, [Racc, RC], [RB[bk]])
                        rc_ = rec.get()
                        S.op("dve", lambda e: e.reciprocal(rc_[0][:], PS[0:64, bk, :]), reads=[RB[bk]], writes=[rc_[1]])
                        S.op("pool", lambda e: e.tensor_tensor(ao[:, hs, ch * 512:(ch + 1) * 512], acc[0:64, hs, ch * 512:(ch + 1) * 512],
                                                              rc_[0][:], op=ALU.mult), reads=[Racc, rc_[1]], writes=[Rao])
                S.dma(aS[:, t0g:t0g + 2048].rearrange("(h e) t -> e h t", h=4), ao[:], reads=[Rao], dsem=dao)
    barrier()

    with ExitStack() as st:
        pww, Rpww = load_w_bf16(st, "pww", pww_in, 4, D)
        wup, Rwup = load_w_bf16(st, "wup", wup_in, 4, D, rows=64)
        wout, Rwout = load_w_bf16(st, "wout", wout_in, 8, D)
        dww, Rdww = load_small(st, "dww", dww_in, [128, 4, 31])
        cpar, Rcp = load_small(st, "cpar", cpar_in, [128, 3, 4])
        pwb, Rpwb = load_small(st, "pwb", pwb_in, [128, 8])
        Dg = sb(st, "Dg", [128, 124, 128], BF16)
        RDg = Res("Dg")
        for c in range(4):
            for k in range(31):
                eng = "dve" if (k % 2 == 0) else "pool"
                S.op(eng, lambda e: e.tensor_scalar(Dg[:, c * 31 + k, :], ident_f[:], dww[:, c, k:k + 1], None, op0=ALU.mult),
                     reads=[Rdww, RC], writes=[RDg])
        Ur = Rot(S, st, nc, "U", [128, 4, 542], BF16, 2)
        Ar = Rot(S, st, nc, "At", [64, 4, 512], BF16, 2)
        Gr = Rot(S, st, nc, "G", [128, 2, 512], BF16, 3)
        xr = Rot(S, st, nc, "x3", [128, D], F32, 2)
        uc = sb(st, "uc", [128, 4, 512], BF16)
        sq = sb(st, "sq", [128, 4, 512], BF16)
        Ruc = Res("uc")
        Rsq = Res("sq")
        msq = sb(st, "msq", [128, 512], F32)
        rstd = sb(st, "rstd", [128, 512], F32)
        nmr = sb(st, "nmr", [128, 512], F32)
        Rst = Res("stats")
        tt = Rot(S, st, nc, "tt", [128, 512], F32, 2)
        sl = sb(st, "sl", [128, 4, 512], BF16)
        Rsl = Res("sl")
        m1 = Rot(S, st, nc, "m1", [128, 512], F32, 2)
        m2 = Rot(S, st, nc, "m2", [128, 512], F32, 2)
        mx = sb(st, "mx", [128, 8, 512], BF16)
        Rmx = Res("mx")
        x2r = Rot(S, st, nc, "x2o", [128, D], F32, 2)

        for ti, (t0, p0, pt) in enumerate(tiles):
            U, RU, dU = Ur.get()
            S.dma(U[:], uS[:, pt - 15:pt + 527].rearrange("(c p) t -> p c t", p=128), writes=[RU], dsem=dU)
            At, RA, dA = Ar.get()
            S.dma(At[:], aS[:, t0:t0 + 512].rearrange("(h e) t -> e h t", h=4), writes=[RA], dsem=dA)
            for c in range(4):
                bk = nbank()
                for k in range(31):
                    mm(PS[:, bk, :], Dg[:, c * 31 + k, :], U[:, c, k:k + 512], k == 0, k == 30, [RDg, RU], [RB[bk]])
                S.op("act", lambda e: e.activation(out=uc[:, c, :], in_=PS[:, bk, :], func=AF.Identity, bias=cpar[:, 0, c:c + 1]),
                     reads=[RB[bk], Rcp], writes=[Ruc])
                S.op("act", lambda e: e.activation(out=sq[:, c, :], in_=PS[:, bk, :], func=AF.Square, bias=cpar[:, 0, c:c + 1]),
                     reads=[RB[bk], Rcp], writes=[Rsq])
            bme = nbank()
            bex = nbank()
            for c in range(4):
                mm(PS[:, bme, :], onesM[:], uc[:, c, :], c == 0, c == 3, [RC, Ruc], [RB[bme]])
            for c in range(4):
                mm(PS[:, bex, :], onesM[:], sq[:, c, :], c == 0, c == 3, [RC, Rsq], [RB[bex]])
            S.op("act", lambda e: e.activation(out=msq[:], in_=PS[:, bme, :], func=AF.Square), reads=[RB[bme]], writes=[Rst])
            S.op("dve", lambda e: e.tensor_tensor(rstd[:], PS[:, bex, :], msq[:], op=ALU.subtract), reads=[RB[bex], Rst], writes=[Rst])
            S.op("act", lambda e: e.activation(out=rstd[:], in_=rstd[:], func=AF.Sqrt, bias=epsc[:]), reads=[Rst, RC], writes=[Rst])
            S.op("dve", lambda e: e.reciprocal(rstd[:], rstd[:]), reads=[Rst], writes=[Rst])
            S.op("dve", lambda e: e.tensor_tensor(nmr[:], PS[:, bme, :], rstd[:], op=ALU.mult), reads=[RB[bme], Rst], writes=[Rst])
            for c in range(4):
                t_, Rt_, _ = tt.get()
                S.op("dve", lambda e: e.tensor_tensor(t_[:], uc[:, c, :], rstd[:], op=ALU.mult), reads=[Ruc, Rst], writes=[Rt_])
                S.op("pool", lambda e: e.tensor_tensor(t_[:], t_[:], nmr[:], op=ALU.subtract), reads=[Rt_, Rst], writes=[Rt_])
                S.op("act", lambda e: e.activation(out=sl[:, c, :], in_=t_[:], func=AF.Silu, bias=cpar[:, 2, c:c + 1], scale=cpar[:, 1, c:c + 1]),
                     reads=[Rt_, Rcp], writes=[Rsl])
            for dc in range(8):
                G, RG, dG = Gr.get()
                S.dma(G[:], gS[:, t0:t0 + 512].rearrange("(b c p) t -> p b c t", b=2, p=128)[:, :, dc, :], writes=[RG], dsem=dG)
                bcv = nbank()
                bat = nbank()
                for c in range(4):
                    mm(PS[:, bcv, :], pww[:, c, dc * 128:(dc + 1) * 128], sl[:, c, :], c == 0, c == 3, [Rpww, Rsl], [RB[bcv]])
                for hs in range(4):
                    mm(PS[:, bat, :], wup[:, hs, dc * 128:(dc + 1) * 128], At[:, hs, :], hs == 0, hs == 3, [Rwup, RA], [RB[bat]])
                a1 = m1.get()
                a2 = m2.get()
                S.op("dve", lambda e: e.scalar_tensor_tensor(a1[0][:], PS[:, bcv, :], pwb[:, dc:dc + 1], G[:, 1, :], op0=ALU.add, op1=ALU.mult),
                     reads=[RB[bcv], Rpwb, RG], writes=[a1[1]])
                S.op("dve", lambda e: e.tensor_tensor(a2[0][:], PS[:, bat, :], G[:, 0, :], op=ALU.mult), reads=[RB[bat], RG], writes=[a2[1]])
                S.op("pool", lambda e: e.tensor_tensor(mx[:, dc, :], a1[0][:], a2[0][:], op=ALU.add), reads=[a1[1], a2[1]], writes=[Rmx])
            for a in range(4):
                xt_, Rx, dX = xr.get()
                S.dma(xt_[:], x_in[t0 + a * 128:t0 + (a + 1) * 128, :], writes=[Rx], dsem=dX)
                b0 = 4 + 2 * (a % 2)
                for n in range(2):
                    for kc in range(8):
                        mm(PS[:, b0 + n, :], mx[:, kc, a * 128:(a + 1) * 128], wout[:, kc, n * 512:(n + 1) * 512], kc == 0, kc == 7,
                           [Rmx, Rwout], [RB[b0 + n]])
                o, Ro, dO = x2r.get()
                S.op("dve", lambda e: e.tensor_tensor(o[:].rearrange("p (n f) -> p n f", n=2), PS[:, b0:b0 + 2, :],
                                                     xt_[:].rearrange("p (n f) -> p n f", n=2), op=ALU.add),
                     reads=[RB[b0], RB[b0 + 1], Rx], writes=[Ro])
                S.dma(x2S[t0 + a * 128:t0 + (a + 1) * 128, :], o[:], reads=[Ro], dsem=dO)
    barrier()

    with ExitStack() as st:
        wq, Rwq = load_w_bf16(st, "wq", wq_in, 8, 2048)
        kst = sb(st, "kst", [128, 16, 128], F32)
        Rks = Res("kst")
        S.dma(kst[:], keysT_in, writes=[Rks], dsem=S.new_dsem("kst"))
        keysT = sb(st, "keysT", [128, 16, 128], BF16)
        RkT = Res("keysT")
        S.op("dve", lambda e: e.tensor_copy(keysT[:], kst[:]), reads=[Rks], writes=[RkT])
        g2b, Rg2 = load_small(st, "g2b", g2b_in, [128, D])
        xr = Rot(S, st, nc, "x2i", [128, D], F32, 3)
        junk = sb(st, "junk2", [128, D], BF16)
        Rjunk = Res("junk2")
        ssq = sb(st, "ssq2", [128, 8], F32)
        Rssq = [Res("ssq2_%d" % i) for i in range(8)]
        hbr = Rot(S, st, nc, "h2b", [128, D], BF16, 2)
        hTr = Rot(S, st, nc, "h2T", [128, 8, 512], BF16, 2)
        qpT = sb(st, "qpT", [128, 16, 512], BF16)
        RqpT = Res("qpT")
        class BS:
            pass

        BSETS = []
        for bi_ in range(2):
            B = BS()
            B.Ssb = sb(st, "Ssb%d" % bi_, [128, 16, 128], F32)
            B.RS_ = Res("Ssb")
            B.wk = sb(st, "wk%d" % bi_, [128, 16, 128], F32)
            B.Rwk = Res("wk")
            B.tops = sb(st, "tops%d" % bi_, [128, 16, 16], F32)
            B.Rtops = Res("tops")
            B.tidx = sb(st, "tidx%d" % bi_, [128, 16, 16], U32)
            B.Rtidx = Res("tidx")
            B.tif = sb(st, "tif%d" % bi_, [128, 16, 16], F32)
            B.Rtif = Res("tif")
            B.best = sb(st, "best%d" % bi_, [128, 8, 16], F32)
            B.Rbest = Res("best")
            B.bidx = sb(st, "bidx%d" % bi_, [128, 8, 16], U32)
            B.Rbidx = Res("bidx")
            B.ab_u = sb(st, "ab_u%d" % bi_, [128, 2, 8, 16], U32)
            B.ab_f = sb(st, "ab_f%d" % bi_, [128, 2, 8, 16], F32)
            B.Rab = Res("ab")
            B.ge = sb(st, "ge%d" % bi_, [128, 8, 16], F32)
            B.gs = sb(st, "gs%d" % bi_, [128, 8], F32)
            B.Rge = Res("ge")
            B.E0 = sb(st, "E0%d" % bi_, [128, 8, 16, 16], F32)
            B.RE0 = Res("E0")
            B.Rt = sb(st, "Rt%d" % bi_, [128, 3, 128], F32)
            B.RRt = Res("Rt")
            BSETS.append(B)
        RTr = Rot(S, st, nc, "RT", [128, 3, 128], F32, 2)

        ust = Rot(S, st, nc, "ust", [128, D], F32, 2)
        vst = Rot(S, st, nc, "vst", [128, D], F32, 2)
        ubf = Rot(S, st, nc, "ubf", [128, D], BF16, 2)
        uts = Rot(S, st, nc, "uts", [128, D], BF16, 2)
        vbf = Rot(S, st, nc, "vbf", [128, D], BF16, 2)
        NI = NEXP // 128
        tp_loaded = {}
        tp_next = [0]
        tp_done = [0]

        def tp_load():
            i = tp_next[0]
            if i >= NI:
                return
            a = ust.get()
            b = vst.get()
            S.dma(a[0][:], pu_in[i * 128:(i + 1) * 128, :], writes=[a[1]], dsem=a[2])
            S.dma(b[0][:], pv_in[i * 128:(i + 1) * 128, :], writes=[b[1]], dsem=b[2])
            tp_loaded[i] = (a, b)
            tp_next[0] += 1

        def tp_step():
            i = tp_done[0]
            if i >= NI:
                return
            if i not in tp_loaded:
                tp_load()
            tp_load()
            a, b = tp_loaded.pop(i)
            ub = ubf.get()
            S.op("act", lambda e: e.copy(ub[0][:], a[0][:]), reads=[a[1]], writes=[ub[1]])
            ptb = PS[:, 7, :].bitcast(BF16)
            for kc in range(8):
                S.op("pe", lambda e: e.transpose(ptb[:, kc * 128:(kc + 1) * 128], ub[0][:, kc * 128:(kc + 1) * 128], ident_bf[:]),
                     reads=[ub[1]], writes=[RB[7]])
            us_ = uts.get()
            S.op("act", lambda e: e.copy(us_[0][:], ptb), reads=[RB[7]], writes=[us_[1]])
            S.dma(UT[i], us_[0][:], reads=[us_[1]], dsem=us_[2])
            vb_ = vbf.get()
            S.op("pool", lambda e: e.tensor_copy(vb_[0][:], b[0][:]), reads=[b[1]], writes=[vb_[1]])
            S.dma(VB[i * 128:(i + 1) * 128, :], vb_[0][:], reads=[vb_[1]], dsem=vb_[2])
            tp_done[0] += 1

        def routing(a, B, t0):
            Ssb, RS_, wk, Rwk = B.Ssb, B.RS_, B.wk, B.Rwk
            tops, Rtops, tidx, Rtidx, tif, Rtif = B.tops, B.Rtops, B.tidx, B.Rtidx, B.tif, B.Rtif
            best, Rbest, bidx, Rbidx = B.best, B.Rbest, B.bidx, B.Rbidx
            ab_u, ab_f, Rab, ge, gs, Rge = B.ab_u, B.ab_f, B.Rab, B.ge, B.gs, B.Rge
            Rt, RRt = B.Rt, B.RRt
            cand = wk[:].rearrange("p (h c) k -> p h (c k)", c=2)
            Rcand = Rwk
            wk2 = Ssb[:].rearrange("p (h c) k -> p h (c k)", c=2)
            Rwk2 = RS_
            for hc in range(16):
                b = hc // 4
                mm(PS[:, b, (hc % 4) * 128:(hc % 4 + 1) * 128], qpT[:, hc, a * 128:(a + 1) * 128], keysT[:, hc, :], True, True,
                   [RqpT, RkT], [RB[b]])
            S.op("act", lambda e: e.copy(Ssb[:].rearrange("p (b q) k -> p b (q k)", b=4), PS[:, 0:4, :]),
                 reads=[RB[0], RB[1], RB[2], RB[3]], writes=[RS_])
            yield
            for hc in range(16):
                S.op("dve", lambda e: e.max(out=tops[:, hc, 0:8], in_=Ssb[:, hc, :]), reads=[RS_], writes=[Rtops])
            yield
            for hc in range(16):
                S.op("dve", lambda e: e.max_index(out=tidx[:, hc, 0:8], in_max=tops[:, hc, 0:8], in_values=Ssb[:, hc, :]),
                     reads=[RS_, Rtops], writes=[Rtidx])
            for hc in range(16):
                S.op("dve", lambda e: e.match_replace(out=wk[:, hc, :], in_to_replace=tops[:, hc, 0:8], in_values=Ssb[:, hc, :], imm_value=-1e30),
                     reads=[RS_, Rtops], writes=[Rwk])
            yield
            for hc in range(16):
                S.op("dve", lambda e: e.max(out=tops[:, hc, 8:16], in_=wk[:, hc, :]), reads=[Rwk], writes=[Rtops])
            yield
            for hc in range(16):
                S.op("dve", lambda e: e.max_index(out=tidx[:, hc, 8:16], in_max=tops[:, hc, 8:16], in_values=wk[:, hc, :]),
                     reads=[Rwk, Rtops], writes=[Rtidx])
            yield
            S.op("dve", lambda e: e.tensor_copy(tif[:], tidx[:]), reads=[Rtidx], writes=[Rtif])
            tops4 = tops[:].rearrange("p (h c) k -> p h c k", c=2)
            tif4 = tif[:].rearrange("p (h c) k -> p h c k", c=2)
            S.op("dve", lambda e: e.tensor_tensor(cand.rearrange("p h (a b) -> p h a b", a=16),
                                                 tops4[:, :, 0, :].unsqueeze(3).broadcast_to([128, 8, 16, 16]),
                                                 tops4[:, :, 1, :].unsqueeze(2).broadcast_to([128, 8, 16, 16]), op=ALU.add),
                 reads=[Rtops], writes=[Rcand])
            yield
            for h in range(8):
                S.op("dve", lambda e: e.max(out=best[:, h, 0:8], in_=cand[:, h, :]), reads=[Rcand], writes=[Rbest])
            yield
            for h in range(8):
                S.op("dve", lambda e: e.max_index(out=bidx[:, h, 0:8], in_max=best[:, h, 0:8], in_values=cand[:, h, :]),
                     reads=[Rcand, Rbest], writes=[Rbidx])
            for h in range(8):
                S.op("dve", lambda e: e.match_replace(out=wk2[:, h, :], in_to_replace=best[:, h, 0:8], in_values=cand[:, h, :], imm_value=-1e30),
                     reads=[Rcand, Rbest], writes=[Rwk2])
            yield
            for h in range(8):
                S.op("dve", lambda e: e.max(out=best[:, h, 8:16], in_=wk2[:, h, :]), reads=[Rwk2], writes=[Rbest])
            yield
            for h in range(8):
                S.op("dve", lambda e: e.max_index(out=bidx[:, h, 8:16], in_max=best[:, h, 8:16], in_values=wk2[:, h, :]),
                     reads=[Rwk2, Rbest], writes=[Rbidx])
            S.op("dve", lambda e: e.tensor_tensor(ge[:], best[:], best[:, :, 0:1].broadcast_to([128, 8, 16]), op=ALU.subtract),
                 reads=[Rbest], writes=[Rge])
            yield
            S.op("act", lambda e: e.activation(out=ge[:], in_=ge[:], func=AF.Exp), reads=[Rge], writes=[Rge])
            S.op("dve", lambda e: e.tensor_single_scalar(ab_u[:, 0, :, :], bidx[:], 4, op=ALU.logical_shift_right), reads=[Rbidx], writes=[Rab])
            S.op("dve", lambda e: e.tensor_single_scalar(ab_u[:, 1, :, :], bidx[:], 15, op=ALU.bitwise_and), reads=[Rbidx], writes=[Rab])
            yield
            S.op("dve", lambda e: e.tensor_copy(ab_f[:], ab_u[:]), reads=[Rab], writes=[Rab])
            S.op("dve", lambda e: e.reduce_sum(gs[:], ge[:], axis=AX.X), reads=[Rge], writes=[Rge])
            yield
            S.op("dve", lambda e: e.reciprocal(gs[:], gs[:]), reads=[Rge], writes=[Rge])
            yield
            S.op("dve", lambda e: e.tensor_tensor(Rt[:, 2, :].rearrange("p (h k) -> p h k", h=8), ge[:],
                                                 gs[:].unsqueeze(2).broadcast_to([128, 8, 16]), op=ALU.mult),
                 reads=[Rge], writes=[RRt])
            io16 = iota_f[:, 0:16].unsqueeze(1).unsqueeze(1).broadcast_to([128, 8, 16, 16])
            for c in range(2):
                EE, REE = B.E0, B.RE0
                S.op("dve", lambda e: e.tensor_tensor(EE[:], io16, ab_f[:, c, :, :].unsqueeze(3).broadcast_to([128, 8, 16, 16]), op=ALU.is_equal),
                     reads=[Rab, RC], writes=[REE])
                yield
                S.op("dve", lambda e: e.tensor_tensor(EE[:], EE[:], tif4[:, :, c, :].unsqueeze(2).broadcast_to([128, 8, 16, 16]), op=ALU.mult),
                     reads=[REE, Rtif], writes=[REE])
                yield
                S.op("dve", lambda e: e.reduce_sum(Rt[:, c, :].rearrange("p (h k) -> p h k", h=8), EE[:], axis=AX.X),
                     reads=[REE], writes=[RRt])
                yield
            for c in range(3):
                S.op("pe", lambda e: e.transpose(PS[:, 6, c * 128:(c + 1) * 128], Rt[:, c, :], ident_f[:]), reads=[RRt, RC], writes=[RB[6]])
            RT_, RRT, dRT = RTr.get()
            S.op("act", lambda e: e.copy(RT_[:], PS[:, 6, 0:384].rearrange("p (c t) -> p c t", c=3)), reads=[RB[6]], writes=[RRT])
            tb = t0 + a * 128
            S.dma(rS[:, :, tb:tb + 128].rearrange("c p t -> p c t"), RT_[:], reads=[RRT], dsem=dRT)

        for ti, (t0, p0, pt) in enumerate(tiles):
            hT, RH, dH = hTr.get()
            for a in range(4):
                bi = ti * 4 + a
                xt_, Rx, dX = xr.get()
                S.dma(xt_[:], x2S[t0 + a * 128:t0 + (a + 1) * 128, :], writes=[Rx], dsem=dX)
                hb, Rhb, _ = hbr.get()
                col = bi % 8
                rms_block(xt_[:], Rx, g2b, Rg2, hb[:], Rhb, junk, Rjunk, ssq, Rssq[col], col)
                bank = 6
                ptb = PS[:, bank, :].bitcast(BF16)
                for kc in range(8):
                    S.op("pe", lambda e: e.transpose(ptb[:, kc * 128:(kc + 1) * 128], hb[:, kc * 128:(kc + 1) * 128], ident_bf[:]),
                         reads=[Rhb], writes=[RB[bank]])
                S.op("act", lambda e: e.copy(hT[:, :, a * 128:(a + 1) * 128], ptb.rearrange("p (k t) -> p k t", k=8)),
                     reads=[RB[bank]], writes=[RH])
            S.dma(h2S[:, t0:t0 + 512].rearrange("(k p) t -> p k t", p=128), hT[:], reads=[RH], dsem=dH)
            for hc in range(16):
                bk = 4 + (hc % 2)
                for kc in range(8):
                    mm(PS[:, bk, :], wq[:, kc, hc * 128:(hc + 1) * 128], hT[:, kc, :], kc == 0, kc == 7, [Rwq, RH], [RB[bk]])
                if hc % 2 == 0:
                    S.op("act", lambda e: e.copy(qpT[:, hc, :], PS[:, bk, :]), reads=[RB[bk]], writes=[RqpT])
                else:
                    S.op("dve", lambda e: e.tensor_copy(qpT[:, hc, :], PS[:, bk, :]), reads=[RB[bk]], writes=[RqpT])
            for a0 in (0, 2):
                for _ in range((NI * 2 + 2 * len(tiles) - 1) // (2 * len(tiles))):
                    tp_step()
                gens = [routing(a0, BSETS[0], t0), routing(a0 + 1, BSETS[1], t0)]
                alive = [True, True]
                while any(alive):
                    for gi in range(2):
                        if alive[gi]:
                            try:
                                next(gens[gi])
                            except StopIteration:
                                alive[gi] = False
        while tp_done[0] < NI:
            tp_step()
    barrier()

    TT = 256
    with ExitStack() as st:
        gfb, Rgf = load_small(st, "gfb", gfb_in, [128, D])
        WTs = [sb(st, "WT%d" % i, [128, TT, 128], BF16) for i in range(2)]
        RWTs = [Res("WT%d" % i) for i in range(2)]
        rtr = Rot(S, st, nc, "rt4", [128, 3, TT], F32, 2)
        h2r = Rot(S, st, nc, "h2T4", [128, 8, TT], BF16, 2)
        x2r = Rot(S, st, nc, "x24", [128, D], F32, 1)
        TB = 16
        Lr = Rot(S, st, nc, "Lb", [128, TB, 128], BF16, 2)
        Rr = Rot(S, st, nc, "Rb", [128, TB, 128], BF16, 2)
        Ubr = Rot(S, st, nc, "Ub", [128, 2, D], BF16, 3)
        Vbr = Rot(S, st, nc, "Vb", [128, 2, D], BF16, 3)
        Asb = Rot(S, st, nc, "Asb", [128, 2, TT], BF16, 2)
        WAr = Rot(S, st, nc, "WA", [128, 2, TT], BF16, 2)
        yo = Rot(S, st, nc, "yo", [128, D], F32, 2)
        junk = sb(st, "junk4", [128, D], BF16)
        Rjunk = Res("junk4")
        ssq = sb(st, "ssq4", [128, 8], F32)
        Rssq = [Res("ssq4_%d" % i) for i in range(8)]
        ntile = NT // TT
        NP = 64
        NTB = TT // TB
        PER = NP // NTB
        pend = {}

        def p4_load(ti):
            tb = ti * TT
            r_ = rtr.get()
            S.dma(r_[0][:], rS[:, :, tb:tb + TT].rearrange("c p t -> p c t"), writes=[r_[1]], dsem=r_[2])
            h_ = h2r.get()
            S.dma(h_[0][:], h2S[:, tb:tb + TT].rearrange("(k p) t -> p k t", p=128), writes=[h_[1]], dsem=h_[2])
            pend[ti] = (r_, h_)

        chunkq = {}
        cnext = [0]
        total = ntile * NP

        def p4_chunk_prefetch(upto):
            while cnext[0] <= upto and cnext[0] < total:
                ip = cnext[0] % NP
                u_ = Ubr.get()
                v_ = Vbr.get()
                S.dma(u_[0][:], UT[2 * ip:2 * ip + 2].rearrange("i p f -> p i f"), writes=[u_[1]], dsem=u_[2])
                S.dma(v_[0][:], VB[2 * ip * 128:(2 * ip + 2) * 128, :].rearrange("(i p) f -> p i f", p=128), writes=[v_[1]], dsem=v_[2])
                chunkq[cnext[0]] = (u_, v_)
                cnext[0] += 1

        io = iota_f[:].unsqueeze(1).broadcast_to([128, TB, 128])
        evn = [0]

        def onehot(ti, tb):
            rt_, Rrt, _ = pend[ti][0]
            Lb, RL, _ = Lr.get()
            Rb, RR, _ = Rr.get()
            S.op("dve", lambda e: e.tensor_tensor(Lb[:], io, rt_[:, 0, tb * TB:(tb + 1) * TB].unsqueeze(2).broadcast_to([128, TB, 128]), op=ALU.is_equal),
                 reads=[Rrt, RC], writes=[RL])
            S.op("dve", lambda e: e.tensor_tensor(Lb[:], Lb[:], rt_[:, 2, tb * TB:(tb + 1) * TB].unsqueeze(2).broadcast_to([128, TB, 128]), op=ALU.mult),
                 reads=[Rrt, RL], writes=[RL])
            S.op("dve", lambda e: e.tensor_tensor(Rb[:], io, rt_[:, 1, tb * TB:(tb + 1) * TB].unsqueeze(2).broadcast_to([128, TB, 128]), op=ALU.is_equal),
                 reads=[Rrt, RC], writes=[RR])
            return (Lb, RL, Rb, RR)

        def wmm(ti, tb, LR):
            Lb, RL, Rb, RR = LR
            WT = WTs[ti % 2]
            RWT = RWTs[ti % 2]
            for q4 in range(TB // 4):
                bk = 6 + (evn[0] % 2)
                evn[0] += 1
                pw = PS[:, bk, :].rearrange("p (t i) -> p t i", t=4)
                for tq in range(4):
                    tl = q4 * 4 + tq
                    mm(pw[:, tq, :], Rb[:, tl, :], Lb[:, tl, :], True, True, [RL, RR], [RB[bk]])
                tg = tb * TB + q4 * 4
                S.op("act", lambda e: e.copy(WT[:, tg:tg + 4, :], pw), reads=[RB[bk]], writes=[RWT])

        def u_mm(ci):
            ti, ip = divmod(ci, NP)
            h2T, Rh2, _ = pend[ti][1]
            (Ub, RU, _), _v = chunkq[ci]
            bk = 4 + (ci % 2)
            pa = PS[:, bk, :].rearrange("p (i t) -> p i t", i=2)
            for ii in range(2):
                for kc in range(8):
                    mm(pa[:, ii, :], Ub[:, ii, kc * 128:(kc + 1) * 128], h2T[:, kc, :], kc == 0, kc == 7, [RU, Rh2], [RB[bk]])

        p4_load(0)
        p4_chunk_prefetch(2)
        for tb in range(NTB):
            wmm(0, tb, onehot(0, tb))
        u_mm(0)
        pendLR = None
        for ti in range(ntile):
            if ti + 1 < ntile:
                p4_load(ti + 1)
            WT = WTs[ti % 2]
            RWT = RWTs[ti % 2]
            for ip in range(NP):
                ci = ti * NP + ip
                p4_chunk_prefetch(ci + 2)
                if ti + 1 < ntile and ip % PER == 0:
                    if pendLR is not None:
                        wmm(ti + 1, ip // PER - 1, pendLR)
                    pendLR = onehot(ti + 1, ip // PER)
                if ci + 1 < total:
                    u_mm(ci + 1)
                _u, (Vb, RVb, _) = chunkq.pop(ci)
                bk = 4 + (ci % 2)
                pa = PS[:, bk, :].rearrange("p (i t) -> p i t", i=2)
                A_, RA_, _ = Asb.get()
                S.op("act", lambda e: e.activation(out=A_[:], in_=pa, func=AF.Gelu), reads=[RB[bk]], writes=[RA_])
                WA, RWA, _ = WAr.get()
                S.op("pool", lambda e: e.tensor_tensor(WA[:], A_[:], WT[:, :, 2 * ip:2 * ip + 2].rearrange("p t i -> p i t"), op=ALU.mult),
                     reads=[RA_, RWT], writes=[RWA])
                for ii in range(2):
                    for th in range(2):
                        for n in range(2):
                            bo = th * 2 + n
                            mm(PS[:, bo, :], WA[:, ii, th * 128:(th + 1) * 128], Vb[:, ii, n * 512:(n + 1) * 512],
                               ip == 0 and ii == 0, ip == NP - 1 and ii == 1, [RWA, RVb], [RB[bo]])
            if pendLR is not None:
                wmm(ti + 1, NTB - 1, pendLR)
                pendLR = None
            for th in range(2):
                tb_ = ti * TT + th * 128
                o, Ro, dX = x2r.get()
                S.dma(o[:], x2S[tb_:tb_ + 128, :], writes=[Ro], dsem=dX)
                S.op("dve", lambda e: e.tensor_tensor(o[:].rearrange("p (n f) -> p n f", n=2), PS[:, 2 * th:2 * th + 2, :],
                                                     o[:].rearrange("p (n f) -> p n f", n=2), op=ALU.add),
                     reads=[RB[2 * th], RB[2 * th + 1], Ro], writes=[Ro])
                y_, Ry, dY = yo.get()
                col = (ti * 2 + th) % 8
                S.op("act", lambda e: e.activation(out=junk[:], in_=o[:], func=AF.Square, accum_out=ssq[:, col:col + 1]),
                     reads=[Ro], writes=[Rjunk, Rssq[col]])
                S.op("act", lambda e: e.activation(out=ssq[:, col:col + 1], in_=ssq[:, col:col + 1], func=AF.Sqrt, bias=epsc[:], scale=1.0 / D),
                     reads=[Rssq[col], RC], writes=[Rssq[col]])
                S.op("dve", lambda e: e.reciprocal(ssq[:, col:col + 1], ssq[:, col:col + 1]), reads=[Rssq[col]], writes=[Rssq[col]])
                S.op("dve", lambda e: e.scalar_tensor_tensor(y_[:], o[:], ssq[:, col:col + 1], gfb[:], op0=ALU.mult, op1=ALU.mult),
                     reads=[Ro, Rssq[col], Rgf], writes=[Ry])
                S.dma(y_out[tb_:tb_ + 128, :], y_[:], reads=[Ry], dsem=dY)
            pend.pop(ti)
    barrier()
    top.close()
    return nc


def rope_tables(smax=8192):
    half = 8
    inv = (np.float32(500000.0) ** (-np.arange(half, dtype=np.float32) * np.float32(2.0) / np.float32(16))).astype(np.float32)
    pos = np.arange(smax, dtype=np.float32)
    ang = (pos[:, None] * inv[None, :]).astype(np.float32)
    cos = np.cos(ang).astype(np.float32).T
    sin = np.sin(ang).astype(np.float32).T
    C = np.ones((128, smax), np.float32)
    Sn = np.zeros((128, smax), np.float32)
    for hh in range(2):
        b = hh * 64
        C[b:b + 8] = cos
        C[b + 8:b + 16] = cos
        Sn[b:b + 8] = sin
        Sn[b + 8:b + 16] = sin
    return C, Sn


def make_shared(inp):
    f = lambda a: np.ascontiguousarray(np.asarray(a, dtype=np.float32))
    C, Sn = rope_tables()
    sh = {
        "w_in": f(inp["w_in"][0]),
        "g1b": f(np.broadcast_to(inp["norm1_g"][0][None, :], (128, D))),
        "g2b": f(np.broadcast_to(inp["norm2_g"][0][None, :], (128, D))),
        "gfb": f(np.broadcast_to(np.asarray(inp["final_g"])[None, :], (128, D))),
        "bgate": f(np.asarray(inp["b_gate"][0]).reshape(16, 128).T),
        "w_up": f(inp["w_attn_up"][0]),
        "dww": f(np.asarray(inp["conv_dw_w"][0]).reshape(31, 4, 128).transpose(2, 1, 0)),
        "cpar": f(np.stack([np.asarray(inp["conv_dw_b"][0]).reshape(4, 128).T,
                            np.asarray(inp["conv_ln_g"][0]).reshape(4, 128).T,
                            np.asarray(inp["conv_ln_b"][0]).reshape(4, 128).T], axis=1)),
        "pw_w": f(inp["conv_pw_w"][0]),
        "pwb": f(np.asarray(inp["conv_pw_b"][0]).reshape(8, 128).T),
        "w_out": f(inp["w_out"][0]),
        "wq": f(inp["peer_wq"][0]),
        "keysT": f(np.asarray(inp["peer_keys"][0]).reshape(16, 128, 128).transpose(2, 0, 1)),
        "peer_u": f(inp["peer_u"][0]),
        "peer_v": f(inp["peer_v"][0]),
        "rope_c": C,
        "rope_s": Sn,
        "dummy8": np.zeros((1, 8), np.float32),
    }
    return sh


def kernel(**inputs):
    xp = np.asarray(inputs["x_prompt"], dtype=np.float32)
    xs = np.asarray(inputs["x_sample"], dtype=np.float32)
    seqs = [xp.shape[1], xs.shape[1], xs.shape[1]]
    nc = build_program(seqs)
    sh = make_shared(inputs)
    in_maps = []
    for c in range(N_CORES):
        m = dict(sh)
        m["x"] = np.ascontiguousarray(np.concatenate([xp[c], xs[2 * c], xs[2 * c + 1]], axis=0))
        in_maps.append(m)
    res = run_bass_kernel_spmd(nc, in_maps, core_ids=list(range(N_CORES)))
    yp = np.empty_like(xp)
    ys = np.empty_like(xs)
    Lp, Ls = seqs[0], seqs[1]
    for c in range(N_CORES):
        y = np.asarray(res.results[c]["y"])
        yp[c] = y[0:Lp]
        ys[2 * c] = y[Lp:Lp + Ls]
        ys[2 * c + 1] = y[Lp + Ls:Lp + 2 * Ls]
    return (yp, ys)
```

```python
from contextlib import ExitStack
import numpy as np
import concourse.bass as bass
import concourse.mybir as mybir
from concourse.bass_utils import run_bass_kernel_spmd

F32 = mybir.dt.float32
BF16 = mybir.dt.bfloat16
U32 = mybir.dt.uint32
AF = mybir.ActivationFunctionType
ALU = mybir.AluOpType
AX = mybir.AxisListType

D = 1024
NCOL = 5376
PAD = 1024
NEXP = 16384
EPS = 1e-6
DIL = (1, 4, 16)
NEG = -30000.0
N_CORES = 8


class Res:
    __slots__ = ("name", "w", "rd")

    def __init__(self, name):
        self.name = name
        self.w = None
        self.rd = {}


class DSem:
    def __init__(self, h, key):
        self.h = h
        self.key = key
        self.count = 0


class Sched:
    CE = ("pe", "act", "dve", "pool")

    def __init__(self, nc):
        self.nc = nc
        self.E = {"pe": nc.tensor, "act": nc.scalar, "dve": nc.vector, "pool": nc.gpsimd, "sp": nc.sync}
        self.semh = {}
        self.cnt = {}
        self.waited = {e: {} for e in self.E}
        for e in self.CE:
            self.semh[e] = nc.alloc_semaphore("s_" + e)
            self.cnt[e] = 0
        self.dsems = []
        self.bar = nc.alloc_semaphore("s_bar")
        self.barcnt = 0
        self.n_ds = 0
        self.free_ds = []
        self.live_ds = []

    def new_dsem(self, name=None):
        if self.free_ds:
            d = self.free_ds.pop()
            self.live_ds.append(d)
            return d
        self.n_ds += 1
        name = "d%d_%s" % (self.n_ds, name or "")
        d = DSem(self.nc.alloc_semaphore(name), name)
        self.semh[name] = d.h
        self.dsems.append(d)
        self.live_ds.append(d)
        return d

    def recycle(self):
        self.free_ds.extend(self.live_ds)
        self.live_ds = []

    def _wait(self, eng, key, val):
        if self.waited[eng].get(key, 0) >= val:
            return
        self.E[eng].wait_ge(self.semh[key], val)
        self.waited[eng][key] = val

    def _deps(self, eng, reads, writes, is_dma=False, nosame=False):
        deps = {}

        def add(key, val, kind):
            if key == eng and not is_dma and (eng == "pe" or kind != "raw" or nosame):
                return
            if deps.get(key, 0) < val:
                deps[key] = val

        for r in reads:
            if r.w is not None:
                add(r.w[0], r.w[1], "raw")
        for r in writes:
            if r.w is not None:
                add(r.w[0], r.w[1], "waw")
            for k, v in r.rd.items():
                add(k, v, "war")
        for k, v in deps.items():
            self._wait(eng, k, v)

    def op(self, eng, fn, reads=(), writes=(), nosame=False):
        self._deps(eng, reads, writes, nosame=nosame)
        ins = fn(self.E[eng])
        self.cnt[eng] += 1
        c = self.cnt[eng]
        ins.then_inc(self.semh[eng], 1)
        for r in reads:
            r.rd[eng] = c
        for r in writes:
            r.w = (eng, c)
            r.rd = {}
        return ins

    def dma(self, out, in_, reads=(), writes=(), dsem=None, first=True, q="sp"):
        if isinstance(dsem, LazyD):
            dsem = dsem.get()
        self._deps(q, reads, writes, is_dma=True)
        if first and dsem.count > 0:
            self._wait(q, dsem.key, dsem.count)
        ins = self.E[q].dma_start(out=out, in_=in_)
        dsem.count += 16
        ins.then_inc(dsem.h, 16)
        for r in reads:
            r.rd[dsem.key] = dsem.count
        for r in writes:
            r.w = (dsem.key, dsem.count)
            r.rd = {}
        return ins

    def barrier(self, dummy_sb, dummy_dram):
        for e in self.CE:
            if self.cnt[e] > 0:
                self._wait("sp", e, self.cnt[e])
        for d in self.dsems:
            if d.count > 0:
                self._wait("sp", d.key, d.count)
        ins = self.E["sp"].dma_start(out=dummy_sb, in_=dummy_dram)
        self.barcnt += 16
        ins.then_inc(self.bar, 16)
        for e in self.E:
            self.E[e].wait_ge(self.bar, self.barcnt)
            for k in self.CE:
                self.waited[e][k] = self.cnt[k]
            for d in self.dsems:
                self.waited[e][d.key] = d.count


class LazyD:
    def __init__(self, S, name):
        self.S = S
        self.name = name
        self.d = None

    def get(self):
        if self.d is None:
            self.d = self.S.new_dsem(self.name)
        return self.d


class Rot:
    ctr = 0

    def __init__(self, S, st, nc, name, shape, dtype, n):
        self.items = []
        for i in range(n):
            Rot.ctr += 1
            t = st.enter_context(nc.sbuf_tensor("rot%d_%s%d" % (Rot.ctr, name, i), list(shape), dtype))
            self.items.append((t, Res("%s%d" % (name, i)), LazyD(S, name)))
        self.i = 0

    def get(self):
        it = self.items[self.i % len(self.items)]
        self.i += 1
        return it


def build_program(seqs, dbg=False):
    nc = bass.Bass("TRN2", target_bir_lowering=False)
    NT = sum(seqs)
    NTP = NT + 2 * PAD * len(seqs)
    seqinfo = []
    ts, ps_ = 0, 0
    for L in seqs:
        assert L % 2048 == 0
        seqinfo.append((ts, L, ps_))
        ts += L
        ps_ += L + 2 * PAD

    def din(name, shape, dt=F32):
        return nc.dram_tensor(name, list(shape), dt, kind="ExternalInput").ap()

    def dscr(name, shape, dt):
        if dbg:
            return nc.dram_tensor(name, list(shape), dt, kind="ExternalOutput").ap()
        return nc.dram_tensor(name, list(shape), dt).ap()

    x_in = din("x", [NT, D])
    w_in = din("w_in", [D, NCOL])
    g1b_in = din("g1b", [128, D])
    g2b_in = din("g2b", [128, D])
    gfb_in = din("gfb", [128, D])
    bgate_in = din("bgate", [128, 16])
    wup_in = din("w_up", [256, D])
    dww_in = din("dww", [128, 4, 31])
    cpar_in = din("cpar", [128, 3, 4])
    pww_in = din("pw_w", [512, D])
    pwb_in = din("pwb", [128, 8])
    wout_in = din("w_out", [D, D])
    wq_in = din("wq", [D, 2048])
    keysT_in = din("keysT", [128, 16, 128])
    pu_in = din("peer_u", [NEXP, D])
    pv_in = din("peer_v", [NEXP, D])
    ropec_in = din("rope_c", [128, 8192])
    ropes_in = din("rope_s", [128, 8192])
    dummy_in = din("dummy8", [1, 8])
    y_out = nc.dram_tensor("y", [NT, D], F32, kind="ExternalOutput").ap()

    qS = dscr("qS", [768, NT], BF16)
    kS = dscr("kS", [768, NTP], BF16)
    vS = dscr("vS", [NTP, 768], BF16)
    uS = dscr("uS", [512, NTP], BF16)
    gS = dscr("gS", [2048, NT], BF16)
    aS = dscr("aS", [256, NT], BF16)
    x2S = dscr("x2S", [NT, D], F32)
    h2S = dscr("h2S", [D, NT], BF16)
    rS = dscr("rS", [3, 128, NT], F32)
    UT = nc.dram_tensor("UT", [128, 128, 1024], BF16).ap()
    VB = nc.dram_tensor("VB", [NEXP, D], BF16).ap()

    S = Sched(nc)
    top = ExitStack()

    nctr = [0]

    def sb(st, name, shape, dt):
        nctr[0] += 1
        return st.enter_context(nc.sbuf_tensor("sb%d_%s" % (nctr[0], name), list(shape), dt))

    PS = top.enter_context(nc.psum_tensor("PS", [128, 8, 512], F32))
    RB = [Res("bank%d" % i) for i in range(8)]
    ident_bf = sb(top, "ident_bf", [128, 128], BF16)
    ident_f = sb(top, "ident_f", [128, 128], F32)
    iota_f = sb(top, "iota_f", [128, 128], F32)
    dif = sb(top, "dif", [128, 128], F32)
    maskAB = sb(top, "maskAB", [128, 2, 128], BF16)
    pcol = sb(top, "pcol", [128, 1], F32)
    bz = sb(top, "bz", [128, 1], F32)
    epsc = sb(top, "epsc", [128, 1], F32)
    blo = sb(top, "blo", [128, 1], F32)
    bhi = sb(top, "bhi", [128, 1], F32)
    ones65 = sb(top, "ones65", [65, 64], F32)
    onesM = sb(top, "onesM", [128, 128], BF16)
    dummy_sb = sb(top, "dummy_sb", [1, 8], F32)
    RC = Res("consts")

    S.op("pool", lambda e: e.iota(dif[:], pattern=[[1, 128]], base=0, channel_multiplier=-1,
                                  allow_small_or_imprecise_dtypes=True), writes=[RC])
    S.op("pool", lambda e: e.iota(iota_f[:], pattern=[[1, 128]], base=0, channel_multiplier=0,
                                  allow_small_or_imprecise_dtypes=True), writes=[RC])
    S.op("pool", lambda e: e.iota(pcol[:], pattern=[[0, 1]], base=0, channel_multiplier=1,
                                  allow_small_or_imprecise_dtypes=True), writes=[RC])
    S.op("dve", lambda e: e.tensor_single_scalar(ident_bf[:], dif[:], 0.0, op=ALU.is_equal), reads=[RC], writes=[RC])
    S.op("dve", lambda e: e.tensor_single_scalar(ident_f[:], dif[:], 0.0, op=ALU.is_equal), reads=[RC], writes=[RC])
    S.op("dve", lambda e: e.tensor_scalar(maskAB[:, 0, :], dif[:], 0.0, NEG, op0=ALU.is_gt, op1=ALU.mult), reads=[RC], writes=[RC])
    S.op("dve", lambda e: e.tensor_scalar(maskAB[:, 1, :], dif[:], 0.0, NEG, op0=ALU.is_lt, op1=ALU.mult), reads=[RC], writes=[RC])
    S.op("dve", lambda e: e.tensor_scalar(blo[:], pcol[:], 64.0, NEG, op0=ALU.is_lt, op1=ALU.mult), reads=[RC], writes=[RC])
    S.op("dve", lambda e: e.tensor_scalar(bhi[:], pcol[:], 64.0, NEG, op0=ALU.is_ge, op1=ALU.mult), reads=[RC], writes=[RC])
    S.op("dve", lambda e: e.memset(bz[:], 0.0), writes=[RC])
    S.op("dve", lambda e: e.memset(epsc[:], EPS), writes=[RC])
    S.op("dve", lambda e: e.memset(ones65[:], 1.0), writes=[RC])
    S.op("dve", lambda e: e.memset(onesM[:], 1.0 / 512.0), writes=[RC])

    def barrier():
        S.barrier(dummy_sb[:], dummy_in)
        S.recycle()

    barrier()

    def mm(out, lhsT, rhs, start, stop, reads, writes):
        S.op("pe", lambda e: e.matmul(out, lhsT=lhsT, rhs=rhs, start=start, stop=stop), reads=reads, writes=writes)

    with ExitStack() as st:
        zt = sb(st, "zt", [128, 6144], BF16)
        RZ = Res("zt")
        S.op("pool", lambda e: e.memset(zt[:], 0.0), writes=[RZ])
        dz = S.new_dsem("zero")
        for (ts_, L, pb) in seqinfo:
            for off in (pb, pb + PAD + L):
                S.dma(kS[:, off:off + PAD].rearrange("(a p) t -> p a t", p=128),
                      zt[:].rearrange("p (a t) -> p a t", a=6), reads=[RZ], dsem=dz, first=False)
                S.dma(vS[off:off + PAD, :].rearrange("(p a) c -> p (a c)", p=128), zt[:], reads=[RZ], dsem=dz, first=False)
                S.dma(uS[:, off:off + PAD].rearrange("(a p) t -> p a t", p=128),
                      zt[:, 0:4096].rearrange("p (a t) -> p a t", a=4), reads=[RZ], dsem=dz, first=False)
    barrier()

    tiles = []
    for (ts_, L, pb) in seqinfo:
        for p0 in range(0, L, 512):
            tiles.append((ts_ + p0, p0, pb + PAD + p0))

    def load_w_bf16(st, name, src, kchunks, ncols, eng_cycle=("act", "dve"), rows=128):
        wt = sb(st, name, [rows, kchunks, ncols], BF16)
        R = Res(name)
        with ExitStack() as st2:
            stg = Rot(S, st2, nc, name + "_stg", [rows, ncols], F32, 2)
            for kc in range(kchunks):
                t, r, d = stg.get()
                S.dma(t[:], src[kc * rows:(kc + 1) * rows, :], writes=[r], dsem=d)
                eng = eng_cycle[kc % len(eng_cycle)]
                if eng == "act":
                    S.op("act", lambda e: e.copy(wt[:, kc, :], t[:]), reads=[r], writes=[R])
                else:
                    S.op(eng, lambda e: e.tensor_copy(wt[:, kc, :], t[:]), reads=[r], writes=[R])
            barrier()
        return wt, R

    def load_small(st, name, src, shape):
        t = sb(st, name, shape, F32)
        R = Res(name)
        S.dma(t[:], src, writes=[R], dsem=S.new_dsem(name))
        return t, R

    def rms_block(xt_ap, Rx, gb, Rg, out_bf, Rout, junk, Rjunk, ssq, Rssq, col):
        S.op("act", lambda e: e.activation(out=junk[:], in_=xt_ap, func=AF.Square, accum_out=ssq[:, col:col + 1]),
             reads=[Rx], writes=[Rjunk, Rssq])
        S.op("act", lambda e: e.activation(out=ssq[:, col:col + 1], in_=ssq[:, col:col + 1], func=AF.Sqrt, bias=epsc[:], scale=1.0 / D),
             reads=[Rssq, RC], writes=[Rssq])
        S.op("dve", lambda e: e.reciprocal(ssq[:, col:col + 1], ssq[:, col:col + 1]), reads=[Rssq], writes=[Rssq])
        S.op("dve", lambda e: e.scalar_tensor_tensor(out_bf, xt_ap, ssq[:, col:col + 1], gb[:], op0=ALU.mult, op1=ALU.mult),
             reads=[Rx, Rssq, Rg], writes=[Rout])

    with ExitStack() as st:
        Wb, RW = load_w_bf16(st, "Wb", w_in, 8, NCOL)
        Wr = sb(st, "Wr", [128, 8, 1536], BF16)
        RWr = Res("Wr")
        S.op("pool", lambda e: e.memset(Wr[:], 0.0), writes=[RWr])
        for kc in range(8):
            wv = Wb[:, kc, 0:1536].rearrange("p (h e) -> p h e", e=64)
            rv = Wr[:, kc, :].rearrange("p (h e) -> p h e", e=64)
            S.op("pool", lambda e: e.tensor_single_scalar(rv[:, :, 0:8], wv[:, :, 8:16], -1.0, op=ALU.mult), reads=[RW], writes=[RWr])
            S.op("pool", lambda e: e.tensor_copy(rv[:, :, 8:16], wv[:, :, 0:8]), reads=[RW], writes=[RWr])
        g1b, Rg1 = load_small(st, "g1b", g1b_in, [128, D])
        bgate, Rbg = load_small(st, "bgate", bgate_in, [128, 16])
        xr = Rot(S, st, nc, "xt", [128, D], F32, 3)
        ropc = Rot(S, st, nc, "ropc", [128, 512], F32, 2)
        rops = Rot(S, st, nc, "rops", [128, 512], F32, 2)
        junk = sb(st, "junk", [128, D], BF16)
        Rjunk = Res("junk")
        ssq = sb(st, "ssq", [128, 8], F32)
        Rssq = [Res("ssq%d" % i) for i in range(8)]
        hbr = Rot(S, st, nc, "hb", [128, D], BF16, 2)
        hTs = [sb(st, "hT%d" % i, [128, 8, 512], BF16) for i in range(2)]
        RhT = [Res("hT%d" % i) for i in range(2)]
        t1 = Rot(S, st, nc, "t1", [128, 512], F32, 2)
        t2 = Rot(S, st, nc, "t2", [128, 512], F32, 2)
        qko = Rot(S, st, nc, "qko", [128, 512], BF16, 3)
        sig = Rot(S, st, nc, "sig", [128, 512], F32, 2)
        uo = Rot(S, st, nc, "uo", [128, 512], BF16, 2)
        go = Rot(S, st, nc, "go", [128, 512], BF16, 3)
        vo = Rot(S, st, nc, "vo", [128, 768], BF16, 2)

        xq = {}
        blocks = [(ti, a) for ti in range(len(tiles)) for a in range(4)]
        nxt = [0]

        def p1_prefetch(upto):
            while nxt[0] <= upto and nxt[0] < len(blocks):
                ti, a = blocks[nxt[0]]
                t0 = tiles[ti][0]
                it = xr.get()
                S.dma(it[0][:], x_in[t0 + a * 128:t0 + (a + 1) * 128, :], writes=[it[1]], dsem=it[2])
                xq[nxt[0]] = it
                nxt[0] += 1

        pbank = [0]

        def nbank():
            b = pbank[0] % 4
            pbank[0] += 1
            return b

        for ti, (t0, p0, pt) in enumerate(tiles):
            hT = hTs[ti % 2]
            RH = RhT[ti % 2]
            rc = ropc.get()
            rs = rops.get()
            S.dma(rc[0][:], ropec_in[:, p0:p0 + 512], writes=[rc[1]], dsem=rc[2])
            S.dma(rs[0][:], ropes_in[:, p0:p0 + 512], writes=[rs[1]], dsem=rs[2])
            for a in range(4):
                bi = ti * 4 + a
                p1_prefetch(bi + 2)
                xt_, Rx, _ = xq.pop(bi)
                hb, Rhb, _ = hbr.get()
                col = bi % 8
                rms_block(xt_[:], Rx, g1b, Rg1, hb[:], Rhb, junk, Rjunk, ssq, Rssq[col], col)
                bank = 6 + (bi % 2)
                ptb = PS[:, bank, :].bitcast(BF16)
                for kc in range(8):
                    S.op("pe", lambda e: e.transpose(ptb[:, kc * 128:(kc + 1) * 128], hb[:, kc * 128:(kc + 1) * 128], ident_bf[:]),
                         reads=[Rhb], writes=[RB[bank]])
                S.op("act", lambda e: e.copy(hT[:, :, a * 128:(a + 1) * 128], ptb.rearrange("p (k t) -> p k t", k=8)),
                     reads=[RB[bank]], writes=[RH])
            for cc in range(12):
                bm = nbank()
                br = nbank()
                for kc in range(8):
                    mm(PS[:, bm, :], Wb[:, kc, cc * 128:(cc + 1) * 128], hT[:, kc, :], kc == 0, kc == 7, [RW, RH], [RB[bm]])
                for kc in range(8):
                    mm(PS[:, br, :], Wr[:, kc, cc * 128:(cc + 1) * 128], hT[:, kc, :], kc == 0, kc == 7, [RWr, RH], [RB[br]])
                a1 = t1.get()
                a2 = t2.get()
                o = qko.get()
                S.op("dve", lambda e: e.tensor_tensor(a1[0][:], PS[:, bm, :], rc[0][:], op=ALU.mult), reads=[RB[bm], rc[1]], writes=[a1[1]])
                S.op("dve", lambda e: e.tensor_tensor(a2[0][:], PS[:, br, :], rs[0][:], op=ALU.mult), reads=[RB[br], rs[1]], writes=[a2[1]])
                S.op("pool", lambda e: e.tensor_tensor(o[0][:], a1[0][:], a2[0][:], op=ALU.add), reads=[a1[1], a2[1]], writes=[o[1]])
                if cc < 6:
                    S.dma(qS[cc * 128:(cc + 1) * 128, t0:t0 + 512], o[0][:], reads=[o[1]], dsem=o[2])
                else:
                    S.dma(kS[(cc - 6) * 128:(cc - 5) * 128, pt:pt + 512], o[0][:], reads=[o[1]], dsem=o[2])
            for c in range(4):
                ba = nbank()
                bb = nbank()
                for kc in range(8):
                    mm(PS[:, ba, :], Wb[:, kc, 2304 + c * 128:2304 + (c + 1) * 128], hT[:, kc, :], kc == 0, kc == 7, [RW, RH], [RB[ba]])
                for kc in range(8):
                    mm(PS[:, bb, :], Wb[:, kc, 2816 + c * 128:2816 + (c + 1) * 128], hT[:, kc, :], kc == 0, kc == 7, [RW, RH], [RB[bb]])
                sg = sig.get()
                o = uo.get()
                S.op("act", lambda e: e.activation(out=sg[0][:], in_=PS[:, bb, :], func=AF.Sigmoid), reads=[RB[bb]], writes=[sg[1]])
                S.op("dve", lambda e: e.tensor_tensor(o[0][:], PS[:, ba, :], sg[0][:], op=ALU.mult), reads=[RB[ba], sg[1]], writes=[o[1]])
                S.dma(uS[c * 128:(c + 1) * 128, pt:pt + 512], o[0][:], reads=[o[1]], dsem=o[2])
            for gc in range(16):
                bg = nbank()
                for kc in range(8):
                    mm(PS[:, bg, :], Wb[:, kc, 3328 + gc * 128:3328 + (gc + 1) * 128], hT[:, kc, :], kc == 0, kc == 7, [RW, RH], [RB[bg]])
                o = go.get()
                S.op("act", lambda e: e.activation(out=o[0][:], in_=PS[:, bg, :], func=AF.Sigmoid, bias=bgate[:, gc:gc + 1]),
                     reads=[RB[bg], Rbg], writes=[o[1]])
                S.dma(gS[gc * 128:(gc + 1) * 128, t0:t0 + 512], o[0][:], reads=[o[1]], dsem=o[2])
            for a in range(4):
                for kc in range(8):
                    mm(PS[:, 4, :], hT[:, kc, a * 128:(a + 1) * 128], Wb[:, kc, 1536:2048], kc == 0, kc == 7, [RW, RH], [RB[4]])
                for kc in range(8):
                    mm(PS[:, 5, 0:256], hT[:, kc, a * 128:(a + 1) * 128], Wb[:, kc, 2048:2304], kc == 0, kc == 7, [RW, RH], [RB[5]])
                o = vo.get()
                S.op("act", lambda e: e.copy(o[0][:, 0:512], PS[:, 4, :]), reads=[RB[4]], writes=[o[1]])
                S.op("dve", lambda e: e.tensor_copy(o[0][:, 512:768], PS[:, 5, 0:256]), reads=[RB[5]], writes=[o[1]])
                S.dma(vS[pt + a * 128:pt + (a + 1) * 128, :], o[0][:], reads=[o[1]], dsem=o[2])
    barrier()

    with ExitStack() as st:
        acc = sb(st, "acc", [65, 4, 2048], F32)
        Racc = [Res("acc%d" % i) for i in range(4)]
        Vaug = [sb(st, "Vaug%d" % i, [128, 32, 4, 65], BF16) for i in range(2)]
        RV = [Res("Vaug%d" % i) for i in range(2)]
        dV = [S.new_dsem("Vaug") for _ in range(2)]
        for i in range(2):
            S.op("pool", lambda e: e.memset(Vaug[i][:], 1.0), writes=[RV[i]])
        Qr = Rot(S, st, nc, "Qt", [64, 2048], BF16, 3)
        Kr = Rot(S, st, nc, "Kt", [64, 4096], BF16, 3)
        Pr = Rot(S, st, nc, "Pt", [128, 2, 128], BF16, 4)
        rec = Rot(S, st, nc, "rec", [64, 512], F32, 2)
        ao = sb(st, "ao", [64, 4, 2048], BF16)
        Rao = Res("ao")
        dao = S.new_dsem("ao")

        jobs = []
        for (ts_, L, pb) in seqinfo:
            for s0 in range(0, L, 2048):
                for g in range(3):
                    for hs in range(4):
                        jobs.append(dict(t0g=ts_ + s0, po=pb + PAD + s0, first=(s0 == 0), last=(s0 + 2048 == L), g=g, hs=hs))
        vcount = [0]
        vbuf_of = {}

        def load_v(jb):
            key = (jb["t0g"], jb["g"])
            if key in vbuf_of:
                return
            g = jb["g"]
            d = DIL[g]
            nq = 16 // d
            gb = vcount[0] % 2
            vcount[0] += 1
            vbuf_of[key] = gb
            fst = True
            for r in range(d):
                for j in range(nq + 1):
                    tok = jb["po"] + (128 * j - 64) * d + r
                    S.dma(Vaug[gb][:, r * (nq + 1) + j, :, 0:64],
                          vS[tok:tok + 127 * d + 1:d, g * 256:(g + 1) * 256].rearrange("p (h e) -> p h e", h=4),
                          writes=[RV[gb]], dsem=dV[gb], first=fst)
                    fst = False

        def load_qk(jb):
            g = jb["g"]
            hs = jb["hs"]
            d = DIL[g]
            halo = 64 * d
            hr = (g * 4 + hs) * 64
            Qt, RQ, dQ = Qr.get()
            Kt, RK, dK = Kr.get()
            S.dma(Qt[:], qS[hr:hr + 64, jb["t0g"]:jb["t0g"] + 2048], writes=[RQ], dsem=dQ)
            S.dma(Kt[:, 0:2048 + 2 * halo], kS[hr:hr + 64, jb["po"] - halo:jb["po"] + 2048 + halo], writes=[RK], dsem=dK)
            jb["qk"] = (Qt, RQ, Kt, RK)

        units = []
        for ji, jb in enumerate(jobs):
            d = DIL[jb["g"]]
            nq = 16 // d
            for r in range(d):
                for qb in range(nq):
                    units.append((ji, r, qb))
        loaded_upto = [-1]

        def ensure_job(ji):
            while loaded_upto[0] < ji and loaded_upto[0] + 1 < len(jobs):
                loaded_upto[0] += 1
                jb = jobs[loaded_upto[0]]
                load_v(jb)
                load_qk(jb)

        ust_ = {}

        def stage_S(u):
            ji, r, qb = units[u]
            jb = jobs[ji]
            ensure_job(ji)
            d = DIL[jb["g"]]
            Qt, RQ, Kt, RK = jb["qk"]
            c0 = 128 * qb * d + r
            qv = Qt[:, c0:c0 + 127 * d + 1:d]
            kA = Kt[:, c0:c0 + 127 * d + 1:d]
            kB = Kt[:, c0 + 128 * d:c0 + 255 * d + 1:d]
            sbk = u % 4
            pst = PS[:, sbk, 0:256].rearrange("p (a q) -> p a q", a=2)
            mm(pst[:, 0, :], kA, qv, True, False, [RK, RQ], [RB[sbk]])
            mm(pst[:, 0, :], ident_bf[:], maskAB[:, 0, :], False, True, [RC], [RB[sbk]])
            mm(pst[:, 1, :], kB, qv, True, False, [RK, RQ], [RB[sbk]])
            mm(pst[:, 1, :], ident_bf[:], maskAB[:, 1, :], False, True, [RC], [RB[sbk]])
            ust_[u] = (sbk, pst, c0)

        def stage_rest(u):
            ji, r, qb = units[u]
            jb = jobs[ji]
            g = jb["g"]
            hs = jb["hs"]
            d = DIL[g]
            nq = 16 // d
            gb = vbuf_of[(jb["t0g"], g)]
            sbk, pst, c0 = ust_.pop(u)
            bA = blo if (jb["first"] and qb == 0) else bz
            bB = bhi if (jb["last"] and qb == nq - 1) else bz
            Pt, RP, _ = Pr.get()
            if bA is bB:
                S.op("act", lambda e: e.activation(out=Pt[:], in_=pst, func=AF.Exp, bias=bz[:], scale=0.125),
                     reads=[RB[sbk], RC], writes=[RP])
            else:
                S.op("act", lambda e: e.activation(out=Pt[:, 0, :], in_=pst[:, 0, :], func=AF.Exp, bias=bA[:], scale=0.125),
                     reads=[RB[sbk], RC], writes=[RP])
                S.op("act", lambda e: e.activation(out=Pt[:, 1, :], in_=pst[:, 1, :], func=AF.Exp, bias=bB[:], scale=0.125),
                     reads=[RB[sbk], RC], writes=[RP])
            return (Pt, RP, gb, g, hs, d, nq, r, qb, c0)

        def stage_PV(u, ctx):
            Pt, RP, gb, g, hs, d, nq, r, qb, c0 = ctx
            obk = 4 + (u % 4)
            pov = PS[0:65, obk, 0:128]
            blk = r * (nq + 1) + qb
            mm(pov, Vaug[gb][:, blk, hs, :], Pt[:, 0, :], True, False, [RV[gb], RP], [RB[obk]])
            mm(pov, Vaug[gb][:, blk + 1, hs, :], Pt[:, 1, :], False, True, [RV[gb], RP], [RB[obk]])
            av = acc[:, hs, c0:c0 + 127 * d + 1:d]
            if g == 0:
                S.op("dve", lambda e: e.tensor_copy(av, pov), reads=[RB[obk]], writes=[Racc[hs]], nosame=True)
            else:
                S.op("dve", lambda e: e.tensor_tensor(av, av, pov, op=ALU.add), reads=[RB[obk], Racc[hs]], writes=[Racc[hs]], nosame=True)

        def normalise(jb):
            t0g = jb["t0g"]
            for hs in range(4):
                for ch in range(4):
                    bk = ch % 4
                    mm(PS[0:64, bk, :], ones65[64:65, :], acc[64:65, hs, ch * 512:(ch + 1) * 512], True, True, [Racc[hs], RC], [RB[bk]])
                    rc_ = rec.get()
                    S.op("dve", lambda e: e.reciprocal(rc_[0][:], PS[0:64, bk, :]), reads=[RB[bk]], writes=[rc_[1]])
                    S.op("pool", lambda e: e.tensor_tensor(ao[:, hs, ch * 512:(ch + 1) * 512], acc[0:64, hs, ch * 512:(ch + 1) * 512],
                                                          rc_[0][:], op=ALU.mult), reads=[Racc[hs], rc_[1]], writes=[Rao])
            S.dma(aS[:, t0g:t0g + 2048].rearrange("(h e) t -> e h t", h=4), ao[:], reads=[Rao], dsem=dao)

        NU = len(units)
        ensure_job(0)
        LOOK = 2
        UPT = 192
        for tile0 in range(0, NU, UPT):
            tend = tile0 + UPT
            for u in range(tile0, min(tile0 + LOOK, tend)):
                stage_S(u)
            for u in range(tile0, tend):
                ctx = stage_rest(u)
                if u + LOOK < tend:
                    stage_S(u + LOOK)
                stage_PV(u, ctx)
                ensure_job(min(units[u][0] + 1, len(jobs) - 1))
            normalise(jobs[units[tile0][0]])
    barrier()

    with ExitStack() as st:
        pww, Rpww = load_w_bf16(st, "pww", pww_in, 4, D)
        wup, Rwup = load_w_bf16(st, "wup", wup_in, 4, D, rows=64)
        wout, Rwout = load_w_bf16(st, "wout", wout_in, 8, D)
        dww, Rdww = load_small(st, "dww", dww_in, [128, 4, 31])
        cpar, Rcp = load_small(st, "cpar", cpar_in, [128, 3, 4])
        pwb, Rpwb = load_small(st, "pwb", pwb_in, [128, 8])
        Dg = sb(st, "Dg", [128, 124, 128], BF16)
        RDg = Res("Dg")
        for c in range(4):
            for k in range(31):
                eng = "dve" if (k % 2 == 0) else "pool"
                S.op(eng, lambda e: e.tensor_scalar(Dg[:, c * 31 + k, :], ident_f[:], dww[:, c, k:k + 1], None, op0=ALU.mult),
                     reads=[Rdww, RC], writes=[RDg])
        Ur = Rot(S, st, nc, "U", [128, 4, 542], BF16, 2)
        Ar = Rot(S, st, nc, "At", [64, 4, 512], BF16, 2)
        Gr = Rot(S, st, nc, "G", [128, 2, 512], BF16, 3)
        xr = Rot(S, st, nc, "x3", [128, D], F32, 2)
        uc = sb(st, "uc", [128, 4, 512], BF16)
        sq = sb(st, "sq", [128, 4, 512], BF16)
        Ruc = Res("uc")
        Rsq = Res("sq")
        msq = sb(st, "msq", [128, 512], F32)
        rstd = sb(st, "rstd", [128, 512], F32)
        nmr = sb(st, "nmr", [128, 512], F32)
        Rst = Res("stats")
        tt = Rot(S, st, nc, "tt", [128, 512], F32, 2)
        sl = sb(st, "sl", [128, 4, 512], BF16)
        Rsl = Res("sl")
        m1 = Rot(S, st, nc, "m1", [128, 512], F32, 2)
        m2 = Rot(S, st, nc, "m2", [128, 512], F32, 2)
        mx = sb(st, "mx", [128, 8, 512], BF16)
        Rmx = Res("mx")
        x2r = Rot(S, st, nc, "x2o", [128, D], F32, 2)

        for ti, (t0, p0, pt) in enumerate(tiles):
            U, RU, dU = Ur.get()
            S.dma(U[:], uS[:, pt - 15:pt + 527].rearrange("(c p) t -> p c t", p=128), writes=[RU], dsem=dU)
            At, RA, dA = Ar.get()
            S.dma(At[:], aS[:, t0:t0 + 512].rearrange("(h e) t -> e h t", h=4), writes=[RA], dsem=dA)
            for c in range(4):
                bk = nbank()
                for k in range(31):
                    mm(PS[:, bk, :], Dg[:, c * 31 + k, :], U[:, c, k:k + 512], k == 0, k == 30, [RDg, RU], [RB[bk]])
                S.op("act", lambda e: e.activation(out=uc[:, c, :], in_=PS[:, bk, :], func=AF.Identity, bias=cpar[:, 0, c:c + 1]),
                     reads=[RB[bk], Rcp], writes=[Ruc])
                S.op("act", lambda e: e.activation(out=sq[:, c, :], in_=PS[:, bk, :], func=AF.Square, bias=cpar[:, 0, c:c + 1]),
                     reads=[RB[bk], Rcp], writes=[Rsq])
            bme = nbank()
            bex = nbank()
            for c in range(4):
                mm(PS[:, bme, :], onesM[:], uc[:, c, :], c == 0, c == 3, [RC, Ruc], [RB[bme]])
            for c in range(4):
                mm(PS[:, bex, :], onesM[:], sq[:, c, :], c == 0, c == 3, [RC, Rsq], [RB[bex]])
            S.op("act", lambda e: e.activation(out=msq[:], in_=PS[:, bme, :], func=AF.Square), reads=[RB[bme]], writes=[Rst])
            S.op("dve", lambda e: e.tensor_tensor(rstd[:], PS[:, bex, :], msq[:], op=ALU.subtract), reads=[RB[bex], Rst], writes=[Rst])
            S.op("act", lambda e: e.activation(out=rstd[:], in_=rstd[:], func=AF.Sqrt, bias=epsc[:]), reads=[Rst, RC], writes=[Rst])
            S.op("dve", lambda e: e.reciprocal(rstd[:], rstd[:]), reads=[Rst], writes=[Rst])
            S.op("dve", lambda e: e.tensor_tensor(nmr[:], PS[:, bme, :], rstd[:], op=ALU.mult), reads=[RB[bme], Rst], writes=[Rst])
            for c in range(4):
                t_, Rt_, _ = tt.get()
                S.op("dve", lambda e: e.tensor_tensor(t_[:], uc[:, c, :], rstd[:], op=ALU.mult), reads=[Ruc, Rst], writes=[Rt_])
                S.op("pool", lambda e: e.tensor_tensor(t_[:], t_[:], nmr[:], op=ALU.subtract), reads=[Rt_, Rst], writes=[Rt_])
                S.op("act", lambda e: e.activation(out=sl[:, c, :], in_=t_[:], func=AF.Silu, bias=cpar[:, 2, c:c + 1], scale=cpar[:, 1, c:c + 1]),
                     reads=[Rt_, Rcp], writes=[Rsl])
            for dc in range(8):
                G, RG, dG = Gr.get()
                S.dma(G[:], gS[:, t0:t0 + 512].rearrange("(b c p) t -> p b c t", b=2, p=128)[:, :, dc, :], writes=[RG], dsem=dG)
                bcv = nbank()
                bat = nbank()
                for c in range(4):
                    mm(PS[:, bcv, :], pww[:, c, dc * 128:(dc + 1) * 128], sl[:, c, :], c == 0, c == 3, [Rpww, Rsl], [RB[bcv]])
                for hs in range(4):
                    mm(PS[:, bat, :], wup[:, hs, dc * 128:(dc + 1) * 128], At[:, hs, :], hs == 0, hs == 3, [Rwup, RA], [RB[bat]])
                a1 = m1.get()
                a2 = m2.get()
                S.op("dve", lambda e: e.scalar_tensor_tensor(a1[0][:], PS[:, bcv, :], pwb[:, dc:dc + 1], G[:, 1, :], op0=ALU.add, op1=ALU.mult),
                     reads=[RB[bcv], Rpwb, RG], writes=[a1[1]])
                S.op("dve", lambda e: e.tensor_tensor(a2[0][:], PS[:, bat, :], G[:, 0, :], op=ALU.mult), reads=[RB[bat], RG], writes=[a2[1]])
                S.op("pool", lambda e: e.tensor_tensor(mx[:, dc, :], a1[0][:], a2[0][:], op=ALU.add), reads=[a1[1], a2[1]], writes=[Rmx])
            for a in range(4):
                xt_, Rx, dX = xr.get()
                S.dma(xt_[:], x_in[t0 + a * 128:t0 + (a + 1) * 128, :], writes=[Rx], dsem=dX)
                b0 = 4 + 2 * (a % 2)
                for n in range(2):
                    for kc in range(8):
                        mm(PS[:, b0 + n, :], mx[:, kc, a * 128:(a + 1) * 128], wout[:, kc, n * 512:(n + 1) * 512], kc == 0, kc == 7,
                           [Rmx, Rwout], [RB[b0 + n]])
                o, Ro, dO = x2r.get()
                S.op("dve", lambda e: e.tensor_tensor(o[:].rearrange("p (n f) -> p n f", n=2), PS[:, b0:b0 + 2, :],
                                                     xt_[:].rearrange("p (n f) -> p n f", n=2), op=ALU.add),
                     reads=[RB[b0], RB[b0 + 1], Rx], writes=[Ro])
                S.dma(x2S[t0 + a * 128:t0 + (a + 1) * 128, :], o[:], reads=[Ro], dsem=dO)
    barrier()

    with ExitStack() as st:
        wq, Rwq = load_w_bf16(st, "wq", wq_in, 8, 2048)
        kst = sb(st, "kst", [128, 16, 128], F32)
        Rks = Res("kst")
        S.dma(kst[:], keysT_in, writes=[Rks], dsem=S.new_dsem("kst"))
        keysT = sb(st, "keysT", [128, 16, 128], BF16)
        RkT = Res("keysT")
        S.op("dve", lambda e: e.tensor_copy(keysT[:], kst[:]), reads=[Rks], writes=[RkT])
        g2b, Rg2 = load_small(st, "g2b", g2b_in, [128, D])
        xr = Rot(S, st, nc, "x2i", [128, D], F32, 2)
        junk = sb(st, "junk2", [128, D], BF16)
        Rjunk = Res("junk2")
        ssq = sb(st, "ssq2", [128, 8], F32)
        Rssq = [Res("ssq2_%d" % i) for i in range(8)]
        hbr = Rot(S, st, nc, "h2b", [128, D], BF16, 2)
        hTr = Rot(S, st, nc, "h2T", [128, 8, 512], BF16, 2)
        qpTs = [sb(st, "qpT%d" % i, [128, 16, 512], BF16) for i in range(2)]
        RqpTs = [Res("qpT%d" % i) for i in range(2)]
        class BS:
            pass

        BSETS = []
        for bi_ in range(2):
            B = BS()
            B.Ssb = sb(st, "Ssb%d" % bi_, [128, 16, 128], F32)
            B.RS_ = Res("Ssb")
            B.wk = sb(st, "wk%d" % bi_, [128, 16, 128], F32)
            B.Rwk = Res("wk")
            B.tops = sb(st, "tops%d" % bi_, [128, 16, 16], F32)
            B.Rtops = Res("tops")
            B.tidx = sb(st, "tidx%d" % bi_, [128, 16, 16], U32)
            B.Rtidx = Res("tidx")
            B.tif = sb(st, "tif%d" % bi_, [128, 16, 16], F32)
            B.Rtif = Res("tif")
            B.best = sb(st, "best%d" % bi_, [128, 8, 16], F32)
            B.Rbest = Res("best")
            B.bidx = sb(st, "bidx%d" % bi_, [128, 8, 16], U32)
            B.Rbidx = Res("bidx")
            B.ab_u = sb(st, "ab_u%d" % bi_, [128, 2, 8, 16], U32)
            B.ab_f = sb(st, "ab_f%d" % bi_, [128, 2, 8, 16], F32)
            B.Rab = Res("ab")
            B.ge = sb(st, "ge%d" % bi_, [128, 8, 16], F32)
            B.gs = sb(st, "gs%d" % bi_, [128, 8], F32)
            B.Rge = Res("ge")
            B.E0 = sb(st, "E0%d" % bi_, [128, 8, 16, 16], F32)
            B.RE0 = Res("E0")
            B.Rt = sb(st, "Rt%d" % bi_, [128, 3, 128], F32)
            B.RRt = Res("Rt")
            BSETS.append(B)
        RTr = Rot(S, st, nc, "RT", [128, 3, 128], F32, 2)

        ust = Rot(S, st, nc, "ust", [128, D], F32, 2)
        vst = Rot(S, st, nc, "vst", [128, D], F32, 2)
        ubf = Rot(S, st, nc, "ubf", [128, D], BF16, 1)
        uts = Rot(S, st, nc, "uts", [128, D], BF16, 2)
        vbf = Rot(S, st, nc, "vbf", [128, D], BF16, 2)
        NI = NEXP // 128
        tp_loaded = {}
        tp_next = [0]
        tp_done = [0]

        def tp_load():
            i = tp_next[0]
            if i >= NI:
                return
            a = ust.get()
            b = vst.get()
            S.dma(a[0][:], pu_in[i * 128:(i + 1) * 128, :], writes=[a[1]], dsem=a[2])
            S.dma(b[0][:], pv_in[i * 128:(i + 1) * 128, :], writes=[b[1]], dsem=b[2])
            tp_loaded[i] = (a, b)
            tp_next[0] += 1

        def tp_step():
            i = tp_done[0]
            if i >= NI:
                return
            if i not in tp_loaded:
                tp_load()
            tp_load()
            a, b = tp_loaded.pop(i)
            ub = ubf.get()
            S.op("act", lambda e: e.copy(ub[0][:], a[0][:]), reads=[a[1]], writes=[ub[1]])
            ptb = PS[:, 7, :].bitcast(BF16)
            for kc in range(8):
                S.op("pe", lambda e: e.transpose(ptb[:, kc * 128:(kc + 1) * 128], ub[0][:, kc * 128:(kc + 1) * 128], ident_bf[:]),
                     reads=[ub[1]], writes=[RB[7]])
            us_ = uts.get()
            S.op("act", lambda e: e.copy(us_[0][:], ptb), reads=[RB[7]], writes=[us_[1]])
            S.dma(UT[i], us_[0][:], reads=[us_[1]], dsem=us_[2])
            vb_ = vbf.get()
            S.op("pool", lambda e: e.tensor_copy(vb_[0][:], b[0][:]), reads=[b[1]], writes=[vb_[1]])
            S.dma(VB[i * 128:(i + 1) * 128, :], vb_[0][:], reads=[vb_[1]], dsem=vb_[2])
            tp_done[0] += 1

        def routing(a, B, t0, qpT, RqpT):
            Ssb, RS_, wk, Rwk = B.Ssb, B.RS_, B.wk, B.Rwk
            tops, Rtops, tidx, Rtidx, tif, Rtif = B.tops, B.Rtops, B.tidx, B.Rtidx, B.tif, B.Rtif
            best, Rbest, bidx, Rbidx = B.best, B.Rbest, B.bidx, B.Rbidx
            ab_u, ab_f, Rab, ge, gs, Rge = B.ab_u, B.ab_f, B.Rab, B.ge, B.gs, B.Rge
            Rt, RRt = B.Rt, B.RRt
            cand = wk[:].rearrange("p (h c) k -> p h (c k)", c=2)
            Rcand = Rwk
            wk2 = Ssb[:].rearrange("p (h c) k -> p h (c k)", c=2)
            Rwk2 = RS_
            for hc in range(16):
                b = hc // 4
                mm(PS[:, b, (hc % 4) * 128:(hc % 4 + 1) * 128], qpT[:, hc, a * 128:(a + 1) * 128], keysT[:, hc, :], True, True,
                   [RqpT, RkT], [RB[b]])
            S.op("act", lambda e: e.copy(Ssb[:].rearrange("p (b q) k -> p b (q k)", b=4), PS[:, 0:4, :]),
                 reads=[RB[0], RB[1], RB[2], RB[3]], writes=[RS_])
            yield
            for hc in range(16):
                S.op("dve", lambda e: e.max(out=tops[:, hc, 0:8], in_=Ssb[:, hc, :]), reads=[RS_], writes=[Rtops])
            yield
            for hc in range(16):
                S.op("dve", lambda e: e.max_index(out=tidx[:, hc, 0:8], in_max=tops[:, hc, 0:8], in_values=Ssb[:, hc, :]),
                     reads=[RS_, Rtops], writes=[Rtidx])
            for hc in range(16):
                S.op("dve", lambda e: e.match_replace(out=wk[:, hc, :], in_to_replace=tops[:, hc, 0:8], in_values=Ssb[:, hc, :], imm_value=-1e30),
                     reads=[RS_, Rtops], writes=[Rwk])
            yield
            for hc in range(16):
                S.op("dve", lambda e: e.max(out=tops[:, hc, 8:16], in_=wk[:, hc, :]), reads=[Rwk], writes=[Rtops])
            yield
            for hc in range(16):
                S.op("dve", lambda e: e.max_index(out=tidx[:, hc, 8:16], in_max=tops[:, hc, 8:16], in_values=wk[:, hc, :]),
                     reads=[Rwk, Rtops], writes=[Rtidx])
            yield
            S.op("dve", lambda e: e.tensor_copy(tif[:], tidx[:]), reads=[Rtidx], writes=[Rtif])
            tops4 = tops[:].rearrange("p (h c) k -> p h c k", c=2)
            tif4 = tif[:].rearrange("p (h c) k -> p h c k", c=2)
            S.op("dve", lambda e: e.tensor_tensor(cand.rearrange("p h (a b) -> p h a b", a=16),
                                                 tops4[:, :, 0, :].unsqueeze(3).broadcast_to([128, 8, 16, 16]),
                                                 tops4[:, :, 1, :].unsqueeze(2).broadcast_to([128, 8, 16, 16]), op=ALU.add),
                 reads=[Rtops], writes=[Rcand])
            yield
            for h in range(8):
                S.op("dve", lambda e: e.max(out=best[:, h, 0:8], in_=cand[:, h, :]), reads=[Rcand], writes=[Rbest])
            yield
            for h in range(8):
                S.op("dve", lambda e: e.max_index(out=bidx[:, h, 0:8], in_max=best[:, h, 0:8], in_values=cand[:, h, :]),
                     reads=[Rcand, Rbest], writes=[Rbidx])
            for h in range(8):
                S.op("dve", lambda e: e.match_replace(out=wk2[:, h, :], in_to_replace=best[:, h, 0:8], in_values=cand[:, h, :], imm_value=-1e30),
                     reads=[Rcand, Rbest], writes=[Rwk2])
            yield
            for h in range(8):
                S.op("dve", lambda e: e.max(out=best[:, h, 8:16], in_=wk2[:, h, :]), reads=[Rwk2], writes=[Rbest])
            yield
            for h in range(8):
                S.op("dve", lambda e: e.max_index(out=bidx[:, h, 8:16], in_max=best[:, h, 8:16], in_values=wk2[:, h, :]),
                     reads=[Rwk2, Rbest], writes=[Rbidx])
            S.op("dve", lambda e: e.tensor_tensor(ge[:], best[:], best[:, :, 0:1].broadcast_to([128, 8, 16]), op=ALU.subtract),
                 reads=[Rbest], writes=[Rge])
            yield
            S.op("act", lambda e: e.activation(out=ge[:], in_=ge[:], func=AF.Exp), reads=[Rge], writes=[Rge])
            S.op("dve", lambda e: e.tensor_single_scalar(ab_u[:, 0, :, :], bidx[:], 4, op=ALU.logical_shift_right), reads=[Rbidx], writes=[Rab])
            S.op("dve", lambda e: e.tensor_single_scalar(ab_u[:, 1, :, :], bidx[:], 15, op=ALU.bitwise_and), reads=[Rbidx], writes=[Rab])
            yield
            S.op("dve", lambda e: e.tensor_copy(ab_f[:], ab_u[:]), reads=[Rab], writes=[Rab])
            S.op("dve", lambda e: e.reduce_sum(gs[:], ge[:], axis=AX.X), reads=[Rge], writes=[Rge])
            yield
            S.op("dve", lambda e: e.reciprocal(gs[:], gs[:]), reads=[Rge], writes=[Rge])
            yield
            S.op("dve", lambda e: e.tensor_tensor(Rt[:, 2, :].rearrange("p (h k) -> p h k", h=8), ge[:],
                                                 gs[:].unsqueeze(2).broadcast_to([128, 8, 16]), op=ALU.mult),
                 reads=[Rge], writes=[RRt])
            io16 = iota_f[:, 0:16].unsqueeze(1).unsqueeze(1).broadcast_to([128, 8, 16, 16])
            for c in range(2):
                EE, REE = B.E0, B.RE0
                S.op("dve", lambda e: e.tensor_tensor(EE[:], io16, ab_f[:, c, :, :].unsqueeze(3).broadcast_to([128, 8, 16, 16]), op=ALU.is_equal),
                     reads=[Rab, RC], writes=[REE])
                yield
                S.op("dve", lambda e: e.tensor_tensor(EE[:], EE[:], tif4[:, :, c, :].unsqueeze(2).broadcast_to([128, 8, 16, 16]), op=ALU.mult),
                     reads=[REE, Rtif], writes=[REE])
                yield
                S.op("dve", lambda e: e.reduce_sum(Rt[:, c, :].rearrange("p (h k) -> p h k", h=8), EE[:], axis=AX.X),
                     reads=[REE], writes=[RRt])
                yield
            for c in range(3):
                S.op("pe", lambda e: e.transpose(PS[:, 6, c * 128:(c + 1) * 128], Rt[:, c, :], ident_f[:]), reads=[RRt, RC], writes=[RB[6]])
            RT_, RRT, dRT = RTr.get()
            S.op("act", lambda e: e.copy(RT_[:], PS[:, 6, 0:384].rearrange("p (c t) -> p c t", c=3)), reads=[RB[6]], writes=[RRT])
            tb = t0 + a * 128
            S.dma(rS[:, :, tb:tb + 128].rearrange("c p t -> p c t"), RT_[:], reads=[RRT], dsem=dRT)

        def front(ti):
            t0, p0, pt = tiles[ti]
            qpT = qpTs[ti % 2]
            RqpT = RqpTs[ti % 2]
            hT, RH, dH = hTr.get()
            for a in range(4):
                bi = ti * 4 + a
                xt_, Rx, dX = xr.get()
                S.dma(xt_[:], x2S[t0 + a * 128:t0 + (a + 1) * 128, :], writes=[Rx], dsem=dX)
                hb, Rhb, _ = hbr.get()
                col = bi % 8
                rms_block(xt_[:], Rx, g2b, Rg2, hb[:], Rhb, junk, Rjunk, ssq, Rssq[col], col)
                bank = 6
                ptb = PS[:, bank, :].bitcast(BF16)
                for kc in range(8):
                    S.op("pe", lambda e: e.transpose(ptb[:, kc * 128:(kc + 1) * 128], hb[:, kc * 128:(kc + 1) * 128], ident_bf[:]),
                         reads=[Rhb], writes=[RB[bank]])
                S.op("act", lambda e: e.copy(hT[:, :, a * 128:(a + 1) * 128], ptb.rearrange("p (k t) -> p k t", k=8)),
                     reads=[RB[bank]], writes=[RH])
            S.dma(h2S[:, t0:t0 + 512].rearrange("(k p) t -> p k t", p=128), hT[:], reads=[RH], dsem=dH)
            for hc in range(16):
                bk = 4 + (hc % 2)
                for kc in range(8):
                    mm(PS[:, bk, :], wq[:, kc, hc * 128:(hc + 1) * 128], hT[:, kc, :], kc == 0, kc == 7, [Rwq, RH], [RB[bk]])
                S.op("act", lambda e: e.copy(qpT[:, hc, :], PS[:, bk, :]), reads=[RB[bk]], writes=[RqpT])

        front(0)
        for ti, (t0, p0, pt) in enumerate(tiles):
            if ti + 1 < len(tiles):
                front(ti + 1)
            for a0 in (0, 2):
                for _ in range((NI * 2 + 2 * len(tiles) - 1) // (2 * len(tiles))):
                    tp_step()
                gens = [routing(a0, BSETS[0], t0, qpTs[ti % 2], RqpTs[ti % 2]), routing(a0 + 1, BSETS[1], t0, qpTs[ti % 2], RqpTs[ti % 2])]
                alive = [True, True]
                while any(alive):
                    for gi in range(2):
                        if alive[gi]:
                            try:
                                next(gens[gi])
                            except StopIteration:
                                alive[gi] = False
        while tp_done[0] < NI:
            tp_step()
    barrier()

    TT = 256
    with ExitStack() as st:
        gfb, Rgf = load_small(st, "gfb", gfb_in, [128, D])
        WTs = [sb(st, "WT%d" % i, [128, TT, 128], BF16) for i in range(2)]
        RWTs = [Res("WT%d" % i) for i in range(2)]
        rtr = Rot(S, st, nc, "rt4", [128, 3, TT], F32, 2)
        h2r = Rot(S, st, nc, "h2T4", [128, 8, TT], BF16, 2)
        x2r = Rot(S, st, nc, "x24", [128, D], F32, 1)
        TB = 16
        Lr = Rot(S, st, nc, "Lb", [128, TB, 128], BF16, 2)
        Rr = Rot(S, st, nc, "Rb", [128, TB, 128], BF16, 2)
        Ubr = Rot(S, st, nc, "Ub", [128, 2, D], BF16, 3)
        Vbr = Rot(S, st, nc, "Vb", [128, 2, D], BF16, 3)
        Asb = Rot(S, st, nc, "Asb", [128, 2, TT], BF16, 2)
        WAr = Rot(S, st, nc, "WA", [128, 2, TT], BF16, 2)
        yo = Rot(S, st, nc, "yo", [128, D], F32, 2)
        junk = sb(st, "junk4", [128, D], BF16)
        Rjunk = Res("junk4")
        ssq = sb(st, "ssq4", [128, 8], F32)
        Rssq = [Res("ssq4_%d" % i) for i in range(8)]
        ntile = NT // TT
        NP = 64
        NTB = TT // TB
        PER = NP // NTB
        pend = {}

        def p4_load(ti):
            tb = ti * TT
            r_ = rtr.get()
            S.dma(r_[0][:], rS[:, :, tb:tb + TT].rearrange("c p t -> p c t"), writes=[r_[1]], dsem=r_[2])
            h_ = h2r.get()
            S.dma(h_[0][:], h2S[:, tb:tb + TT].rearrange("(k p) t -> p k t", p=128), writes=[h_[1]], dsem=h_[2])
            pend[ti] = (r_, h_)

        chunkq = {}
        cnext = [0]
        total = ntile * NP

        def p4_chunk_prefetch(upto):
            while cnext[0] <= upto and cnext[0] < total:
                ip = cnext[0] % NP
                u_ = Ubr.get()
                v_ = Vbr.get()
                S.dma(u_[0][:], UT[2 * ip:2 * ip + 2].rearrange("i p f -> p i f"), writes=[u_[1]], dsem=u_[2])
                S.dma(v_[0][:], VB[2 * ip * 128:(2 * ip + 2) * 128, :].rearrange("(i p) f -> p i f", p=128), writes=[v_[1]], dsem=v_[2])
                chunkq[cnext[0]] = (u_, v_)
                cnext[0] += 1

        io = iota_f[:].unsqueeze(1).broadcast_to([128, TB, 128])
        evn = [0]

        def onehot(ti, tb):
            rt_, Rrt, _ = pend[ti][0]
            Lb, RL, _ = Lr.get()
            Rb, RR, _ = Rr.get()
            S.op("dve", lambda e: e.tensor_tensor(Lb[:], io, rt_[:, 0, tb * TB:(tb + 1) * TB].unsqueeze(2).broadcast_to([128, TB, 128]), op=ALU.is_equal),
                 reads=[Rrt, RC], writes=[RL])
            S.op("dve", lambda e: e.tensor_tensor(Lb[:], Lb[:], rt_[:, 2, tb * TB:(tb + 1) * TB].unsqueeze(2).broadcast_to([128, TB, 128]), op=ALU.mult),
                 reads=[Rrt, RL], writes=[RL])
            S.op("dve", lambda e: e.tensor_tensor(Rb[:], io, rt_[:, 1, tb * TB:(tb + 1) * TB].unsqueeze(2).broadcast_to([128, TB, 128]), op=ALU.is_equal),
                 reads=[Rrt, RC], writes=[RR])
            return (Lb, RL, Rb, RR)

        def wmm(ti, tb, LR):
            Lb, RL, Rb, RR = LR
            WT = WTs[ti % 2]
            RWT = RWTs[ti % 2]
            for q4 in range(TB // 4):
                bk = 6 + (evn[0] % 2)
                evn[0] += 1
                pw = PS[:, bk, :].rearrange("p (t i) -> p t i", t=4)
                for tq in range(4):
                    tl = q4 * 4 + tq
                    mm(pw[:, tq, :], Rb[:, tl, :], Lb[:, tl, :], True, True, [RL, RR], [RB[bk]])
                tg = tb * TB + q4 * 4
                S.op("act", lambda e: e.copy(WT[:, tg:tg + 4, :], pw), reads=[RB[bk]], writes=[RWT])

        def u_mm(ci):
            ti, ip = divmod(ci, NP)
            h2T, Rh2, _ = pend[ti][1]
            (Ub, RU, _), _v = chunkq[ci]
            bk = 4 + (ci % 2)
            pa = PS[:, bk, :].rearrange("p (i t) -> p i t", i=2)
            for ii in range(2):
                for kc in range(8):
                    mm(pa[:, ii, :], Ub[:, ii, kc * 128:(kc + 1) * 128], h2T[:, kc, :], kc == 0, kc == 7, [RU, Rh2], [RB[bk]])

        p4_load(0)
        p4_chunk_prefetch(2)
        for tb in range(NTB):
            wmm(0, tb, onehot(0, tb))
        u_mm(0)
        pendLR = None
        for ti in range(ntile):
            if ti + 1 < ntile:
                p4_load(ti + 1)
            WT = WTs[ti % 2]
            RWT = RWTs[ti % 2]
            for ip in range(NP):
                ci = ti * NP + ip
                p4_chunk_prefetch(ci + 2)
                if ti + 1 < ntile and ip % PER == 0:
                    if pendLR is not None:
                        wmm(ti + 1, ip // PER - 1, pendLR)
                    pendLR = onehot(ti + 1, ip // PER)
                if ci + 1 < total:
                    u_mm(ci + 1)
                _u, (Vb, RVb, _) = chunkq.pop(ci)
                bk = 4 + (ci % 2)
                pa = PS[:, bk, :].rearrange("p (i t) -> p i t", i=2)
                A_, RA_, _ = Asb.get()
                S.op("act", lambda e: e.activation(out=A_[:], in_=pa, func=AF.Gelu), reads=[RB[bk]], writes=[RA_])
                WA, RWA, _ = WAr.get()
                S.op("pool", lambda e: e.tensor_tensor(WA[:], A_[:], WT[:, :, 2 * ip:2 * ip + 2].rearrange("p t i -> p i t"), op=ALU.mult),
                     reads=[RA_, RWT], writes=[RWA])
                for ii in range(2):
                    for th in range(2):
                        for n in range(2):
                            bo = th * 2 + n
                            mm(PS[:, bo, :], WA[:, ii, th * 128:(th + 1) * 128], Vb[:, ii, n * 512:(n + 1) * 512],
                               ip == 0 and ii == 0, ip == NP - 1 and ii == 1, [RWA, RVb], [RB[bo]])
            if pendLR is not None:
                wmm(ti + 1, NTB - 1, pendLR)
                pendLR = None
            for th in range(2):
                tb_ = ti * TT + th * 128
                o, Ro, dX = x2r.get()
                S.dma(o[:], x2S[tb_:tb_ + 128, :], writes=[Ro], dsem=dX)
                S.op("dve", lambda e: e.tensor_tensor(o[:].rearrange("p (n f) -> p n f", n=2), PS[:, 2 * th:2 * th + 2, :],
                                                     o[:].rearrange("p (n f) -> p n f", n=2), op=ALU.add),
                     reads=[RB[2 * th], RB[2 * th + 1], Ro], writes=[Ro])
                y_, Ry, dY = yo.get()
                col = (ti * 2 + th) % 8
                S.op("act", lambda e: e.activation(out=junk[:], in_=o[:], func=AF.Square, accum_out=ssq[:, col:col + 1]),
                     reads=[Ro], writes=[Rjunk, Rssq[col]])
                S.op("act", lambda e: e.activation(out=ssq[:, col:col + 1], in_=ssq[:, col:col + 1], func=AF.Sqrt, bias=epsc[:], scale=1.0 / D),
                     reads=[Rssq[col], RC], writes=[Rssq[col]])
                S.op("dve", lambda e: e.reciprocal(ssq[:, col:col + 1], ssq[:, col:col + 1]), reads=[Rssq[col]], writes=[Rssq[col]])
                S.op("dve", lambda e: e.scalar_tensor_tensor(y_[:], o[:], ssq[:, col:col + 1], gfb[:], op0=ALU.mult, op1=ALU.mult),
                     reads=[Ro, Rssq[col], Rgf], writes=[Ry])
                S.dma(y_out[tb_:tb_ + 128, :], y_[:], reads=[Ry], dsem=dY)
            pend.pop(ti)
    barrier()
    top.close()
    return nc


def rope_tables(smax=8192):
    half = 8
    inv = (np.float32(500000.0) ** (-np.arange(half, dtype=np.float32) * np.float32(2.0) / np.float32(16))).astype(np.float32)
    pos = np.arange(smax, dtype=np.float32)
    ang = (pos[:, None] * inv[None, :]).astype(np.float32)
    cos = np.cos(ang).astype(np.float32).T
    sin = np.sin(ang).astype(np.float32).T
    C = np.ones((128, smax), np.float32)
    Sn = np.zeros((128, smax), np.float32)
    for hh in range(2):
        b = hh * 64
        C[b:b + 8] = cos
        C[b + 8:b + 16] = cos
        Sn[b:b + 8] = sin
        Sn[b + 8:b + 16] = sin
    return C, Sn


def make_shared(inp):
    f = lambda a: np.ascontiguousarray(np.asarray(a, dtype=np.float32))
    C, Sn = rope_tables()
    sh = {
        "w_in": f(inp["w_in"][0]),
        "g1b": f(np.broadcast_to(inp["norm1_g"][0][None, :], (128, D))),
        "g2b": f(np.broadcast_to(inp["norm2_g"][0][None, :], (128, D))),
        "gfb": f(np.broadcast_to(np.asarray(inp["final_g"])[None, :], (128, D))),
        "bgate": f(np.asarray(inp["b_gate"][0]).reshape(16, 128).T),
        "w_up": f(inp["w_attn_up"][0]),
        "dww": f(np.asarray(inp["conv_dw_w"][0]).reshape(31, 4, 128).transpose(2, 1, 0)),
        "cpar": f(np.stack([np.asarray(inp["conv_dw_b"][0]).reshape(4, 128).T,
                            np.asarray(inp["conv_ln_g"][0]).reshape(4, 128).T,
                            np.asarray(inp["conv_ln_b"][0]).reshape(4, 128).T], axis=1)),
        "pw_w": f(inp["conv_pw_w"][0]),
        "pwb": f(np.asarray(inp["conv_pw_b"][0]).reshape(8, 128).T),
        "w_out": f(inp["w_out"][0]),
        "wq": f(inp["peer_wq"][0]),
        "keysT": f(np.asarray(inp["peer_keys"][0]).reshape(16, 128, 128).transpose(2, 0, 1)),
        "peer_u": f(inp["peer_u"][0]),
        "peer_v": f(inp["peer_v"][0]),
        "rope_c": C,
        "rope_s": Sn,
        "dummy8": np.zeros((1, 8), np.float32),
    }
    return sh


def kernel(**inputs):
    xp = np.asarray(inputs["x_prompt"], dtype=np.float32)
    xs = np.asarray(inputs["x_sample"], dtype=np.float32)
    seqs = [xp.shape[1], xs.shape[1], xs.shape[1]]
    nc = build_program(seqs)
    sh = make_shared(inputs)
    in_maps = []
    for c in range(N_CORES):
        m = dict(sh)
        m["x"] = np.ascontiguousarray(np.concatenate([xp[c], xs[2 * c], xs[2 * c + 1]], axis=0))
        in_maps.append(m)
    res = run_bass_kernel_spmd(nc, in_maps, core_ids=list(range(N_CORES)))
    yp = np.empty_like(xp)
    ys = np.empty_like(xs)
    Lp, Ls = seqs[0], seqs[1]
    for c in range(N_CORES):
        y = np.asarray(res.results[c]["y"])
        yp[c] = y[0:Lp]
        ys[2 * c] = y[Lp:Lp + Ls]
        ys[2 * c + 1] = y[Lp + Ls:Lp + 2 * Ls]
    return (yp, ys)
```

```python
from contextlib import ExitStack
import numpy as np
import concourse.bass as bass
import concourse.mybir as mybir
from concourse.bass_utils import run_bass_kernel_spmd

F32 = mybir.dt.float32
BF16 = mybir.dt.bfloat16
U32 = mybir.dt.uint32
AF = mybir.ActivationFunctionType
ALU = mybir.AluOpType
AX = mybir.AxisListType

D = 1024
NCOL = 5376
PAD = 1024
NEXP = 16384
EPS = 1e-6
DIL = (1, 4, 16)
NEG = -30000.0
N_CORES = 8


class Res:
    __slots__ = ("name", "w", "rd")

    def __init__(self, name):
        self.name = name
        self.w = None
        self.rd = {}


class DSem:
    def __init__(self, h, key):
        self.h = h
        self.key = key
        self.count = 0


class Sched:
    CE = ("pe", "act", "dve", "pool")

    def __init__(self, nc):
        self.nc = nc
        self.E = {"pe": nc.tensor, "act": nc.scalar, "dve": nc.vector, "pool": nc.gpsimd, "sp": nc.sync}
        self.semh = {}
        self.cnt = {}
        self.waited = {e: {} for e in self.E}
        for e in self.CE:
            self.semh[e] = nc.alloc_semaphore("s_" + e)
            self.cnt[e] = 0
        self.dsems = []
        self.bar = nc.alloc_semaphore("s_bar")
        self.barcnt = 0
        self.n_ds = 0
        self.free_ds = []
        self.live_ds = []

    def new_dsem(self, name=None):
        if self.free_ds:
            d = self.free_ds.pop()
            self.live_ds.append(d)
            return d
        self.n_ds += 1
        name = "d%d_%s" % (self.n_ds, name or "")
        d = DSem(self.nc.alloc_semaphore(name), name)
        self.semh[name] = d.h
        self.dsems.append(d)
        self.live_ds.append(d)
        return d

    def recycle(self):
        self.free_ds.extend(self.live_ds)
        self.live_ds = []

    def _wait(self, eng, key, val):
        if self.waited[eng].get(key, 0) >= val:
            return
        self.E[eng].wait_ge(self.semh[key], val)
        self.waited[eng][key] = val

    def _deps(self, eng, reads, writes, is_dma=False, nosame=False):
        deps = {}

        def add(key, val, kind):
            if key == eng and not is_dma and (eng == "pe" or kind != "raw" or nosame):
                return
            if deps.get(key, 0) < val:
                deps[key] = val

        for r in reads:
            if r.w is not None:
                add(r.w[0], r.w[1], "raw")
        for r in writes:
            if r.w is not None:
                add(r.w[0], r.w[1], "waw")
            for k, v in r.rd.items():
                add(k, v, "war")
        for k, v in deps.items():
            self._wait(eng, k, v)

    def op(self, eng, fn, reads=(), writes=(), nosame=False):
        self._deps(eng, reads, writes, nosame=nosame)
        ins = fn(self.E[eng])
        self.cnt[eng] += 1
        c = self.cnt[eng]
        ins.then_inc(self.semh[eng], 1)
        for r in reads:
            r.rd[eng] = c
        for r in writes:
            r.w = (eng, c)
            r.rd = {}
        return ins

    def dma(self, out, in_, reads=(), writes=(), dsem=None, first=True, q="sp"):
        if isinstance(dsem, LazyD):
            dsem = dsem.get()
        self._deps(q, reads, writes, is_dma=True)
        if first and dsem.count > 0:
            self._wait(q, dsem.key, dsem.count)
        ins = self.E[q].dma_start(out=out, in_=in_)
        dsem.count += 16
        ins.then_inc(dsem.h, 16)
        for r in reads:
            r.rd[dsem.key] = dsem.count
        for r in writes:
            r.w = (dsem.key, dsem.count)
            r.rd = {}
        return ins

    def barrier(self, dummy_sb, dummy_dram):
        for e in self.CE:
            if self.cnt[e] > 0:
                self._wait("sp", e, self.cnt[e])
        for d in self.dsems:
            if d.count > 0:
                self._wait("sp", d.key, d.count)
        ins = self.E["sp"].dma_start(out=dummy_sb, in_=dummy_dram)
        self.barcnt += 16
        ins.then_inc(self.bar, 16)
        for e in self.E:
            self.E[e].wait_ge(self.bar, self.barcnt)
            for k in self.CE:
                self.waited[e][k] = self.cnt[k]
            for d in self.dsems:
                self.waited[e][d.key] = d.count


class LazyD:
    def __init__(self, S, name):
        self.S = S
        self.name = name
        self.d = None

    def get(self):
        if self.d is None:
            self.d = self.S.new_dsem(self.name)
        return self.d


class Rot:
    ctr = 0

    def __init__(self, S, st, nc, name, shape, dtype, n):
        self.items = []
        for i in range(n):
            Rot.ctr += 1
            t = st.enter_context(nc.sbuf_tensor("rot%d_%s%d" % (Rot.ctr, name, i), list(shape), dtype))
            self.items.append((t, Res("%s%d" % (name, i)), LazyD(S, name)))
        self.i = 0

    def get(self):
        it = self.items[self.i % len(self.items)]
        self.i += 1
        return it


def build_program(seqs, dbg=False):
    nc = bass.Bass("TRN2", target_bir_lowering=False)
    NT = sum(seqs)
    NTP = NT + 2 * PAD * len(seqs)
    seqinfo = []
    ts, ps_ = 0, 0
    for L in seqs:
        assert L % 2048 == 0
        seqinfo.append((ts, L, ps_))
        ts += L
        ps_ += L + 2 * PAD

    def din(name, shape, dt=F32):
        return nc.dram_tensor(name, list(shape), dt, kind="ExternalInput").ap()

    def dscr(name, shape, dt):
        if dbg:
            return nc.dram_tensor(name, list(shape), dt, kind="ExternalOutput").ap()
        return nc.dram_tensor(name, list(shape), dt).ap()

    x_in = din("x", [NT, D])
    w_in = din("w_in", [D, NCOL])
    g1b_in = din("g1b", [128, D])
    g2b_in = din("g2b", [128, D])
    gfb_in = din("gfb", [128, D])
    bgate_in = din("bgate", [128, 16])
    wup_in = din("w_up", [256, D])
    dww_in = din("dww", [128, 4, 31])
    cpar_in = din("cpar", [128, 3, 4])
    pww_in = din("pw_w", [512, D])
    pwb_in = din("pwb", [128, 8])
    wout_in = din("w_out", [D, D])
    wq_in = din("wq", [D, 2048])
    keysT_in = din("keysT", [128, 16, 128])
    pu_in = din("peer_u", [NEXP, D])
    pv_in = din("peer_v", [NEXP, D])
    ropec_in = din("rope_c", [128, 8192])
    ropes_in = din("rope_s", [128, 8192])
    dummy_in = din("dummy8", [1, 8])
    y_out = nc.dram_tensor("y", [NT, D], F32, kind="ExternalOutput").ap()

    qS = dscr("qS", [768, NT], BF16)
    kS = dscr("kS", [768, NTP], BF16)
    vS = dscr("vS", [NTP, 768], BF16)
    uS = dscr("uS", [512, NTP], BF16)
    gS = dscr("gS", [2048, NT], BF16)
    aS = dscr("aS", [256, NT], BF16)
    x2S = dscr("x2S", [NT, D], F32)
    h2S = dscr("h2S", [D, NT], BF16)
    rS = dscr("rS", [3, 128, NT], F32)
    UT = nc.dram_tensor("UT", [128, 128, 1024], BF16).ap()
    VB = nc.dram_tensor("VB", [NEXP, D], BF16).ap()

    S = Sched(nc)
    top = ExitStack()

    nctr = [0]

    def sb(st, name, shape, dt):
        nctr[0] += 1
        return st.enter_context(nc.sbuf_tensor("sb%d_%s" % (nctr[0], name), list(shape), dt))

    PS = top.enter_context(nc.psum_tensor("PS", [128, 8, 512], F32))
    RB = [Res("bank%d" % i) for i in range(8)]
    ident_bf = sb(top, "ident_bf", [128, 128], BF16)
    ident_f = sb(top, "ident_f", [128, 128], F32)
    iota_f = sb(top, "iota_f", [128, 128], F32)
    dif = sb(top, "dif", [128, 128], F32)
    maskAB = sb(top, "maskAB", [128, 2, 128], BF16)
    pcol = sb(top, "pcol", [128, 1], F32)
    bz = sb(top, "bz", [128, 1], F32)
    epsc = sb(top, "epsc", [128, 1], F32)
    blo = sb(top, "blo", [128, 1], F32)
    bhi = sb(top, "bhi", [128, 1], F32)
    ones65 = sb(top, "ones65", [65, 64], F32)
    onesM = sb(top, "onesM", [128, 128], BF16)
    dummy_sb = sb(top, "dummy_sb", [1, 8], F32)
    RC = Res("consts")

    S.op("pool", lambda e: e.iota(dif[:], pattern=[[1, 128]], base=0, channel_multiplier=-1,
                                  allow_small_or_imprecise_dtypes=True), writes=[RC])
    S.op("pool", lambda e: e.iota(iota_f[:], pattern=[[1, 128]], base=0, channel_multiplier=0,
                                  allow_small_or_imprecise_dtypes=True), writes=[RC])
    S.op("pool", lambda e: e.iota(pcol[:], pattern=[[0, 1]], base=0, channel_multiplier=1,
                                  allow_small_or_imprecise_dtypes=True), writes=[RC])
    S.op("dve", lambda e: e.tensor_single_scalar(ident_bf[:], dif[:], 0.0, op=ALU.is_equal), reads=[RC], writes=[RC])
    S.op("dve", lambda e: e.tensor_single_scalar(ident_f[:], dif[:], 0.0, op=ALU.is_equal), reads=[RC], writes=[RC])
    S.op("dve", lambda e: e.tensor_scalar(maskAB[:, 0, :], dif[:], 0.0, NEG, op0=ALU.is_gt, op1=ALU.mult), reads=[RC], writes=[RC])
    S.op("dve", lambda e: e.tensor_scalar(maskAB[:, 1, :], dif[:], 0.0, NEG, op0=ALU.is_lt, op1=ALU.mult), reads=[RC], writes=[RC])
    S.op("dve", lambda e: e.tensor_scalar(blo[:], pcol[:], 64.0, NEG, op0=ALU.is_lt, op1=ALU.mult), reads=[RC], writes=[RC])
    S.op("dve", lambda e: e.tensor_scalar(bhi[:], pcol[:], 64.0, NEG, op0=ALU.is_ge, op1=ALU.mult), reads=[RC], writes=[RC])
    S.op("dve", lambda e: e.memset(bz[:], 0.0), writes=[RC])
    S.op("dve", lambda e: e.memset(epsc[:], EPS), writes=[RC])
    S.op("dve", lambda e: e.memset(ones65[:], 1.0), writes=[RC])
    S.op("dve", lambda e: e.memset(onesM[:], 1.0 / 512.0), writes=[RC])

    def barrier():
        S.barrier(dummy_sb[:], dummy_in)
        S.recycle()

    barrier()

    def mm(out, lhsT, rhs, start, stop, reads, writes):
        S.op("pe", lambda e: e.matmul(out, lhsT=lhsT, rhs=rhs, start=start, stop=stop), reads=reads, writes=writes)

    with ExitStack() as st:
        zt = sb(st, "zt", [128, 6144], BF16)
        RZ = Res("zt")
        S.op("pool", lambda e: e.memset(zt[:], 0.0), writes=[RZ])
        dz = S.new_dsem("zero")
        for (ts_, L, pb) in seqinfo:
            for off in (pb, pb + PAD + L):
                S.dma(kS[:, off:off + PAD].rearrange("(a p) t -> p a t", p=128),
                      zt[:].rearrange("p (a t) -> p a t", a=6), reads=[RZ], dsem=dz, first=False)
                S.dma(vS[off:off + PAD, :].rearrange("(p a) c -> p (a c)", p=128), zt[:], reads=[RZ], dsem=dz, first=False)
                S.dma(uS[:, off:off + PAD].rearrange("(a p) t -> p a t", p=128),
                      zt[:, 0:4096].rearrange("p (a t) -> p a t", a=4), reads=[RZ], dsem=dz, first=False)
    barrier()

    tiles = []
    for (ts_, L, pb) in seqinfo:
        for p0 in range(0, L, 512):
            tiles.append((ts_ + p0, p0, pb + PAD + p0))

    def load_w_bf16(st, name, src, kchunks, ncols, eng_cycle=("act", "dve"), rows=128):
        wt = sb(st, name, [rows, kchunks, ncols], BF16)
        R = Res(name)
        with ExitStack() as st2:
            stg = Rot(S, st2, nc, name + "_stg", [rows, ncols], F32, 2)
            for kc in range(kchunks):
                t, r, d = stg.get()
                S.dma(t[:], src[kc * rows:(kc + 1) * rows, :], writes=[r], dsem=d)
                eng = eng_cycle[kc % len(eng_cycle)]
                if eng == "act":
                    S.op("act", lambda e: e.copy(wt[:, kc, :], t[:]), reads=[r], writes=[R])
                else:
                    S.op(eng, lambda e: e.tensor_copy(wt[:, kc, :], t[:]), reads=[r], writes=[R])
            barrier()
        return wt, R

    def load_small(st, name, src, shape):
        t = sb(st, name, shape, F32)
        R = Res(name)
        S.dma(t[:], src, writes=[R], dsem=S.new_dsem(name))
        return t, R

    def rms_block(xt_ap, Rx, gb, Rg, out_bf, Rout, junk, Rjunk, ssq, Rssq, col):
        S.op("act", lambda e: e.activation(out=junk[:], in_=xt_ap, func=AF.Square, accum_out=ssq[:, col:col + 1]),
             reads=[Rx], writes=[Rjunk, Rssq])
        S.op("act", lambda e: e.activation(out=ssq[:, col:col + 1], in_=ssq[:, col:col + 1], func=AF.Sqrt, bias=epsc[:], scale=1.0 / D),
             reads=[Rssq, RC], writes=[Rssq])
        S.op("dve", lambda e: e.reciprocal(ssq[:, col:col + 1], ssq[:, col:col + 1]), reads=[Rssq], writes=[Rssq])
        S.op("dve", lambda e: e.scalar_tensor_tensor(out_bf, xt_ap, ssq[:, col:col + 1], gb[:], op0=ALU.mult, op1=ALU.mult),
             reads=[Rx, Rssq, Rg], writes=[Rout])

    with ExitStack() as st:
        Wb, RW = load_w_bf16(st, "Wb", w_in, 8, NCOL)
        Wr = sb(st, "Wr", [128, 8, 1536], BF16)
        RWr = Res("Wr")
        S.op("pool", lambda e: e.memset(Wr[:], 0.0), writes=[RWr])
        for kc in range(8):
            wv = Wb[:, kc, 0:1536].rearrange("p (h e) -> p h e", e=64)
            rv = Wr[:, kc, :].rearrange("p (h e) -> p h e", e=64)
            S.op("pool", lambda e: e.tensor_single_scalar(rv[:, :, 0:8], wv[:, :, 8:16], -1.0, op=ALU.mult), reads=[RW], writes=[RWr])
            S.op("pool", lambda e: e.tensor_copy(rv[:, :, 8:16], wv[:, :, 0:8]), reads=[RW], writes=[RWr])
        g1b, Rg1 = load_small(st, "g1b", g1b_in, [128, D])
        bgate, Rbg = load_small(st, "bgate", bgate_in, [128, 16])
        xr = Rot(S, st, nc, "xt", [128, D], F32, 3)
        ropc = Rot(S, st, nc, "ropc", [128, 512], F32, 2)
        rops = Rot(S, st, nc, "rops", [128, 512], F32, 2)
        junk = sb(st, "junk", [128, D], BF16)
        Rjunk = Res("junk")
        ssq = sb(st, "ssq", [128, 8], F32)
        Rssq = [Res("ssq%d" % i) for i in range(8)]
        hbr = Rot(S, st, nc, "hb", [128, D], BF16, 2)
        hTs = [sb(st, "hT%d" % i, [128, 8, 512], BF16) for i in range(2)]
        RhT = [Res("hT%d" % i) for i in range(2)]
        t1 = Rot(S, st, nc, "t1", [128, 512], F32, 2)
        t2 = Rot(S, st, nc, "t2", [128, 512], F32, 2)
        qko = Rot(S, st, nc, "qko", [128, 512], BF16, 3)
        sig = Rot(S, st, nc, "sig", [128, 512], F32, 2)
        uo = Rot(S, st, nc, "uo", [128, 512], BF16, 2)
        go = Rot(S, st, nc, "go", [128, 512], BF16, 3)
        vo = Rot(S, st, nc, "vo", [128, 768], BF16, 2)

        xq = {}
        blocks = [(ti, a) for ti in range(len(tiles)) for a in range(4)]
        nxt = [0]

        def p1_prefetch(upto):
            while nxt[0] <= upto and nxt[0] < len(blocks):
                ti, a = blocks[nxt[0]]
                t0 = tiles[ti][0]
                it = xr.get()
                S.dma(it[0][:], x_in[t0 + a * 128:t0 + (a + 1) * 128, :], writes=[it[1]], dsem=it[2])
                xq[nxt[0]] = it
                nxt[0] += 1

        pbank = [0]

        def nbank():
            b = pbank[0] % 4
            pbank[0] += 1
            return b

        for ti, (t0, p0, pt) in enumerate(tiles):
            hT = hTs[ti % 2]
            RH = RhT[ti % 2]
            rc = ropc.get()
            rs = rops.get()
            S.dma(rc[0][:], ropec_in[:, p0:p0 + 512], writes=[rc[1]], dsem=rc[2])
            S.dma(rs[0][:], ropes_in[:, p0:p0 + 512], writes=[rs[1]], dsem=rs[2])
            for a in range(4):
                bi = ti * 4 + a
                p1_prefetch(bi + 2)
                xt_, Rx, _ = xq.pop(bi)
                hb, Rhb, _ = hbr.get()
                col = bi % 8
                rms_block(xt_[:], Rx, g1b, Rg1, hb[:], Rhb, junk, Rjunk, ssq, Rssq[col], col)
                bank = 6 + (bi % 2)
                ptb = PS[:, bank, :].bitcast(BF16)
                for kc in range(8):
                    S.op("pe", lambda e: e.transpose(ptb[:, kc * 128:(kc + 1) * 128], hb[:, kc * 128:(kc + 1) * 128], ident_bf[:]),
                         reads=[Rhb], writes=[RB[bank]])
                S.op("act", lambda e: e.copy(hT[:, :, a * 128:(a + 1) * 128], ptb.rearrange("p (k t) -> p k t", k=8)),
                     reads=[RB[bank]], writes=[RH])
            for cc in range(12):
                bm = nbank()
                br = nbank()
                for kc in range(8):
                    mm(PS[:, bm, :], Wb[:, kc, cc * 128:(cc + 1) * 128], hT[:, kc, :], kc == 0, kc == 7, [RW, RH], [RB[bm]])
                for kc in range(8):
                    mm(PS[:, br, :], Wr[:, kc, cc * 128:(cc + 1) * 128], hT[:, kc, :], kc == 0, kc == 7, [RWr, RH], [RB[br]])
                a1 = t1.get()
                a2 = t2.get()
                o = qko.get()
                S.op("dve", lambda e: e.tensor_tensor(a1[0][:], PS[:, bm, :], rc[0][:], op=ALU.mult), reads=[RB[bm], rc[1]], writes=[a1[1]])
                S.op("dve", lambda e: e.tensor_tensor(a2[0][:], PS[:, br, :], rs[0][:], op=ALU.mult), reads=[RB[br], rs[1]], writes=[a2[1]])
                S.op("pool", lambda e: e.tensor_tensor(o[0][:], a1[0][:], a2[0][:], op=ALU.add), reads=[a1[1], a2[1]], writes=[o[1]])
                if cc < 6:
                    S.dma(qS[cc * 128:(cc + 1) * 128, t0:t0 + 512], o[0][:], reads=[o[1]], dsem=o[2])
                else:
                    S.dma(kS[(cc - 6) * 128:(cc - 5) * 128, pt:pt + 512], o[0][:], reads=[o[1]], dsem=o[2])
            for c in range(4):
                ba = nbank()
                bb = nbank()
                for kc in range(8):
                    mm(PS[:, ba, :], Wb[:, kc, 2304 + c * 128:2304 + (c + 1) * 128], hT[:, kc, :], kc == 0, kc == 7, [RW, RH], [RB[ba]])
                for kc in range(8):
                    mm(PS[:, bb, :], Wb[:, kc, 2816 + c * 128:2816 + (c + 1) * 128], hT[:, kc, :], kc == 0, kc == 7, [RW, RH], [RB[bb]])
                sg = sig.get()
                o = uo.get()
                S.op("act", lambda e: e.activation(out=sg[0][:], in_=PS[:, bb, :], func=AF.Sigmoid), reads=[RB[bb]], writes=[sg[1]])
                S.op("dve", lambda e: e.tensor_tensor(o[0][:], PS[:, ba, :], sg[0][:], op=ALU.mult), reads=[RB[ba], sg[1]], writes=[o[1]])
                S.dma(uS[c * 128:(c + 1) * 128, pt:pt + 512], o[0][:], reads=[o[1]], dsem=o[2])
            for gc in range(16):
                bg = nbank()
                for kc in range(8):
                    mm(PS[:, bg, :], Wb[:, kc, 3328 + gc * 128:3328 + (gc + 1) * 128], hT[:, kc, :], kc == 0, kc == 7, [RW, RH], [RB[bg]])
                o = go.get()
                S.op("act", lambda e: e.activation(out=o[0][:], in_=PS[:, bg, :], func=AF.Sigmoid, bias=bgate[:, gc:gc + 1]),
                     reads=[RB[bg], Rbg], writes=[o[1]])
                S.dma(gS[gc * 128:(gc + 1) * 128, t0:t0 + 512], o[0][:], reads=[o[1]], dsem=o[2])
            for a in range(4):
                for kc in range(8):
                    mm(PS[:, 4, :], hT[:, kc, a * 128:(a + 1) * 128], Wb[:, kc, 1536:2048], kc == 0, kc == 7, [RW, RH], [RB[4]])
                for kc in range(8):
                    mm(PS[:, 5, 0:256], hT[:, kc, a * 128:(a + 1) * 128], Wb[:, kc, 2048:2304], kc == 0, kc == 7, [RW, RH], [RB[5]])
                o = vo.get()
                S.op("act", lambda e: e.copy(o[0][:, 0:512], PS[:, 4, :]), reads=[RB[4]], writes=[o[1]])
                S.op("dve", lambda e: e.tensor_copy(o[0][:, 512:768], PS[:, 5, 0:256]), reads=[RB[5]], writes=[o[1]])
                S.dma(vS[pt + a * 128:pt + (a + 1) * 128, :], o[0][:], reads=[o[1]], dsem=o[2])
    barrier()

    with ExitStack() as st:
        acc = sb(st, "acc", [65, 4, 2048], F32)
        Racc = [Res("acc%d" % i) for i in range(4)]
        Vaug = [sb(st, "Vaug%d" % i, [128, 32, 4, 65], BF16) for i in range(2)]
        RV = [Res("Vaug%d" % i) for i in range(2)]
        dV = [S.new_dsem("Vaug") for _ in range(2)]
        for i in range(2):
            S.op("pool", lambda e: e.memset(Vaug[i][:], 1.0), writes=[RV[i]])
        Qr = Rot(S, st, nc, "Qt", [64, 2048], BF16, 3)
        Kr = Rot(S, st, nc, "Kt", [64, 4096], BF16, 3)
        Pr = Rot(S, st, nc, "Pt", [128, 2, 128], BF16, 4)
        rec = Rot(S, st, nc, "rec", [64, 512], F32, 2)
        ao = sb(st, "ao", [64, 4, 2048], BF16)
        Rao = Res("ao")
        dao = S.new_dsem("ao")

        jobs = []
        for (ts_, L, pb) in seqinfo:
            for s0 in range(0, L, 2048):
                for g in range(3):
                    for hs in range(4):
                        jobs.append(dict(t0g=ts_ + s0, po=pb + PAD + s0, first=(s0 == 0), last=(s0 + 2048 == L), g=g, hs=hs))
        vcount = [0]
        vbuf_of = {}

        def load_v(jb):
            key = (jb["t0g"], jb["g"])
            if key in vbuf_of:
                return
            g = jb["g"]
            d = DIL[g]
            nq = 16 // d
            gb = vcount[0] % 2
            vcount[0] += 1
            vbuf_of[key] = gb
            fst = True
            for r in range(d):
                for j in range(nq + 1):
                    tok = jb["po"] + (128 * j - 64) * d + r
                    S.dma(Vaug[gb][:, r * (nq + 1) + j, :, 0:64],
                          vS[tok:tok + 127 * d + 1:d, g * 256:(g + 1) * 256].rearrange("p (h e) -> p h e", h=4),
                          writes=[RV[gb]], dsem=dV[gb], first=fst)
                    fst = False

        def load_qk(jb):
            g = jb["g"]
            hs = jb["hs"]
            d = DIL[g]
            halo = 64 * d
            hr = (g * 4 + hs) * 64
            Qt, RQ, dQ = Qr.get()
            Kt, RK, dK = Kr.get()
            S.dma(Qt[:], qS[hr:hr + 64, jb["t0g"]:jb["t0g"] + 2048], writes=[RQ], dsem=dQ)
            S.dma(Kt[:, 0:2048 + 2 * halo], kS[hr:hr + 64, jb["po"] - halo:jb["po"] + 2048 + halo], writes=[RK], dsem=dK)
            jb["qk"] = (Qt, RQ, Kt, RK)

        units = []
        for ji, jb in enumerate(jobs):
            d = DIL[jb["g"]]
            nq = 16 // d
            for r in range(d):
                for qb in range(nq):
                    units.append((ji, r, qb))
        loaded_upto = [-1]

        def ensure_job(ji):
            while loaded_upto[0] < ji and loaded_upto[0] + 1 < len(jobs):
                loaded_upto[0] += 1
                jb = jobs[loaded_upto[0]]
                load_v(jb)
                load_qk(jb)

        ust_ = {}

        def stage_S(u):
            ji, r, qb = units[u]
            jb = jobs[ji]
            ensure_job(ji)
            d = DIL[jb["g"]]
            Qt, RQ, Kt, RK = jb["qk"]
            c0 = 128 * qb * d + r
            qv = Qt[:, c0:c0 + 127 * d + 1:d]
            kA = Kt[:, c0:c0 + 127 * d + 1:d]
            kB = Kt[:, c0 + 128 * d:c0 + 255 * d + 1:d]
            sbk = u % 4
            pst = PS[:, sbk, 0:256].rearrange("p (a q) -> p a q", a=2)
            mm(pst[:, 0, :], kA, qv, True, False, [RK, RQ], [RB[sbk]])
            mm(pst[:, 0, :], ident_bf[:], maskAB[:, 0, :], False, True, [RC], [RB[sbk]])
            mm(pst[:, 1, :], kB, qv, True, False, [RK, RQ], [RB[sbk]])
            mm(pst[:, 1, :], ident_bf[:], maskAB[:, 1, :], False, True, [RC], [RB[sbk]])
            ust_[u] = (sbk, pst, c0)

        def stage_rest(u):
            ji, r, qb = units[u]
            jb = jobs[ji]
            g = jb["g"]
            hs = jb["hs"]
            d = DIL[g]
            nq = 16 // d
            gb = vbuf_of[(jb["t0g"], g)]
            sbk, pst, c0 = ust_.pop(u)
            bA = blo if (jb["first"] and qb == 0) else bz
            bB = bhi if (jb["last"] and qb == nq - 1) else bz
            Pt, RP, _ = Pr.get()
            if bA is bB:
                S.op("act", lambda e: e.activation(out=Pt[:], in_=pst, func=AF.Exp, bias=bz[:], scale=0.125),
                     reads=[RB[sbk], RC], writes=[RP])
            else:
                S.op("act", lambda e: e.activation(out=Pt[:, 0, :], in_=pst[:, 0, :], func=AF.Exp, bias=bA[:], scale=0.125),
                     reads=[RB[sbk], RC], writes=[RP])
                S.op("act", lambda e: e.activation(out=Pt[:, 1, :], in_=pst[:, 1, :], func=AF.Exp, bias=bB[:], scale=0.125),
                     reads=[RB[sbk], RC], writes=[RP])
            return (Pt, RP, gb, g, hs, d, nq, r, qb, c0)

        def stage_PV(u, ctx):
            Pt, RP, gb, g, hs, d, nq, r, qb, c0 = ctx
            obk = 4 + (u % 4)
            pov = PS[0:65, obk, 0:128]
            blk = r * (nq + 1) + qb
            mm(pov, Vaug[gb][:, blk, hs, :], Pt[:, 0, :], True, False, [RV[gb], RP], [RB[obk]])
            mm(pov, Vaug[gb][:, blk + 1, hs, :], Pt[:, 1, :], False, True, [RV[gb], RP], [RB[obk]])
            av = acc[:, hs, c0:c0 + 127 * d + 1:d]
            if g == 0:
                S.op("dve", lambda e: e.tensor_copy(av, pov), reads=[RB[obk]], writes=[Racc[hs]], nosame=True)
            else:
                S.op("dve", lambda e: e.tensor_tensor(av, av, pov, op=ALU.add), reads=[RB[obk], Racc[hs]], writes=[Racc[hs]], nosame=True)

        def normalise(jb):
            t0g = jb["t0g"]
            for hs in range(4):
                for ch in range(4):
                    bk = ch % 4
                    mm(PS[0:64, bk, :], ones65[64:65, :], acc[64:65, hs, ch * 512:(ch + 1) * 512], True, True, [Racc[hs], RC], [RB[bk]])
                    rc_ = rec.get()
                    S.op("dve", lambda e: e.reciprocal(rc_[0][:], PS[0:64, bk, :]), reads=[RB[bk]], writes=[rc_[1]])
                    S.op("pool", lambda e: e.tensor_tensor(ao[:, hs, ch * 512:(ch + 1) * 512], acc[0:64, hs, ch * 512:(ch + 1) * 512],
                                                          rc_[0][:], op=ALU.mult), reads=[Racc[hs], rc_[1]], writes=[Rao])
            S.dma(aS[:, t0g:t0g + 2048].rearrange("(h e) t -> e h t", h=4), ao[:], reads=[Rao], dsem=dao)

        NU = len(units)
        ensure_job(0)
        LOOK = 2
        UPT = 192
        for tile0 in range(0, NU, UPT):
            tend = tile0 + UPT
            for u in range(tile0, min(tile0 + LOOK, tend)):
                stage_S(u)
            for u in range(tile0, tend):
                ctx = stage_rest(u)
                if u + LOOK < tend:
                    stage_S(u + LOOK)
                stage_PV(u, ctx)
                ensure_job(min(units[u][0] + 1, len(jobs) - 1))
            normalise(jobs[units[tile0][0]])
    barrier()

    with ExitStack() as st:
        pww, Rpww = load_w_bf16(st, "pww", pww_in, 4, D)
        wup, Rwup = load_w_bf16(st, "wup", wup_in, 4, D, rows=64)
        wout, Rwout = load_w_bf16(st, "wout", wout_in, 8, D)
        dww, Rdww = load_small(st, "dww", dww_in, [128, 4, 31])
        cpar, Rcp = load_small(st, "cpar", cpar_in, [128, 3, 4])
        pwb, Rpwb = load_small(st, "pwb", pwb_in, [128, 8])
        Dg = sb(st, "Dg", [128, 124, 128], BF16)
        RDg = Res("Dg")
        for c in range(4):
            for k in range(31):
                eng = "dve" if (k % 2 == 0) else "pool"
                S.op(eng, lambda e: e.tensor_scalar(Dg[:, c * 31 + k, :], ident_f[:], dww[:, c, k:k + 1], None, op0=ALU.mult),
                     reads=[Rdww, RC], writes=[RDg])
        Ur = Rot(S, st, nc, "U", [128, 4, 542], BF16, 2)
        Ar = Rot(S, st, nc, "At", [64, 4, 512], BF16, 2)
        Gr = Rot(S, st, nc, "G", [128, 2, 512], BF16, 3)
        xr = Rot(S, st, nc, "x3", [128, D], F32, 2)
        uc = sb(st, "uc", [128, 4, 512], BF16)
        sq = sb(st, "sq", [128, 4, 512], BF16)
        Ruc = Res("uc")
        Rsq = Res("sq")
        msq = sb(st, "msq", [128, 512], F32)
        rstd = sb(st, "rstd", [128, 512], F32)
        nmr = sb(st, "nmr", [128, 512], F32)
        Rst = Res("stats")
        tt = Rot(S, st, nc, "tt", [128, 512], F32, 2)
        sl = sb(st, "sl", [128, 4, 512], BF16)
        Rsl = Res("sl")
        m1 = Rot(S, st, nc, "m1", [128, 512], F32, 2)
        m2 = Rot(S, st, nc, "m2", [128, 512], F32, 2)
        mx = sb(st, "mx", [128, 8, 512], BF16)
        Rmx = Res("mx")
        x2r = Rot(S, st, nc, "x2o", [128, D], F32, 2)

        for ti, (t0, p0, pt) in enumerate(tiles):
            U, RU, dU = Ur.get()
            S.dma(U[:], uS[:, pt - 15:pt + 527].rearrange("(c p) t -> p c t", p=128), writes=[RU], dsem=dU)
            At, RA, dA = Ar.get()
            S.dma(At[:], aS[:, t0:t0 + 512].rearrange("(h e) t -> e h t", h=4), writes=[RA], dsem=dA)
            for c in range(4):
                bk = nbank()
                for k in range(31):
                    mm(PS[:, bk, :], Dg[:, c * 31 + k, :], U[:, c, k:k + 512], k == 0, k == 30, [RDg, RU], [RB[bk]])
                S.op("act", lambda e: e.activation(out=uc[:, c, :], in_=PS[:, bk, :], func=AF.Identity, bias=cpar[:, 0, c:c + 1]),
                     reads=[RB[bk], Rcp], writes=[Ruc])
                S.op("act", lambda e: e.activation(out=sq[:, c, :], in_=PS[:, bk, :], func=AF.Square, bias=cpar[:, 0, c:c + 1]),
                     reads=[RB[bk], Rcp], writes=[Rsq])
            bme = nbank()
            bex = nbank()
            for c in range(4):
                mm(PS[:, bme, :], onesM[:], uc[:, c, :], c == 0, c == 3, [RC, Ruc], [RB[bme]])
            for c in range(4):
                mm(PS[:, bex, :], onesM[:], sq[:, c, :], c == 0, c == 3, [RC, Rsq], [RB[bex]])
            S.op("act", lambda e: e.activation(out=msq[:], in_=PS[:, bme, :], func=AF.Square), reads=[RB[bme]], writes=[Rst])
            S.op("dve", lambda e: e.tensor_tensor(rstd[:], PS[:, bex, :], msq[:], op=ALU.subtract), reads=[RB[bex], Rst], writes=[Rst])
            S.op("act", lambda e: e.activation(out=rstd[:], in_=rstd[:], func=AF.Sqrt, bias=epsc[:]), reads=[Rst, RC], writes=[Rst])
            S.op("dve", lambda e: e.reciprocal(rstd[:], rstd[:]), reads=[Rst], writes=[Rst])
            S.op("dve", lambda e: e.tensor_tensor(nmr[:], PS[:, bme, :], rstd[:], op=ALU.mult), reads=[RB[bme], Rst], writes=[Rst])
            for c in range(4):
                t_, Rt_, _ = tt.get()
                S.op("dve", lambda e: e.tensor_tensor(t_[:], uc[:, c, :], rstd[:], op=ALU.mult), reads=[Ruc, Rst], writes=[Rt_])
                S.op("pool", lambda e: e.tensor_tensor(t_[:], t_[:], nmr[:], op=ALU.subtract), reads=[Rt_, Rst], writes=[Rt_])
                S.op("act", lambda e: e.activation(out=sl[:, c, :], in_=t_[:], func=AF.Silu, bias=cpar[:, 2, c:c + 1], scale=cpar[:, 1, c:c + 1]),
                     reads=[Rt_, Rcp], writes=[Rsl])
            for dc in range(8):
                G, RG, dG = Gr.get()
                S.dma(G[:], gS[:, t0:t0 + 512].rearrange("(b c p) t -> p b c t", b=2, p=128)[:, :, dc, :], writes=[RG], dsem=dG)
                bcv = nbank()
                bat = nbank()
                for c in range(4):
                    mm(PS[:, bcv, :], pww[:, c, dc * 128:(dc + 1) * 128], sl[:, c, :], c == 0, c == 3, [Rpww, Rsl], [RB[bcv]])
                for hs in range(4):
                    mm(PS[:, bat, :], wup[:, hs, dc * 128:(dc + 1) * 128], At[:, hs, :], hs == 0, hs == 3, [Rwup, RA], [RB[bat]])
                a1 = m1.get()
                a2 = m2.get()
                S.op("dve", lambda e: e.scalar_tensor_tensor(a1[0][:], PS[:, bcv, :], pwb[:, dc:dc + 1], G[:, 1, :], op0=ALU.add, op1=ALU.mult),
                     reads=[RB[bcv], Rpwb, RG], writes=[a1[1]])
                S.op("dve", lambda e: e.tensor_tensor(a2[0][:], PS[:, bat, :], G[:, 0, :], op=ALU.mult), reads=[RB[bat], RG], writes=[a2[1]])
                S.op("pool", lambda e: e.tensor_tensor(mx[:, dc, :], a1[0][:], a2[0][:], op=ALU.add), reads=[a1[1], a2[1]], writes=[Rmx])
            for a in range(4):
                xt_, Rx, dX = xr.get()
                S.dma(xt_[:], x_in[t0 + a * 128:t0 + (a + 1) * 128, :], writes=[Rx], dsem=dX)
                b0 = 4 + 2 * (a % 2)
                for n in range(2):
                    for kc in range(8):
                        mm(PS[:, b0 + n, :], mx[:, kc, a * 128:(a + 1) * 128], wout[:, kc, n * 512:(n + 1) * 512], kc == 0, kc == 7,
                           [Rmx, Rwout], [RB[b0 + n]])
                o, Ro, dO = x2r.get()
                S.op("dve", lambda e: e.tensor_tensor(o[:].rearrange("p (n f) -> p n f", n=2), PS[:, b0:b0 + 2, :],
                                                     xt_[:].rearrange("p (n f) -> p n f", n=2), op=ALU.add),
                     reads=[RB[b0], RB[b0 + 1], Rx], writes=[Ro])
                S.dma(x2S[t0 + a * 128:t0 + (a + 1) * 128, :], o[:], reads=[Ro], dsem=dO)
    barrier()

    with ExitStack() as st:
        wq, Rwq = load_w_bf16(st, "wq", wq_in, 8, 2048)
        keysT3, RkT = load_w_bf16(st, "keysT", keysT_in.rearrange("p a b -> p (a b)"), 1, 2048)
        keysT = keysT3[:, 0, :].rearrange("p (a b) -> p a b", a=16)
        g2b, Rg2 = load_small(st, "g2b", g2b_in, [128, D])
        xr = Rot(S, st, nc, "x2i", [128, D], F32, 2)
        junk = sb(st, "junk2", [128, D], BF16)
        Rjunk = Res("junk2")
        ssq = sb(st, "ssq2", [128, 8], F32)
        Rssq = [Res("ssq2_%d" % i) for i in range(8)]
        hbr = Rot(S, st, nc, "h2b", [128, D], BF16, 2)
        hTr = Rot(S, st, nc, "h2T", [128, 8, 512], BF16, 2)
        qpTs = [sb(st, "qpT%d" % i, [128, 16, 512], BF16) for i in range(2)]
        RqpTs = [Res("qpT%d" % i) for i in range(2)]
        class BS:
            pass

        BSETS = []
        for bi_ in range(2):
            B = BS()
            B.Ssb = sb(st, "Ssb%d" % bi_, [128, 16, 128], F32)
            B.RS_ = Res("Ssb")
            B.wk = sb(st, "wk%d" % bi_, [128, 16, 128], F32)
            B.Rwk = Res("wk")
            B.tops = sb(st, "tops%d" % bi_, [128, 16, 16], F32)
            B.Rtops = Res("tops")
            B.tidx = sb(st, "tidx%d" % bi_, [128, 16, 16], U32)
            B.Rtidx = Res("tidx")
            B.tif = sb(st, "tif%d" % bi_, [128, 16, 16], F32)
            B.Rtif = Res("tif")
            B.best = sb(st, "best%d" % bi_, [128, 8, 16], F32)
            B.Rbest = Res("best")
            B.bidx = sb(st, "bidx%d" % bi_, [128, 8, 16], U32)
            B.Rbidx = Res("bidx")
            B.ab_u = sb(st, "ab_u%d" % bi_, [128, 2, 8, 16], U32)
            B.ab_f = sb(st, "ab_f%d" % bi_, [128, 2, 8, 16], F32)
            B.Rab = Res("ab")
            B.ge = sb(st, "ge%d" % bi_, [128, 8, 16], F32)
            B.gs = sb(st, "gs%d" % bi_, [128, 8], F32)
            B.Rge = Res("ge")
            B.E0 = sb(st, "E0%d" % bi_, [128, 8, 16, 16], F32)
            B.RE0 = Res("E0")
            B.E1 = sb(st, "E1%d" % bi_, [128, 8, 16, 16], F32)
            B.RE1 = Res("E1")
            B.Rt = sb(st, "Rt%d" % bi_, [128, 3, 128], F32)
            B.RRt = Res("Rt")
            BSETS.append(B)
        RTr = Rot(S, st, nc, "RT", [128, 3, 128], F32, 2)

        ust = Rot(S, st, nc, "ust", [128, D], F32, 1)
        vst = Rot(S, st, nc, "vst", [128, D], F32, 1)
        ubf = Rot(S, st, nc, "ubf", [128, D], BF16, 1)
        uts = Rot(S, st, nc, "uts", [128, D], BF16, 2)
        vbf = Rot(S, st, nc, "vbf", [128, D], BF16, 2)
        NI = NEXP // 128
        tp_loaded = {}
        tp_next = [0]
        tp_done = [0]

        def tp_load():
            i = tp_next[0]
            if i >= NI:
                return
            a = ust.get()
            b = vst.get()
            S.dma(a[0][:], pu_in[i * 128:(i + 1) * 128, :], writes=[a[1]], dsem=a[2])
            S.dma(b[0][:], pv_in[i * 128:(i + 1) * 128, :], writes=[b[1]], dsem=b[2])
            tp_loaded[i] = (a, b)
            tp_next[0] += 1

        def tp_step():
            i = tp_done[0]
            if i >= NI:
                return
            if i not in tp_loaded:
                tp_load()
            a, b = tp_loaded.pop(i)
            ub = ubf.get()
            S.op("act", lambda e: e.copy(ub[0][:], a[0][:]), reads=[a[1]], writes=[ub[1]])
            ptb = PS[:, 7, :].bitcast(BF16)
            for kc in range(8):
                S.op("pe", lambda e: e.transpose(ptb[:, kc * 128:(kc + 1) * 128], ub[0][:, kc * 128:(kc + 1) * 128], ident_bf[:]),
                     reads=[ub[1]], writes=[RB[7]])
            us_ = uts.get()
            S.op("act", lambda e: e.copy(us_[0][:], ptb), reads=[RB[7]], writes=[us_[1]])
            S.dma(UT[i], us_[0][:], reads=[us_[1]], dsem=us_[2])
            vb_ = vbf.get()
            S.op("pool", lambda e: e.tensor_copy(vb_[0][:], b[0][:]), reads=[b[1]], writes=[vb_[1]])
            S.dma(VB[i * 128:(i + 1) * 128, :], vb_[0][:], reads=[vb_[1]], dsem=vb_[2])
            tp_done[0] += 1
            tp_load()

        def routing(a, B, t0, qpT, RqpT):
            Ssb, RS_, wk, Rwk = B.Ssb, B.RS_, B.wk, B.Rwk
            tops, Rtops, tidx, Rtidx, tif, Rtif = B.tops, B.Rtops, B.tidx, B.Rtidx, B.tif, B.Rtif
            best, Rbest, bidx, Rbidx = B.best, B.Rbest, B.bidx, B.Rbidx
            ab_u, ab_f, Rab, ge, gs, Rge = B.ab_u, B.ab_f, B.Rab, B.ge, B.gs, B.Rge
            Rt, RRt = B.Rt, B.RRt
            cand = wk[:].rearrange("p (h c) k -> p h (c k)", c=2)
            Rcand = Rwk
            wk2 = Ssb[:].rearrange("p (h c) k -> p h (c k)", c=2)
            Rwk2 = RS_
            for hc in range(16):
                b = hc // 4
                mm(PS[:, b, (hc % 4) * 128:(hc % 4 + 1) * 128], qpT[:, hc, a * 128:(a + 1) * 128], keysT[:, hc, :], True, True,
                   [RqpT, RkT], [RB[b]])
            S.op("act", lambda e: e.copy(Ssb[:].rearrange("p (b q) k -> p b (q k)", b=4), PS[:, 0:4, :]),
                 reads=[RB[0], RB[1], RB[2], RB[3]], writes=[RS_])
            yield
            for hc in range(16):
                S.op("dve", lambda e: e.max(out=tops[:, hc, 0:8], in_=Ssb[:, hc, :]), reads=[RS_], writes=[Rtops])
            yield
            for hc in range(16):
                S.op("dve", lambda e: e.max_index(out=tidx[:, hc, 0:8], in_max=tops[:, hc, 0:8], in_values=Ssb[:, hc, :]),
                     reads=[RS_, Rtops], writes=[Rtidx])
            for hc in range(16):
                S.op("dve", lambda e: e.match_replace(out=wk[:, hc, :], in_to_replace=tops[:, hc, 0:8], in_values=Ssb[:, hc, :], imm_value=-1e30),
                     reads=[RS_, Rtops], writes=[Rwk])
            yield
            for hc in range(16):
                S.op("dve", lambda e: e.max(out=tops[:, hc, 8:16], in_=wk[:, hc, :]), reads=[Rwk], writes=[Rtops])
            yield
            for hc in range(16):
                S.op("dve", lambda e: e.max_index(out=tidx[:, hc, 8:16], in_max=tops[:, hc, 8:16], in_values=wk[:, hc, :]),
                     reads=[Rwk, Rtops], writes=[Rtidx])
            yield
            S.op("dve", lambda e: e.tensor_copy(tif[:], tidx[:]), reads=[Rtidx], writes=[Rtif])
            tops4 = tops[:].rearrange("p (h c) k -> p h c k", c=2)
            tif4 = tif[:].rearrange("p (h c) k -> p h c k", c=2)
            S.op("dve", lambda e: e.tensor_tensor(cand.rearrange("p h (a b) -> p h a b", a=16),
                                                 tops4[:, :, 0, :].unsqueeze(3).broadcast_to([128, 8, 16, 16]),
                                                 tops4[:, :, 1, :].unsqueeze(2).broadcast_to([128, 8, 16, 16]), op=ALU.add),
                 reads=[Rtops], writes=[Rcand])
            yield
            for h in range(8):
                S.op("dve", lambda e: e.max(out=best[:, h, 0:8], in_=cand[:, h, :]), reads=[Rcand], writes=[Rbest])
            yield
            for h in range(8):
                S.op("dve", lambda e: e.max_index(out=bidx[:, h, 0:8], in_max=best[:, h, 0:8], in_values=cand[:, h, :]),
                     reads=[Rcand, Rbest], writes=[Rbidx])
            for h in range(8):
                S.op("dve", lambda e: e.match_replace(out=wk2[:, h, :], in_to_replace=best[:, h, 0:8], in_values=cand[:, h, :], imm_value=-1e30),
                     reads=[Rcand, Rbest], writes=[Rwk2])
            yield
            for h in range(8):
                S.op("dve", lambda e: e.max(out=best[:, h, 8:16], in_=wk2[:, h, :]), reads=[Rwk2], writes=[Rbest])
            yield
            for h in range(8):
                S.op("dve", lambda e: e.max_index(out=bidx[:, h, 8:16], in_max=best[:, h, 8:16], in_values=wk2[:, h, :]),
                     reads=[Rwk2, Rbest], writes=[Rbidx])
            S.op("dve", lambda e: e.tensor_tensor(ge[:], best[:], best[:, :, 0:1].broadcast_to([128, 8, 16]), op=ALU.subtract),
                 reads=[Rbest], writes=[Rge])
            yield
            S.op("act", lambda e: e.activation(out=ge[:], in_=ge[:], func=AF.Exp), reads=[Rge], writes=[Rge])
            S.op("dve", lambda e: e.tensor_single_scalar(ab_u[:, 0, :, :], bidx[:], 4, op=ALU.logical_shift_right), reads=[Rbidx], writes=[Rab])
            S.op("dve", lambda e: e.tensor_single_scalar(ab_u[:, 1, :, :], bidx[:], 15, op=ALU.bitwise_and), reads=[Rbidx], writes=[Rab])
            yield
            S.op("dve", lambda e: e.tensor_copy(ab_f[:], ab_u[:]), reads=[Rab], writes=[Rab])
            S.op("dve", lambda e: e.reduce_sum(gs[:], ge[:], axis=AX.X), reads=[Rge], writes=[Rge])
            yield
            S.op("dve", lambda e: e.reciprocal(gs[:], gs[:]), reads=[Rge], writes=[Rge])
            yield
            S.op("dve", lambda e: e.tensor_tensor(Rt[:, 2, :].rearrange("p (h k) -> p h k", h=8), ge[:],
                                                 gs[:].unsqueeze(2).broadcast_to([128, 8, 16]), op=ALU.mult),
                 reads=[Rge], writes=[RRt])
            io16 = iota_f[:, 0:16].unsqueeze(1).unsqueeze(1).broadcast_to([128, 8, 16, 16])
            for c in range(2):
                EE, REE = (B.E0, B.RE0) if c == 0 else (B.E1, B.RE1)
                S.op("dve", lambda e: e.tensor_tensor(EE[:], io16, ab_f[:, c, :, :].unsqueeze(3).broadcast_to([128, 8, 16, 16]), op=ALU.is_equal),
                     reads=[Rab, RC], writes=[REE])
                S.op("pool", lambda e: e.tensor_tensor(EE[:], EE[:], tif4[:, :, c, :].unsqueeze(2).broadcast_to([128, 8, 16, 16]), op=ALU.mult),
                     reads=[REE, Rtif], writes=[REE])
                yield
                S.op("dve", lambda e: e.reduce_sum(Rt[:, c, :].rearrange("p (h k) -> p h k", h=8), EE[:], axis=AX.X),
                     reads=[REE], writes=[RRt])
                yield
            for c in range(3):
                S.op("pe", lambda e: e.transpose(PS[:, 6, c * 128:(c + 1) * 128], Rt[:, c, :], ident_f[:]), reads=[RRt, RC], writes=[RB[6]])
            RT_, RRT, dRT = RTr.get()
            S.op("act", lambda e: e.copy(RT_[:], PS[:, 6, 0:384].rearrange("p (c t) -> p c t", c=3)), reads=[RB[6]], writes=[RRT])
            tb = t0 + a * 128
            S.dma(rS[:, :, tb:tb + 128].rearrange("c p t -> p c t"), RT_[:], reads=[RRT], dsem=dRT)

        def front(ti):
            t0, p0, pt = tiles[ti]
            qpT = qpTs[ti % 2]
            RqpT = RqpTs[ti % 2]
            hT, RH, dH = hTr.get()
            for a in range(4):
                bi = ti * 4 + a
                xt_, Rx, dX = xr.get()
                S.dma(xt_[:], x2S[t0 + a * 128:t0 + (a + 1) * 128, :], writes=[Rx], dsem=dX)
                hb, Rhb, _ = hbr.get()
                col = bi % 8
                rms_block(xt_[:], Rx, g2b, Rg2, hb[:], Rhb, junk, Rjunk, ssq, Rssq[col], col)
                bank = 6
                ptb = PS[:, bank, :].bitcast(BF16)
                for kc in range(8):
                    S.op("pe", lambda e: e.transpose(ptb[:, kc * 128:(kc + 1) * 128], hb[:, kc * 128:(kc + 1) * 128], ident_bf[:]),
                         reads=[Rhb], writes=[RB[bank]])
                S.op("act", lambda e: e.copy(hT[:, :, a * 128:(a + 1) * 128], ptb.rearrange("p (k t) -> p k t", k=8)),
                     reads=[RB[bank]], writes=[RH])
            S.dma(h2S[:, t0:t0 + 512].rearrange("(k p) t -> p k t", p=128), hT[:], reads=[RH], dsem=dH)
            for hc in range(16):
                bk = 4 + (hc % 2)
                for kc in range(8):
                    mm(PS[:, bk, :], wq[:, kc, hc * 128:(hc + 1) * 128], hT[:, kc, :], kc == 0, kc == 7, [Rwq, RH], [RB[bk]])
                S.op("act", lambda e: e.copy(qpT[:, hc, :], PS[:, bk, :]), reads=[RB[bk]], writes=[RqpT])

        def run_gens(gens):
            alive = [True] * len(gens)
            while any(alive):
                for gi in range(len(gens)):
                    if alive[gi]:
                        try:
                            next(gens[gi])
                        except StopIteration:
                            alive[gi] = False

        front(0)
        NTP_STEPS = (NI + len(tiles) - 1) // len(tiles)
        for ti, (t0, p0, pt) in enumerate(tiles):
            qa, Rqa = qpTs[ti % 2], RqpTs[ti % 2]
            run_gens([routing(0, BSETS[0], t0, qa, Rqa), routing(1, BSETS[1], t0, qa, Rqa)])
            g23 = [routing(2, BSETS[0], t0, qa, Rqa), routing(3, BSETS[1], t0, qa, Rqa)]
            for g_ in g23:
                next(g_)
            if ti + 1 < len(tiles):
                front(ti + 1)
            for _ in range(NTP_STEPS):
                tp_step()
            run_gens(g23)
        while tp_done[0] < NI:
            tp_step()
    barrier()

    TT = 256
    with ExitStack() as st:
        gfb, Rgf = load_small(st, "gfb", gfb_in, [128, D])
        WTs = [sb(st, "WT%d" % i, [128, TT, 128], BF16) for i in range(2)]
        RWTs = [Res("WT%d" % i) for i in range(2)]
        rtr = Rot(S, st, nc, "rt4", [128, 3, TT], F32, 1)
        h2r = Rot(S, st, nc, "h2T4", [128, 8, TT], BF16, 2)
        x2r = Rot(S, st, nc, "x24", [128, D], F32, 1)
        rtr_n = 2
        TB = 8
        Lr = Rot(S, st, nc, "Lb", [128, TB, 128], BF16, 2)
        Rr = Rot(S, st, nc, "Rb", [128, TB, 128], BF16, 2)
        Ubr = Rot(S, st, nc, "Ub", [128, 2, D], BF16, 3)
        Vbr = Rot(S, st, nc, "Vb", [128, 2, D], BF16, 4)
        Asb = Rot(S, st, nc, "Asb", [128, 2, TT], BF16, 2)
        WAr = Rot(S, st, nc, "WA", [128, 2, TT], BF16, 3)
        yo = Rot(S, st, nc, "yo", [128, D], F32, 1)
        ssq = sb(st, "ssq4", [128, 8], F32)
        Rssq = [Res("ssq4_%d" % i) for i in range(8)]
        ntile = NT // TT
        NP = 64
        NTB = TT // TB
        PER = NP // NTB
        pend = {}

        def p4_load(ti):
            tb = ti * TT
            r_ = rtr.get()
            S.dma(r_[0][:], rS[:, :, tb:tb + TT].rearrange("c p t -> p c t"), writes=[r_[1]], dsem=r_[2])
            h_ = h2r.get()
            S.dma(h_[0][:], h2S[:, tb:tb + TT].rearrange("(k p) t -> p k t", p=128), writes=[h_[1]], dsem=h_[2])
            pend[ti] = (r_, h_)

        chunkq = {}
        cnext = [0]
        total = ntile * NP

        def p4_chunk_prefetch(upto):
            while cnext[0] <= upto and cnext[0] < total:
                ip = cnext[0] % NP
                u_ = Ubr.get()
                v_ = Vbr.get()
                S.dma(u_[0][:], UT[2 * ip:2 * ip + 2].rearrange("i p f -> p i f"), writes=[u_[1]], dsem=u_[2])
                S.dma(v_[0][:], VB[2 * ip * 128:(2 * ip + 2) * 128, :].rearrange("(i p) f -> p i f", p=128), writes=[v_[1]], dsem=v_[2])
                chunkq[cnext[0]] = (u_, v_)
                cnext[0] += 1

        iota_b = sb(st, "iota_b", [128, 128], BF16)
        S.op("dve", lambda e: e.tensor_copy(iota_b[:], iota_f[:]), reads=[RC], writes=[RC])
        io = iota_b[:].unsqueeze(1).broadcast_to([128, TB, 128])
        rtb = Rot(S, st, nc, "rtb", [128, 3, TT], BF16, 1)
        rtb_of = {}
        evn = [0]

        def onehot(ti, tb):
            if ti not in rtb_of:
                rt32, Rrt32, _ = pend[ti][0]
                rb_ = rtb.get()
                S.op("dve", lambda e: e.tensor_copy(rb_[0][:], rt32[:]), reads=[Rrt32], writes=[rb_[1]])
                rtb_of.clear()
                rtb_of[ti] = rb_
            rt_, Rrt, _ = rtb_of[ti]
            Lb, RL, _ = Lr.get()
            Rb, RR, _ = Rr.get()
            S.op("dve", lambda e: e.tensor_tensor(Lb[:], io, rt_[:, 0, tb * TB:(tb + 1) * TB].unsqueeze(2).broadcast_to([128, TB, 128]), op=ALU.is_equal),
                 reads=[Rrt, RC], writes=[RL])
            S.op("dve", lambda e: e.tensor_tensor(Lb[:], Lb[:], rt_[:, 2, tb * TB:(tb + 1) * TB].unsqueeze(2).broadcast_to([128, TB, 128]), op=ALU.mult),
                 reads=[Rrt, RL], writes=[RL])
            S.op("dve", lambda e: e.tensor_tensor(Rb[:], io, rt_[:, 1, tb * TB:(tb + 1) * TB].unsqueeze(2).broadcast_to([128, TB, 128]), op=ALU.is_equal),
                 reads=[Rrt, RC], writes=[RR])
            return (Lb, RL, Rb, RR)

        def wmm(ti, tb, LR):
            Lb, RL, Rb, RR = LR
            WT = WTs[ti % 2]
            RWT = RWTs[ti % 2]
            for q4 in range(TB // 4):
                bk = 6 + (evn[0] % 2)
                evn[0] += 1
                pw = PS[:, bk, :].rearrange("p (t i) -> p t i", t=4)
                for tq in range(4):
                    tl = q4 * 4 + tq
                    mm(pw[:, tq, :], Rb[:, tl, :], Lb[:, tl, :], True, True, [RL, RR], [RB[bk]])
                tg = tb * TB + q4 * 4
                S.op("act", lambda e: e.copy(WT[:, tg:tg + 4, :], pw), reads=[RB[bk]], writes=[RWT])

        def u_mm(ci):
            ti, ip = divmod(ci, NP)
            h2T, Rh2, _ = pend[ti][1]
            (Ub, RU, _), _v = chunkq[ci]
            bk = 4 + (ci % 2)
            pa = PS[:, bk, :].rearrange("p (i t) -> p i t", i=2)
            for ii in range(2):
                for kc in range(8):
                    mm(pa[:, ii, :], Ub[:, ii, kc * 128:(kc + 1) * 128], h2T[:, kc, :], kc == 0, kc == 7, [RU, Rh2], [RB[bk]])

        xo = Rot(S, st, nc, "xo4", [128, D], F32, 1)

        def v_mm(ci, WA, RWA):
            ti, ip = divmod(ci, NP)
            _u, (Vb, RVb, _) = chunkq.pop(ci)
            for ii in range(2):
                for th in range(2):
                    for n in range(2):
                        bo = th * 2 + n
                        mm(PS[:, bo, :], WA[:, ii, th * 128:(th + 1) * 128], Vb[:, ii, n * 512:(n + 1) * 512],
                           ip == 0 and ii == 0, ip == NP - 1 and ii == 1, [RWA, RVb], [RB[bo]])

        def finalize(ti):
            for th in range(2):
                tb_ = ti * TT + th * 128
                xo_, Rxo, _ = xo.get()
                S.op("act", lambda e: e.copy(xo_[:].rearrange("p (n f) -> p n f", n=2), PS[:, 2 * th:2 * th + 2, :]),
                     reads=[RB[2 * th], RB[2 * th + 1]], writes=[Rxo])
                o, Ro, dX = x2r.get()
                S.dma(o[:], x2S[tb_:tb_ + 128, :], writes=[Ro], dsem=dX)
                S.op("pool", lambda e: e.tensor_tensor(o[:], o[:], xo_[:], op=ALU.add), reads=[Rxo, Ro], writes=[Ro])
                y_, Ry, dY = yo.get()
                col = (ti * 2 + th) % 8
                S.op("act", lambda e: e.activation(out=y_[:], in_=o[:], func=AF.Square, accum_out=ssq[:, col:col + 1]),
                     reads=[Ro], writes=[Ry, Rssq[col]])
                S.op("act", lambda e: e.activation(out=ssq[:, col:col + 1], in_=ssq[:, col:col + 1], func=AF.Sqrt, bias=epsc[:], scale=1.0 / D),
                     reads=[Rssq[col], RC], writes=[Rssq[col]])
                S.op("dve", lambda e: e.reciprocal(ssq[:, col:col + 1], ssq[:, col:col + 1]), reads=[Rssq[col]], writes=[Rssq[col]])
                S.op("dve", lambda e: e.scalar_tensor_tensor(y_[:], o[:], ssq[:, col:col + 1], gfb[:], op0=ALU.mult, op1=ALU.mult),
                     reads=[Ro, Rssq[col], Rgf], writes=[Ry])
                S.dma(y_out[tb_:tb_ + 128, :], y_[:], reads=[Ry], dsem=dY)

        p4_load(0)
        p4_chunk_prefetch(2)
        for tb in range(NTB):
            wmm(0, tb, onehot(0, tb))
        u_mm(0)
        pendLR = None
        pendV = None
        for ti in range(ntile):
            if ti + 1 < ntile:
                p4_load(ti + 1)
            WT = WTs[ti % 2]
            RWT = RWTs[ti % 2]
            for ip in range(NP):
                ci = ti * NP + ip
                p4_chunk_prefetch(ci + 2)
                if ti + 1 < ntile and ip % PER == 0:
                    if pendLR is not None:
                        wmm(ti + 1, ip // PER - 1, pendLR)
                    pendLR = onehot(ti + 1, ip // PER)
                if ci + 1 < total:
                    u_mm(ci + 1)
                bk = 4 + (ci % 2)
                pa = PS[:, bk, :].rearrange("p (i t) -> p i t", i=2)
                A_, RA_, _ = Asb.get()
                S.op("act", lambda e: e.activation(out=A_[:], in_=pa, func=AF.Gelu), reads=[RB[bk]], writes=[RA_])
                WA, RWA, _ = WAr.get()
                S.op("pool", lambda e: e.tensor_tensor(WA[:], A_[:], WT[:, :, 2 * ip:2 * ip + 2].rearrange("p t i -> p i t"), op=ALU.mult),
                     reads=[RA_, RWT], writes=[RWA])
                if pendV is not None:
                    pci, pWA, pRWA = pendV
                    v_mm(pci, pWA, pRWA)
                    if pci % NP == NP - 1:
                        finalize(pci // NP)
                        pend.pop(pci // NP)
                pendV = (ci, WA, RWA)
            if pendLR is not None:
                wmm(ti + 1, NTB - 1, pendLR)
                pendLR = None
        pci, pWA, pRWA = pendV
        v_mm(pci, pWA, pRWA)
        finalize(pci // NP)
    barrier()
    top.close()
    return nc


def rope_tables(smax=8192):
    half = 8
    inv = (np.float32(500000.0) ** (-np.arange(half, dtype=np.float32) * np.float32(2.0) / np.float32(16))).astype(np.float32)
    pos = np.arange(smax, dtype=np.float32)
    ang = (pos[:, None] * inv[None, :]).astype(np.float32)
    cos = np.cos(ang).astype(np.float32).T
    sin = np.sin(ang).astype(np.float32).T
    C = np.ones((128, smax), np.float32)
    Sn = np.zeros((128, smax), np.float32)
    for hh in range(2):
        b = hh * 64
        C[b:b + 8] = cos
        C[b + 8:b + 16] = cos
        Sn[b:b + 8] = sin
        Sn[b + 8:b + 16] = sin
    return C, Sn


def make_shared(inp):
    f = lambda a: np.ascontiguousarray(np.asarray(a, dtype=np.float32))
    C, Sn = rope_tables()
    sh = {
        "w_in": f(inp["w_in"][0]),
        "g1b": f(np.broadcast_to(inp["norm1_g"][0][None, :], (128, D))),
        "g2b": f(np.broadcast_to(inp["norm2_g"][0][None, :], (128, D))),
        "gfb": f(np.broadcast_to(np.asarray(inp["final_g"])[None, :], (128, D))),
        "bgate": f(np.asarray(inp["b_gate"][0]).reshape(16, 128).T),
        "w_up": f(inp["w_attn_up"][0]),
        "dww": f(np.asarray(inp["conv_dw_w"][0]).reshape(31, 4, 128).transpose(2, 1, 0)),
        "cpar": f(np.stack([np.asarray(inp["conv_dw_b"][0]).reshape(4, 128).T,
                            np.asarray(inp["conv_ln_g"][0]).reshape(4, 128).T,
                            np.asarray(inp["conv_ln_b"][0]).reshape(4, 128).T], axis=1)),
        "pw_w": f(inp["conv_pw_w"][0]),
        "pwb": f(np.asarray(inp["conv_pw_b"][0]).reshape(8, 128).T),
        "w_out": f(inp["w_out"][0]),
        "wq": f(inp["peer_wq"][0]),
        "keysT": f(np.asarray(inp["peer_keys"][0]).reshape(16, 128, 128).transpose(2, 0, 1)),
        "peer_u": f(inp["peer_u"][0]),
        "peer_v": f(inp["peer_v"][0]),
        "rope_c": C,
        "rope_s": Sn,
        "dummy8": np.zeros((1, 8), np.float32),
    }
    return sh


def kernel(**inputs):
    xp = np.asarray(inputs["x_prompt"], dtype=np.float32)
    xs = np.asarray(inputs["x_sample"], dtype=np.float32)
    seqs = [xp.shape[1], xs.shape[1], xs.shape[1]]
    nc = build_program(seqs)
    sh = make_shared(inputs)
    in_maps = []
    for c in range(N_CORES):
        m = dict(sh)
        m["x"] = np.ascontiguousarray(np.concatenate([xp[c], xs[2 * c], xs[2 * c + 1]], axis=0))
        in_maps.append(m)
    res = run_bass_kernel_spmd(nc, in_maps, core_ids=list(range(N_CORES)))
    yp = np.empty_like(xp)
    ys = np.empty_like(xs)
    Lp, Ls = seqs[0], seqs[1]
    for c in range(N_CORES):
        y = np.asarray(res.results[c]["y"])
        yp[c] = y[0:Lp]
        ys[2 * c] = y[Lp:Lp + Ls]
        ys[2 * c + 1] = y[Lp + Ls:Lp + 2 * Ls]
    return (yp, ys)
```

```python
from contextlib import ExitStack
import numpy as np
import concourse.bass as bass
import concourse.mybir as mybir
from concourse.bass_utils import run_bass_kernel_spmd

F32 = mybir.dt.float32
BF16 = mybir.dt.bfloat16
U32 = mybir.dt.uint32
AF = mybir.ActivationFunctionType
ALU = mybir.AluOpType
AX = mybir.AxisListType

D = 1024
NCOL = 5376
PAD = 1024
NEXP = 16384
EPS = 1e-6
DIL = (1, 4, 16)
NEG = -30000.0
N_CORES = 8


class Res:
    __slots__ = ("name", "w", "rd")

    def __init__(self, name):
        self.name = name
        self.w = None
        self.rd = {}


class DSem:
    def __init__(self, h, key):
        self.h = h
        self.key = key
        self.count = 0


class Sched:
    CE = ("pe", "act", "dve", "pool")

    def __init__(self, nc):
        self.nc = nc
        self.E = {"pe": nc.tensor, "act": nc.scalar, "dve": nc.vector, "pool": nc.gpsimd, "sp": nc.sync}
        self.semh = {}
        self.cnt = {}
        self.waited = {e: {} for e in self.E}
        for e in self.CE:
            self.semh[e] = nc.alloc_semaphore("s_" + e)
            self.cnt[e] = 0
        self.dsems = []
        self.bar = nc.alloc_semaphore("s_bar")
        self.barcnt = 0
        self.n_ds = 0
        self.free_ds = []
        self.live_ds = []

    def new_dsem(self, name=None):
        if self.free_ds:
            d = self.free_ds.pop()
            self.live_ds.append(d)
            return d
        self.n_ds += 1
        name = "d%d_%s" % (self.n_ds, name or "")
        d = DSem(self.nc.alloc_semaphore(name), name)
        self.semh[name] = d.h
        self.dsems.append(d)
        self.live_ds.append(d)
        return d

    def recycle(self):
        self.free_ds.extend(self.live_ds)
        self.live_ds = []

    def _wait(self, eng, key, val):
        if self.waited[eng].get(key, 0) >= val:
            return
        self.E[eng].wait_ge(self.semh[key], val)
        self.waited[eng][key] = val

    def _deps(self, eng, reads, writes, is_dma=False, nosame=False):
        deps = {}

        def add(key, val, kind):
            if key == eng and not is_dma and (eng == "pe" or kind != "raw" or nosame):
                return
            if deps.get(key, 0) < val:
                deps[key] = val

        for r in reads:
            if r.w is not None:
                add(r.w[0], r.w[1], "raw")
        for r in writes:
            if r.w is not None:
                add(r.w[0], r.w[1], "waw")
            for k, v in r.rd.items():
                add(k, v, "war")
        for k, v in deps.items():
            self._wait(eng, k, v)

    def op(self, eng, fn, reads=(), writes=(), nosame=False):
        self._deps(eng, reads, writes, nosame=nosame)
        ins = fn(self.E[eng])
        self.cnt[eng] += 1
        c = self.cnt[eng]
        ins.then_inc(self.semh[eng], 1)
        for r in reads:
            r.rd[eng] = c
        for r in writes:
            r.w = (eng, c)
            r.rd = {}
        return ins

    def dma(self, out, in_, reads=(), writes=(), dsem=None, first=True, q="sp"):
        if isinstance(dsem, LazyD):
            dsem = dsem.get()
        self._deps(q, reads, writes, is_dma=True)
        if first and dsem.count > 0:
            self._wait(q, dsem.key, dsem.count)
        ins = self.E[q].dma_start(out=out, in_=in_)
        dsem.count += 16
        ins.then_inc(dsem.h, 16)
        for r in reads:
            r.rd[dsem.key] = dsem.count
        for r in writes:
            r.w = (dsem.key, dsem.count)
            r.rd = {}
        return ins

    def barrier(self, dummy_sb, dummy_dram):
        for e in self.CE:
            if self.cnt[e] > 0:
                self._wait("sp", e, self.cnt[e])
        for d in self.dsems:
            if d.count > 0:
                self._wait("sp", d.key, d.count)
        ins = self.E["sp"].dma_start(out=dummy_sb, in_=dummy_dram)
        self.barcnt += 16
        ins.then_inc(self.bar, 16)
        for e in self.E:
            self.E[e].wait_ge(self.bar, self.barcnt)
            for k in self.CE:
                self.waited[e][k] = self.cnt[k]
            for d in self.dsems:
                self.waited[e][d.key] = d.count


class LazyD:
    def __init__(self, S, name):
        self.S = S
        self.name = name
        self.d = None

    def get(self):
        if self.d is None:
            self.d = self.S.new_dsem(self.name)
        return self.d


class Rot:
    ctr = 0

    def __init__(self, S, st, nc, name, shape, dtype, n):
        self.items = []
        for i in range(n):
            Rot.ctr += 1
            t = st.enter_context(nc.sbuf_tensor("rot%d_%s%d" % (Rot.ctr, name, i), list(shape), dtype))
            self.items.append((t, Res("%s%d" % (name, i)), LazyD(S, name)))
        self.i = 0

    def get(self):
        it = self.items[self.i % len(self.items)]
        self.i += 1
        return it


def build_program(seqs, dbg=False):
    nc = bass.Bass("TRN2", target_bir_lowering=False)
    NT = sum(seqs)
    NTP = NT + 2 * PAD * len(seqs)
    seqinfo = []
    ts, ps_ = 0, 0
    for L in seqs:
        assert L % 2048 == 0
        seqinfo.append((ts, L, ps_))
        ts += L
        ps_ += L + 2 * PAD

    def din(name, shape, dt=F32):
        return nc.dram_tensor(name, list(shape), dt, kind="ExternalInput").ap()

    def dscr(name, shape, dt):
        if dbg:
            return nc.dram_tensor(name, list(shape), dt, kind="ExternalOutput").ap()
        return nc.dram_tensor(name, list(shape), dt).ap()

    x_in = din("x", [NT, D])
    w_in = din("w_in", [D, NCOL])
    g1b_in = din("g1b", [128, D])
    g2b_in = din("g2b", [128, D])
    gfb_in = din("gfb", [128, D])
    bgate_in = din("bgate", [128, 16])
    wup_in = din("w_up", [256, D])
    dww_in = din("dww", [128, 4, 31])
    cpar_in = din("cpar", [128, 3, 4])
    pww_in = din("pw_w", [512, D])
    pwb_in = din("pwb", [128, 8])
    wout_in = din("w_out", [D, D])
    wq_in = din("wq", [D, 2048])
    keysT_in = din("keysT", [128, 16, 128])
    pu_in = din("peer_u", [NEXP, D])
    pv_in = din("peer_v", [NEXP, D])
    ropec_in = din("rope_c", [128, 8192])
    ropes_in = din("rope_s", [128, 8192])
    dummy_in = din("dummy8", [1, 8])
    y_out = nc.dram_tensor("y", [NT, D], F32, kind="ExternalOutput").ap()

    qS = dscr("qS", [768, NT], BF16)
    kS = dscr("kS", [768, NTP], BF16)
    vS = dscr("vS", [NTP, 768], BF16)
    uS = dscr("uS", [512, NTP], BF16)
    gS = dscr("gS", [2048, NT], BF16)
    aS = dscr("aS", [256, NT], BF16)
    x2S = dscr("x2S", [NT, D], F32)
    h2S = dscr("h2S", [D, NT], BF16)
    rS = dscr("rS", [3, 128, NT], F32)
    UT = nc.dram_tensor("UT", [128, 128, 1024], BF16).ap()
    VB = nc.dram_tensor("VB", [NEXP, D], BF16).ap()

    S = Sched(nc)
    top = ExitStack()

    nctr = [0]

    def sb(st, name, shape, dt):
        nctr[0] += 1
        return st.enter_context(nc.sbuf_tensor("sb%d_%s" % (nctr[0], name), list(shape), dt))

    PS = top.enter_context(nc.psum_tensor("PS", [128, 8, 512], F32))
    RB = [Res("bank%d" % i) for i in range(8)]
    ident_bf = sb(top, "ident_bf", [128, 128], BF16)
    ident_f = sb(top, "ident_f", [128, 128], F32)
    iota_f = sb(top, "iota_f", [128, 128], F32)
    dif = sb(top, "dif", [128, 128], F32)
    maskAB = sb(top, "maskAB", [128, 2, 128], BF16)
    pcol = sb(top, "pcol", [128, 1], F32)
    bz = sb(top, "bz", [128, 1], F32)
    epsc = sb(top, "epsc", [128, 1], F32)
    blo = sb(top, "blo", [128, 1], F32)
    bhi = sb(top, "bhi", [128, 1], F32)
    ones65 = sb(top, "ones65", [65, 64], F32)
    onesM = sb(top, "onesM", [128, 128], BF16)
    dummy_sb = sb(top, "dummy_sb", [1, 8], F32)
    RC = Res("consts")

    S.op("pool", lambda e: e.iota(dif[:], pattern=[[1, 128]], base=0, channel_multiplier=-1,
                                  allow_small_or_imprecise_dtypes=True), writes=[RC])
    S.op("pool", lambda e: e.iota(iota_f[:], pattern=[[1, 128]], base=0, channel_multiplier=0,
                                  allow_small_or_imprecise_dtypes=True), writes=[RC])
    S.op("pool", lambda e: e.iota(pcol[:], pattern=[[0, 1]], base=0, channel_multiplier=1,
                                  allow_small_or_imprecise_dtypes=True), writes=[RC])
    S.op("dve", lambda e: e.tensor_single_scalar(ident_bf[:], dif[:], 0.0, op=ALU.is_equal), reads=[RC], writes=[RC])
    S.op("dve", lambda e: e.tensor_single_scalar(ident_f[:], dif[:], 0.0, op=ALU.is_equal), reads=[RC], writes=[RC])
    S.op("dve", lambda e: e.tensor_scalar(maskAB[:, 0, :], dif[:], 0.0, NEG, op0=ALU.is_gt, op1=ALU.mult), reads=[RC], writes=[RC])
    S.op("dve", lambda e: e.tensor_scalar(maskAB[:, 1, :], dif[:], 0.0, NEG, op0=ALU.is_lt, op1=ALU.mult), reads=[RC], writes=[RC])
    S.op("dve", lambda e: e.tensor_scalar(blo[:], pcol[:], 64.0, NEG, op0=ALU.is_lt, op1=ALU.mult), reads=[RC], writes=[RC])
    S.op("dve", lambda e: e.tensor_scalar(bhi[:], pcol[:], 64.0, NEG, op0=ALU.is_ge, op1=ALU.mult), reads=[RC], writes=[RC])
    S.op("dve", lambda e: e.memset(bz[:], 0.0), writes=[RC])
    S.op("dve", lambda e: e.memset(epsc[:], EPS), writes=[RC])
    S.op("dve", lambda e: e.memset(ones65[:], 1.0), writes=[RC])
    S.op("dve", lambda e: e.memset(onesM[:], 1.0 / 512.0), writes=[RC])

    def barrier():
        S.barrier(dummy_sb[:], dummy_in)
        S.recycle()

    barrier()

    def mm(out, lhsT, rhs, start, stop, reads, writes):
        S.op("pe", lambda e: e.matmul(out, lhsT=lhsT, rhs=rhs, start=start, stop=stop), reads=reads, writes=writes)

    with ExitStack() as st:
        zt = sb(st, "zt", [128, 6144], BF16)
        RZ = Res("zt")
        S.op("pool", lambda e: e.memset(zt[:], 0.0), writes=[RZ])
        dz = S.new_dsem("zero")
        for (ts_, L, pb) in seqinfo:
            for off in (pb, pb + PAD + L):
                S.dma(kS[:, off:off + PAD].rearrange("(a p) t -> p a t", p=128),
                      zt[:].rearrange("p (a t) -> p a t", a=6), reads=[RZ], dsem=dz, first=False)
                S.dma(vS[off:off + PAD, :].rearrange("(p a) c -> p (a c)", p=128), zt[:], reads=[RZ], dsem=dz, first=False)
                S.dma(uS[:, off:off + PAD].rearrange("(a p) t -> p a t", p=128),
                      zt[:, 0:4096].rearrange("p (a t) -> p a t", a=4), reads=[RZ], dsem=dz, first=False)
    barrier()

    tiles = []
    for (ts_, L, pb) in seqinfo:
        for p0 in range(0, L, 512):
            tiles.append((ts_ + p0, p0, pb + PAD + p0))

    def load_w_bf16(st, name, src, kchunks, ncols, eng_cycle=("act", "dve"), rows=128):
        wt = sb(st, name, [rows, kchunks, ncols], BF16)
        R = Res(name)
        with ExitStack() as st2:
            stg = Rot(S, st2, nc, name + "_stg", [rows, ncols], F32, 2)
            for kc in range(kchunks):
                t, r, d = stg.get()
                S.dma(t[:], src[kc * rows:(kc + 1) * rows, :], writes=[r], dsem=d)
                eng = eng_cycle[kc % len(eng_cycle)]
                if eng == "act":
                    S.op("act", lambda e: e.copy(wt[:, kc, :], t[:]), reads=[r], writes=[R])
                else:
                    S.op(eng, lambda e: e.tensor_copy(wt[:, kc, :], t[:]), reads=[r], writes=[R])
            barrier()
        return wt, R

    def load_small(st, name, src, shape):
        t = sb(st, name, shape, F32)
        R = Res(name)
        S.dma(t[:], src, writes=[R], dsem=S.new_dsem(name))
        return t, R

    def rms_block(xt_ap, Rx, gb, Rg, out_bf, Rout, junk, Rjunk, ssq, Rssq, col):
        S.op("act", lambda e: e.activation(out=junk[:], in_=xt_ap, func=AF.Square, accum_out=ssq[:, col:col + 1]),
             reads=[Rx], writes=[Rjunk, Rssq])
        S.op("act", lambda e: e.activation(out=ssq[:, col:col + 1], in_=ssq[:, col:col + 1], func=AF.Sqrt, bias=epsc[:], scale=1.0 / D),
             reads=[Rssq, RC], writes=[Rssq])
        S.op("dve", lambda e: e.reciprocal(ssq[:, col:col + 1], ssq[:, col:col + 1]), reads=[Rssq], writes=[Rssq])
        S.op("dve", lambda e: e.scalar_tensor_tensor(out_bf, xt_ap, ssq[:, col:col + 1], gb[:], op0=ALU.mult, op1=ALU.mult),
             reads=[Rx, Rssq, Rg], writes=[Rout])

    with ExitStack() as st:
        Wb, RW = load_w_bf16(st, "Wb", w_in, 8, NCOL)
        Wr = sb(st, "Wr", [128, 8, 1536], BF16)
        RWr = Res("Wr")
        S.op("pool", lambda e: e.memset(Wr[:], 0.0), writes=[RWr])
        for kc in range(8):
            wv = Wb[:, kc, 0:1536].rearrange("p (h e) -> p h e", e=64)
            rv = Wr[:, kc, :].rearrange("p (h e) -> p h e", e=64)
            S.op("pool", lambda e: e.tensor_single_scalar(rv[:, :, 0:8], wv[:, :, 8:16], -1.0, op=ALU.mult), reads=[RW], writes=[RWr])
            S.op("pool", lambda e: e.tensor_copy(rv[:, :, 8:16], wv[:, :, 0:8]), reads=[RW], writes=[RWr])
        g1b, Rg1 = load_small(st, "g1b", g1b_in, [128, D])
        bgate, Rbg = load_small(st, "bgate", bgate_in, [128, 16])
        xr = Rot(S, st, nc, "xt", [128, D], F32, 3)
        ropc = Rot(S, st, nc, "ropc", [128, 512], F32, 2)
        rops = Rot(S, st, nc, "rops", [128, 512], F32, 2)
        junk = sb(st, "junk", [128, D], BF16)
        Rjunk = Res("junk")
        ssq = sb(st, "ssq", [128, 8], F32)
        Rssq = [Res("ssq%d" % i) for i in range(8)]
        hbr = Rot(S, st, nc, "hb", [128, D], BF16, 2)
        hTs = [sb(st, "hT%d" % i, [128, 8, 512], BF16) for i in range(2)]
        RhT = [Res("hT%d" % i) for i in range(2)]
        t1 = Rot(S, st, nc, "t1", [128, 512], F32, 2)
        t2 = Rot(S, st, nc, "t2", [128, 512], F32, 2)
        qko = Rot(S, st, nc, "qko", [128, 512], BF16, 3)
        sig = Rot(S, st, nc, "sig", [128, 512], F32, 2)
        uo = Rot(S, st, nc, "uo", [128, 512], BF16, 2)
        go = Rot(S, st, nc, "go", [128, 512], BF16, 3)
        vo = Rot(S, st, nc, "vo", [128, 768], BF16, 2)

        xq = {}
        blocks = [(ti, a) for ti in range(len(tiles)) for a in range(4)]
        nxt = [0]

        def p1_prefetch(upto):
            while nxt[0] <= upto and nxt[0] < len(blocks):
                ti, a = blocks[nxt[0]]
                t0 = tiles[ti][0]
                it = xr.get()
                S.dma(it[0][:], x_in[t0 + a * 128:t0 + (a + 1) * 128, :], writes=[it[1]], dsem=it[2])
                xq[nxt[0]] = it
                nxt[0] += 1

        pbank = [0]

        def nbank():
            b = pbank[0] % 4
            pbank[0] += 1
            return b

        for ti, (t0, p0, pt) in enumerate(tiles):
            hT = hTs[ti % 2]
            RH = RhT[ti % 2]
            rc = ropc.get()
            rs = rops.get()
            S.dma(rc[0][:], ropec_in[:, p0:p0 + 512], writes=[rc[1]], dsem=rc[2])
            S.dma(rs[0][:], ropes_in[:, p0:p0 + 512], writes=[rs[1]], dsem=rs[2])
            for a in range(4):
                bi = ti * 4 + a
                p1_prefetch(bi + 2)
                xt_, Rx, _ = xq.pop(bi)
                hb, Rhb, _ = hbr.get()
                col = bi % 8
                rms_block(xt_[:], Rx, g1b, Rg1, hb[:], Rhb, junk, Rjunk, ssq, Rssq[col], col)
                bank = 6 + (bi % 2)
                ptb = PS[:, bank, :].bitcast(BF16)
                for kc in range(8):
                    S.op("pe", lambda e: e.transpose(ptb[:, kc * 128:(kc + 1) * 128], hb[:, kc * 128:(kc + 1) * 128], ident_bf[:]),
                         reads=[Rhb], writes=[RB[bank]])
                S.op("act", lambda e: e.copy(hT[:, :, a * 128:(a + 1) * 128], ptb.rearrange("p (k t) -> p k t", k=8)),
                     reads=[RB[bank]], writes=[RH])
            for cc in range(12):
                bm = nbank()
                br = nbank()
                for kc in range(8):
                    mm(PS[:, bm, :], Wb[:, kc, cc * 128:(cc + 1) * 128], hT[:, kc, :], kc == 0, kc == 7, [RW, RH], [RB[bm]])
                for kc in range(8):
                    mm(PS[:, br, :], Wr[:, kc, cc * 128:(cc + 1) * 128], hT[:, kc, :], kc == 0, kc == 7, [RWr, RH], [RB[br]])
                a1 = t1.get()
                a2 = t2.get()
                o = qko.get()
                S.op("dve", lambda e: e.tensor_tensor(a1[0][:], PS[:, bm, :], rc[0][:], op=ALU.mult), reads=[RB[bm], rc[1]], writes=[a1[1]])
                S.op("dve", lambda e: e.tensor_tensor(a2[0][:], PS[:, br, :], rs[0][:], op=ALU.mult), reads=[RB[br], rs[1]], writes=[a2[1]])
                S.op("pool", lambda e: e.tensor_tensor(o[0][:], a1[0][:], a2[0][:], op=ALU.add), reads=[a1[1], a2[1]], writes=[o[1]])
                if cc < 6:
                    S.dma(qS[cc * 128:(cc + 1) * 128, t0:t0 + 512], o[0][:], reads=[o[1]], dsem=o[2])
                else:
                    S.dma(kS[(cc - 6) * 128:(cc - 5) * 128, pt:pt + 512], o[0][:], reads=[o[1]], dsem=o[2])
            for c in range(4):
                ba = nbank()
                bb = nbank()
                for kc in range(8):
                    mm(PS[:, ba, :], Wb[:, kc, 2304 + c * 128:2304 + (c + 1) * 128], hT[:, kc, :], kc == 0, kc == 7, [RW, RH], [RB[ba]])
                for kc in range(8):
                    mm(PS[:, bb, :], Wb[:, kc, 2816 + c * 128:2816 + (c + 1) * 128], hT[:, kc, :], kc == 0, kc == 7, [RW, RH], [RB[bb]])
                sg = sig.get()
                o = uo.get()
                S.op("act", lambda e: e.activation(out=sg[0][:], in_=PS[:, bb, :], func=AF.Sigmoid), reads=[RB[bb]], writes=[sg[1]])
                S.op("dve", lambda e: e.tensor_tensor(o[0][:], PS[:, ba, :], sg[0][:], op=ALU.mult), reads=[RB[ba], sg[1]], writes=[o[1]])
                S.dma(uS[c * 128:(c + 1) * 128, pt:pt + 512], o[0][:], reads=[o[1]], dsem=o[2])
            for gc in range(16):
                bg = nbank()
                for kc in range(8):
                    mm(PS[:, bg, :], Wb[:, kc, 3328 + gc * 128:3328 + (gc + 1) * 128], hT[:, kc, :], kc == 0, kc == 7, [RW, RH], [RB[bg]])
                o = go.get()
                S.op("act", lambda e: e.activation(out=o[0][:], in_=PS[:, bg, :], func=AF.Sigmoid, bias=bgate[:, gc:gc + 1]),
                     reads=[RB[bg], Rbg], writes=[o[1]])
                S.dma(gS[gc * 128:(gc + 1) * 128, t0:t0 + 512], o[0][:], reads=[o[1]], dsem=o[2])
            for a in range(4):
                for kc in range(8):
                    mm(PS[:, 4, :], hT[:, kc, a * 128:(a + 1) * 128], Wb[:, kc, 1536:2048], kc == 0, kc == 7, [RW, RH], [RB[4]])
                for kc in range(8):
                    mm(PS[:, 5, 0:256], hT[:, kc, a * 128:(a + 1) * 128], Wb[:, kc, 2048:2304], kc == 0, kc == 7, [RW, RH], [RB[5]])
                o = vo.get()
                S.op("act", lambda e: e.copy(o[0][:, 0:512], PS[:, 4, :]), reads=[RB[4]], writes=[o[1]])
                S.op("dve", lambda e: e.tensor_copy(o[0][:, 512:768], PS[:, 5, 0:256]), reads=[RB[5]], writes=[o[1]])
                S.dma(vS[pt + a * 128:pt + (a + 1) * 128, :], o[0][:], reads=[o[1]], dsem=o[2])
    barrier()

    with ExitStack() as st:
        acc = sb(st, "acc", [65, 4, 2048], F32)
        Racc = [Res("acc%d" % i) for i in range(4)]
        Vaug = [sb(st, "Vaug%d" % i, [128, 32, 4, 65], BF16) for i in range(2)]
        RV = [Res("Vaug%d" % i) for i in range(2)]
        dV = [S.new_dsem("Vaug") for _ in range(2)]
        for i in range(2):
            S.op("pool", lambda e: e.memset(Vaug[i][:], 1.0), writes=[RV[i]])
        Qr = Rot(S, st, nc, "Qt", [64, 2048], BF16, 3)
        Kr = Rot(S, st, nc, "Kt", [64, 4096], BF16, 3)
        Pr = Rot(S, st, nc, "Pt", [128, 2, 128], BF16, 4)
        rec = Rot(S, st, nc, "rec", [64, 512], F32, 2)
        ao = sb(st, "ao", [64, 4, 2048], BF16)
        Rao = Res("ao")
        dao = S.new_dsem("ao")

        jobs = []
        for (ts_, L, pb) in seqinfo:
            for s0 in range(0, L, 2048):
                for g in range(3):
                    for hs in range(4):
                        jobs.append(dict(t0g=ts_ + s0, po=pb + PAD + s0, first=(s0 == 0), last=(s0 + 2048 == L), g=g, hs=hs))
        vcount = [0]
        vbuf_of = {}

        def load_v(jb):
            key = (jb["t0g"], jb["g"])
            if key in vbuf_of:
                return
            g = jb["g"]
            d = DIL[g]
            nq = 16 // d
            gb = vcount[0] % 2
            vcount[0] += 1
            vbuf_of[key] = gb
            fst = True
            for r in range(d):
                for j in range(nq + 1):
                    tok = jb["po"] + (128 * j - 64) * d + r
                    S.dma(Vaug[gb][:, r * (nq + 1) + j, :, 0:64],
                          vS[tok:tok + 127 * d + 1:d, g * 256:(g + 1) * 256].rearrange("p (h e) -> p h e", h=4),
                          writes=[RV[gb]], dsem=dV[gb], first=fst)
                    fst = False

        def load_qk(jb):
            g = jb["g"]
            hs = jb["hs"]
            d = DIL[g]
            halo = 64 * d
            hr = (g * 4 + hs) * 64
            Qt, RQ, dQ = Qr.get()
            Kt, RK, dK = Kr.get()
            S.dma(Qt[:], qS[hr:hr + 64, jb["t0g"]:jb["t0g"] + 2048], writes=[RQ], dsem=dQ)
            S.dma(Kt[:, 0:2048 + 2 * halo], kS[hr:hr + 64, jb["po"] - halo:jb["po"] + 2048 + halo], writes=[RK], dsem=dK)
            jb["qk"] = (Qt, RQ, Kt, RK)

        units = []
        for ji, jb in enumerate(jobs):
            d = DIL[jb["g"]]
            nq = 16 // d
            for r in range(d):
                for qb in range(nq):
                    units.append((ji, r, qb))
        loaded_upto = [-1]

        def ensure_job(ji):
            while loaded_upto[0] < ji and loaded_upto[0] + 1 < len(jobs):
                loaded_upto[0] += 1
                jb = jobs[loaded_upto[0]]
                load_v(jb)
                load_qk(jb)

        ust_ = {}

        def stage_S(u):
            ji, r, qb = units[u]
            jb = jobs[ji]
            ensure_job(ji)
            d = DIL[jb["g"]]
            Qt, RQ, Kt, RK = jb["qk"]
            c0 = 128 * qb * d + r
            qv = Qt[:, c0:c0 + 127 * d + 1:d]
            kA = Kt[:, c0:c0 + 127 * d + 1:d]
            kB = Kt[:, c0 + 128 * d:c0 + 255 * d + 1:d]
            sbk = u % 4
            pst = PS[:, sbk, 0:256].rearrange("p (a q) -> p a q", a=2)
            mm(pst[:, 0, :], kA, qv, True, False, [RK, RQ], [RB[sbk]])
            mm(pst[:, 0, :], ident_bf[:], maskAB[:, 0, :], False, True, [RC], [RB[sbk]])
            mm(pst[:, 1, :], kB, qv, True, False, [RK, RQ], [RB[sbk]])
            mm(pst[:, 1, :], ident_bf[:], maskAB[:, 1, :], False, True, [RC], [RB[sbk]])
            ust_[u] = (sbk, pst, c0)

        def stage_rest(u):
            ji, r, qb = units[u]
            jb = jobs[ji]
            g = jb["g"]
            hs = jb["hs"]
            d = DIL[g]
            nq = 16 // d
            gb = vbuf_of[(jb["t0g"], g)]
            sbk, pst, c0 = ust_.pop(u)
            bA = blo if (jb["first"] and qb == 0) else bz
            bB = bhi if (jb["last"] and qb == nq - 1) else bz
            Pt, RP, _ = Pr.get()
            if bA is bB:
                S.op("act", lambda e: e.activation(out=Pt[:], in_=pst, func=AF.Exp, bias=bz[:], scale=0.125),
                     reads=[RB[sbk], RC], writes=[RP])
            else:
                S.op("act", lambda e: e.activation(out=Pt[:, 0, :], in_=pst[:, 0, :], func=AF.Exp, bias=bA[:], scale=0.125),
                     reads=[RB[sbk], RC], writes=[RP])
                S.op("act", lambda e: e.activation(out=Pt[:, 1, :], in_=pst[:, 1, :], func=AF.Exp, bias=bB[:], scale=0.125),
                     reads=[RB[sbk], RC], writes=[RP])
            return (Pt, RP, gb, g, hs, d, nq, r, qb, c0)

        def stage_PV(u, ctx):
            Pt, RP, gb, g, hs, d, nq, r, qb, c0 = ctx
            obk = 4 + (u % 4)
            pov = PS[0:65, obk, 0:128]
            blk = r * (nq + 1) + qb
            mm(pov, Vaug[gb][:, blk, hs, :], Pt[:, 0, :], True, False, [RV[gb], RP], [RB[obk]])
            mm(pov, Vaug[gb][:, blk + 1, hs, :], Pt[:, 1, :], False, True, [RV[gb], RP], [RB[obk]])
            av = acc[:, hs, c0:c0 + 127 * d + 1:d]
            if g == 0:
                S.op("dve", lambda e: e.tensor_copy(av, pov), reads=[RB[obk]], writes=[Racc[hs]], nosame=True)
            else:
                S.op("dve", lambda e: e.tensor_tensor(av, av, pov, op=ALU.add), reads=[RB[obk], Racc[hs]], writes=[Racc[hs]], nosame=True)

        def normalise(jb):
            t0g = jb["t0g"]
            for hs in range(4):
                for ch in range(4):
                    bk = ch % 4
                    mm(PS[0:64, bk, :], ones65[64:65, :], acc[64:65, hs, ch * 512:(ch + 1) * 512], True, True, [Racc[hs], RC], [RB[bk]])
                    rc_ = rec.get()
                    S.op("dve", lambda e: e.reciprocal(rc_[0][:], PS[0:64, bk, :]), reads=[RB[bk]], writes=[rc_[1]])
                    S.op("pool", lambda e: e.tensor_tensor(ao[:, hs, ch * 512:(ch + 1) * 512], acc[0:64, hs, ch * 512:(ch + 1) * 512],
                                                          rc_[0][:], op=ALU.mult), reads=[Racc[hs], rc_[1]], writes=[Rao])
            S.dma(aS[:, t0g:t0g + 2048].rearrange("(h e) t -> e h t", h=4), ao[:], reads=[Rao], dsem=dao)

        NU = len(units)
        ensure_job(0)
        LOOK = 2
        UPT = 192
        for tile0 in range(0, NU, UPT):
            tend = tile0 + UPT
            for u in range(tile0, min(tile0 + LOOK, tend)):
                stage_S(u)
            for u in range(tile0, tend):
                ctx = stage_rest(u)
                if u + LOOK < tend:
                    stage_S(u + LOOK)
                stage_PV(u, ctx)
                ensure_job(min(units[u][0] + 1, len(jobs) - 1))
            normalise(jobs[units[tile0][0]])
    barrier()

    with ExitStack() as st:
        pww, Rpww = load_w_bf16(st, "pww", pww_in, 4, D)
        wup, Rwup = load_w_bf16(st, "wup", wup_in, 4, D, rows=64)
        wout, Rwout = load_w_bf16(st, "wout", wout_in, 8, D)
        dww, Rdww = load_small(st, "dww", dww_in, [128, 4, 31])
        cpar, Rcp = load_small(st, "cpar", cpar_in, [128, 3, 4])
        pwb, Rpwb = load_small(st, "pwb", pwb_in, [128, 8])
        Dg = sb(st, "Dg", [128, 124, 128], BF16)
        RDg = Res("Dg")
        for c in range(4):
            for k in range(31):
                eng = "dve" if (k % 2 == 0) else "pool"
                S.op(eng, lambda e: e.tensor_scalar(Dg[:, c * 31 + k, :], ident_f[:], dww[:, c, k:k + 1], None, op0=ALU.mult),
                     reads=[Rdww, RC], writes=[RDg])
        Ur = Rot(S, st, nc, "U", [128, 4, 542], BF16, 2)
        Ar = Rot(S, st, nc, "At", [64, 4, 512], BF16, 2)
        Gr = Rot(S, st, nc, "G", [128, 2, 512], BF16, 3)
        xr = Rot(S, st, nc, "x3", [128, D], F32, 2)
        ucs = [sb(st, "uc%d" % i, [128, 4, 512], BF16) for i in range(2)]
        sqs = [sb(st, "sq%d" % i, [128, 4, 512], BF16) for i in range(2)]
        Rucs = [Res("uc%d" % i) for i in range(2)]
        Rsqs = [Res("sq%d" % i) for i in range(2)]
        msq = sb(st, "msq", [128, 512], F32)
        rstd = sb(st, "rstd", [128, 512], F32)
        nmr = sb(st, "nmr", [128, 512], F32)
        Rst = Res("stats")
        tt = Rot(S, st, nc, "tt", [128, 512], F32, 2)
        sl = sb(st, "sl", [128, 4, 512], BF16)
        Rsl = Res("sl")
        m1 = Rot(S, st, nc, "m1", [128, 512], F32, 2)
        m2 = Rot(S, st, nc, "m2", [128, 512], F32, 2)
        mx = sb(st, "mx", [128, 8, 512], BF16)
        Rmx = Res("mx")
        x2r = Rot(S, st, nc, "x2o", [128, D], F32, 2)
        actx = {}

        def stage_conv(ti):
            t0, p0, pt = tiles[ti]
            uc, sq, Ruc, Rsq = ucs[ti % 2], sqs[ti % 2], Rucs[ti % 2], Rsqs[ti % 2]
            U, RU, dU = Ur.get()
            S.dma(U[:], uS[:, pt - 15:pt + 527].rearrange("(c p) t -> p c t", p=128), writes=[RU], dsem=dU)
            At, RA, dA = Ar.get()
            S.dma(At[:], aS[:, t0:t0 + 512].rearrange("(h e) t -> e h t", h=4), writes=[RA], dsem=dA)
            for c in range(4):
                bk = c % 2
                for k in range(31):
                    mm(PS[:, bk, :], Dg[:, c * 31 + k, :], U[:, c, k:k + 512], k == 0, k == 30, [RDg, RU], [RB[bk]])
                S.op("act", lambda e: e.activation(out=uc[:, c, :], in_=PS[:, bk, :], func=AF.Identity, bias=cpar[:, 0, c:c + 1]),
                     reads=[RB[bk], Rcp], writes=[Ruc])
                S.op("act", lambda e: e.activation(out=sq[:, c, :], in_=PS[:, bk, :], func=AF.Square, bias=cpar[:, 0, c:c + 1]),
                     reads=[RB[bk], Rcp], writes=[Rsq])
            actx[ti] = (At, RA)

        def stage_stats(ti):
            uc, sq, Ruc, Rsq = ucs[ti % 2], sqs[ti % 2], Rucs[ti % 2], Rsqs[ti % 2]
            for c in range(4):
                mm(PS[:, 2, :], onesM[:], uc[:, c, :], c == 0, c == 3, [RC, Ruc], [RB[2]])
            for c in range(4):
                mm(PS[:, 3, :], onesM[:], sq[:, c, :], c == 0, c == 3, [RC, Rsq], [RB[3]])

        def stage_rest(ti):
            t0, p0, pt = tiles[ti]
            uc, Ruc = ucs[ti % 2], Rucs[ti % 2]
            At, RA = actx.pop(ti)
            bme, bex = 2, 3
            S.op("act", lambda e: e.activation(out=msq[:], in_=PS[:, bme, :], func=AF.Square), reads=[RB[bme]], writes=[Rst])
            S.op("dve", lambda e: e.tensor_tensor(rstd[:], PS[:, bex, :], msq[:], op=ALU.subtract), reads=[RB[bex], Rst], writes=[Rst])
            S.op("act", lambda e: e.activation(out=rstd[:], in_=rstd[:], func=AF.Sqrt, bias=epsc[:]), reads=[Rst, RC], writes=[Rst])
            S.op("dve", lambda e: e.reciprocal(rstd[:], rstd[:]), reads=[Rst], writes=[Rst])
            S.op("dve", lambda e: e.tensor_tensor(nmr[:], PS[:, bme, :], rstd[:], op=ALU.mult), reads=[RB[bme], Rst], writes=[Rst])
            if ti + 1 < len(tiles):
                stage_stats(ti + 1)
            for c in range(4):
                t_, Rt_, _ = tt.get()
                S.op("dve", lambda e: e.tensor_tensor(t_[:], uc[:, c, :], rstd[:], op=ALU.mult), reads=[Ruc, Rst], writes=[Rt_])
                S.op("pool", lambda e: e.tensor_tensor(t_[:], t_[:], nmr[:], op=ALU.subtract), reads=[Rt_, Rst], writes=[Rt_])
                S.op("act", lambda e: e.activation(out=sl[:, c, :], in_=t_[:], func=AF.Silu, bias=cpar[:, 2, c:c + 1], scale=cpar[:, 1, c:c + 1]),
                     reads=[Rt_, Rcp], writes=[Rsl])
            for dc in range(8):
                G, RG, dG = Gr.get()
                S.dma(G[:], gS[:, t0:t0 + 512].rearrange("(b c p) t -> p b c t", b=2, p=128)[:, :, dc, :], writes=[RG], dsem=dG)
                bcv = 4 + 2 * (dc % 2)
                bat = bcv + 1
                for c in range(4):
                    mm(PS[:, bcv, :], pww[:, c, dc * 128:(dc + 1) * 128], sl[:, c, :], c == 0, c == 3, [Rpww, Rsl], [RB[bcv]])
                for hs in range(4):
                    mm(PS[:, bat, :], wup[:, hs, dc * 128:(dc + 1) * 128], At[:, hs, :], hs == 0, hs == 3, [Rwup, RA], [RB[bat]])
                a1 = m1.get()
                a2 = m2.get()
                S.op("dve", lambda e: e.scalar_tensor_tensor(a1[0][:], PS[:, bcv, :], pwb[:, dc:dc + 1], G[:, 1, :], op0=ALU.add, op1=ALU.mult),
                     reads=[RB[bcv], Rpwb, RG], writes=[a1[1]])
                S.op("dve", lambda e: e.tensor_tensor(a2[0][:], PS[:, bat, :], G[:, 0, :], op=ALU.mult), reads=[RB[bat], RG], writes=[a2[1]])
                S.op("pool", lambda e: e.tensor_tensor(mx[:, dc, :], a1[0][:], a2[0][:], op=ALU.add), reads=[a1[1], a2[1]], writes=[Rmx])
            for a in range(4):
                xt_, Rx, dX = xr.get()
                S.dma(xt_[:], x_in[t0 + a * 128:t0 + (a + 1) * 128, :], writes=[Rx], dsem=dX)
                b0 = 4 + 2 * (a % 2)
                for n in range(2):
                    for kc in range(8):
                        mm(PS[:, b0 + n, :], mx[:, kc, a * 128:(a + 1) * 128], wout[:, kc, n * 512:(n + 1) * 512], kc == 0, kc == 7,
                           [Rmx, Rwout], [RB[b0 + n]])
                o, Ro, dO = x2r.get()
                S.op("dve", lambda e: e.tensor_tensor(o[:].rearrange("p (n f) -> p n f", n=2), PS[:, b0:b0 + 2, :],
                                                     xt_[:].rearrange("p (n f) -> p n f", n=2), op=ALU.add),
                     reads=[RB[b0], RB[b0 + 1], Rx], writes=[Ro])
                S.dma(x2S[t0 + a * 128:t0 + (a + 1) * 128, :], o[:], reads=[Ro], dsem=dO)

        stage_conv(0)
        stage_stats(0)
        for ti in range(len(tiles)):
            if ti + 1 < len(tiles):
                stage_conv(ti + 1)
            stage_rest(ti)
    barrier()

    with ExitStack() as st:
        wq, Rwq = load_w_bf16(st, "wq", wq_in, 8, 2048)
        keysT3, RkT = load_w_bf16(st, "keysT", keysT_in.rearrange("p a b -> p (a b)"), 1, 2048)
        keysT = keysT3[:, 0, :].rearrange("p (a b) -> p a b", a=16)
        g2b, Rg2 = load_small(st, "g2b", g2b_in, [128, D])
        xr = Rot(S, st, nc, "x2i", [128, D], F32, 2)
        junk = sb(st, "junk2", [128, D], BF16)
        Rjunk = Res("junk2")
        ssq = sb(st, "ssq2", [128, 8], F32)
        Rssq = [Res("ssq2_%d" % i) for i in range(8)]
        hbr = Rot(S, st, nc, "h2b", [128, D], BF16, 2)
        hTr = Rot(S, st, nc, "h2T", [128, 8, 512], BF16, 2)
        qpTs = [sb(st, "qpT%d" % i, [128, 16, 512], BF16) for i in range(2)]
        RqpTs = [Res("qpT%d" % i) for i in range(2)]
        class BS:
            pass

        BSETS = []
        for bi_ in range(2):
            B = BS()
            B.Ssb = sb(st, "Ssb%d" % bi_, [128, 16, 128], F32)
            B.RS_ = Res("Ssb")
            B.wk = sb(st, "wk%d" % bi_, [128, 16, 128], F32)
            B.Rwk = Res("wk")
            B.tops = sb(st, "tops%d" % bi_, [128, 16, 16], F32)
            B.Rtops = Res("tops")
            B.tidx = sb(st, "tidx%d" % bi_, [128, 16, 16], U32)
            B.Rtidx = Res("tidx")
            B.tif = sb(st, "tif%d" % bi_, [128, 16, 16], F32)
            B.Rtif = Res("tif")
            B.best = sb(st, "best%d" % bi_, [128, 8, 16], F32)
            B.Rbest = Res("best")
            B.bidx = sb(st, "bidx%d" % bi_, [128, 8, 16], U32)
            B.Rbidx = Res("bidx")
            B.ab_u = sb(st, "ab_u%d" % bi_, [128, 2, 8, 16], U32)
            B.ab_f = sb(st, "ab_f%d" % bi_, [128, 2, 8, 16], F32)
            B.Rab = Res("ab")
            B.ge = sb(st, "ge%d" % bi_, [128, 8, 16], F32)
            B.gs = sb(st, "gs%d" % bi_, [128, 8], F32)
            B.Rge = Res("ge")
            B.E0 = sb(st, "E0%d" % bi_, [128, 8, 16, 16], F32)
            B.RE0 = Res("E0")
            B.E1 = sb(st, "E1%d" % bi_, [128, 8, 16, 16], F32)
            B.RE1 = Res("E1")
            B.Rt = sb(st, "Rt%d" % bi_, [128, 3, 128], F32)
            B.RRt = Res("Rt")
            BSETS.append(B)
        RTr = Rot(S, st, nc, "RT", [128, 3, 128], F32, 2)

        ust = Rot(S, st, nc, "ust", [128, D], F32, 1)
        vst = Rot(S, st, nc, "vst", [128, D], F32, 1)
        ubf = Rot(S, st, nc, "ubf", [128, D], BF16, 1)
        uts = Rot(S, st, nc, "uts", [128, D], BF16, 2)
        vbf = Rot(S, st, nc, "vbf", [128, D], BF16, 2)
        NI = NEXP // 128
        tp_loaded = {}
        tp_next = [0]
        tp_done = [0]

        def tp_load():
            i = tp_next[0]
            if i >= NI:
                return
            a = ust.get()
            b = vst.get()
            S.dma(a[0][:], pu_in[i * 128:(i + 1) * 128, :], writes=[a[1]], dsem=a[2])
            S.dma(b[0][:], pv_in[i * 128:(i + 1) * 128, :], writes=[b[1]], dsem=b[2])
            tp_loaded[i] = (a, b)
            tp_next[0] += 1

        def tp_step():
            i = tp_done[0]
            if i >= NI:
                return
            if i not in tp_loaded:
                tp_load()
            a, b = tp_loaded.pop(i)
            ub = ubf.get()
            S.op("act", lambda e: e.copy(ub[0][:], a[0][:]), reads=[a[1]], writes=[ub[1]])
            ptb = PS[:, 7, :].bitcast(BF16)
            for kc in range(8):
                S.op("pe", lambda e: e.transpose(ptb[:, kc * 128:(kc + 1) * 128], ub[0][:, kc * 128:(kc + 1) * 128], ident_bf[:]),
                     reads=[ub[1]], writes=[RB[7]])
            us_ = uts.get()
            S.op("act", lambda e: e.copy(us_[0][:], ptb), reads=[RB[7]], writes=[us_[1]])
            S.dma(UT[i], us_[0][:], reads=[us_[1]], dsem=us_[2])
            vb_ = vbf.get()
            S.op("pool", lambda e: e.tensor_copy(vb_[0][:], b[0][:]), reads=[b[1]], writes=[vb_[1]])
            S.dma(VB[i * 128:(i + 1) * 128, :], vb_[0][:], reads=[vb_[1]], dsem=vb_[2])
            tp_done[0] += 1
            tp_load()

        def routing(a, B, t0, qpT, RqpT):
            Ssb, RS_, wk, Rwk = B.Ssb, B.RS_, B.wk, B.Rwk
            tops, Rtops, tidx, Rtidx, tif, Rtif = B.tops, B.Rtops, B.tidx, B.Rtidx, B.tif, B.Rtif
            best, Rbest, bidx, Rbidx = B.best, B.Rbest, B.bidx, B.Rbidx
            ab_u, ab_f, Rab, ge, gs, Rge = B.ab_u, B.ab_f, B.Rab, B.ge, B.gs, B.Rge
            Rt, RRt = B.Rt, B.RRt
            cand = wk[:].rearrange("p (h c) k -> p h (c k)", c=2)
            Rcand = Rwk
            wk2 = Ssb[:].rearrange("p (h c) k -> p h (c k)", c=2)
            Rwk2 = RS_
            for hc in range(16):
                b = hc // 4
                mm(PS[:, b, (hc % 4) * 128:(hc % 4 + 1) * 128], qpT[:, hc, a * 128:(a + 1) * 128], keysT[:, hc, :], True, True,
                   [RqpT, RkT], [RB[b]])
            S.op("act", lambda e: e.copy(Ssb[:].rearrange("p (b q) k -> p b (q k)", b=4), PS[:, 0:4, :]),
                 reads=[RB[0], RB[1], RB[2], RB[3]], writes=[RS_])
            yield
            for hc in range(16):
                S.op("dve", lambda e: e.max(out=tops[:, hc, 0:8], in_=Ssb[:, hc, :]), reads=[RS_], writes=[Rtops])
            yield
            for hc in range(16):
                S.op("dve", lambda e: e.max_index(out=tidx[:, hc, 0:8], in_max=tops[:, hc, 0:8], in_values=Ssb[:, hc, :]),
                     reads=[RS_, Rtops], writes=[Rtidx])
            for hc in range(16):
                S.op("dve", lambda e: e.match_replace(out=wk[:, hc, :], in_to_replace=tops[:, hc, 0:8], in_values=Ssb[:, hc, :], imm_value=-1e30),
                     reads=[RS_, Rtops], writes=[Rwk])
            yield
            for hc in range(16):
                S.op("dve", lambda e: e.max(out=tops[:, hc, 8:16], in_=wk[:, hc, :]), reads=[Rwk], writes=[Rtops])
            yield
            for hc in range(16):
                S.op("dve", lambda e: e.max_index(out=tidx[:, hc, 8:16], in_max=tops[:, hc, 8:16], in_values=wk[:, hc, :]),
                     reads=[Rwk, Rtops], writes=[Rtidx])
            yield
            S.op("dve", lambda e: e.tensor_copy(tif[:], tidx[:]), reads=[Rtidx], writes=[Rtif])
            tops4 = tops[:].rearrange("p (h c) k -> p h c k", c=2)
            tif4 = tif[:].rearrange("p (h c) k -> p h c k", c=2)
            S.op("dve", lambda e: e.tensor_tensor(cand.rearrange("p h (a b) -> p h a b", a=16),
                                                 tops4[:, :, 0, :].unsqueeze(3).broadcast_to([128, 8, 16, 16]),
                                                 tops4[:, :, 1, :].unsqueeze(2).broadcast_to([128, 8, 16, 16]), op=ALU.add),
                 reads=[Rtops], writes=[Rcand])
            yield
            for h in range(8):
                S.op("dve", lambda e: e.max(out=best[:, h, 0:8], in_=cand[:, h, :]), reads=[Rcand], writes=[Rbest])
            yield
            for h in range(8):
                S.op("dve", lambda e: e.max_index(out=bidx[:, h, 0:8], in_max=best[:, h, 0:8], in_values=cand[:, h, :]),
                     reads=[Rcand, Rbest], writes=[Rbidx])
            for h in range(8):
                S.op("dve", lambda e: e.match_replace(out=wk2[:, h, :], in_to_replace=best[:, h, 0:8], in_values=cand[:, h, :], imm_value=-1e30),
                     reads=[Rcand, Rbest], writes=[Rwk2])
            yield
            for h in range(8):
                S.op("dve", lambda e: e.max(out=best[:, h, 8:16], in_=wk2[:, h, :]), reads=[Rwk2], writes=[Rbest])
            yield
            for h in range(8):
                S.op("dve", lambda e: e.max_index(out=bidx[:, h, 8:16], in_max=best[:, h, 8:16], in_values=wk2[:, h, :]),
                     reads=[Rwk2, Rbest], writes=[Rbidx])
            S.op("dve", lambda e: e.tensor_tensor(ge[:], best[:], best[:, :, 0:1].broadcast_to([128, 8, 16]), op=ALU.subtract),
                 reads=[Rbest], writes=[Rge])
            yield
            S.op("act", lambda e: e.activation(out=ge[:], in_=ge[:], func=AF.Exp), reads=[Rge], writes=[Rge])
            S.op("dve", lambda e: e.tensor_single_scalar(ab_u[:, 0, :, :], bidx[:], 4, op=ALU.logical_shift_right), reads=[Rbidx], writes=[Rab])
            S.op("dve", lambda e: e.tensor_single_scalar(ab_u[:, 1, :, :], bidx[:], 15, op=ALU.bitwise_and), reads=[Rbidx], writes=[Rab])
            yield
            S.op("dve", lambda e: e.tensor_copy(ab_f[:], ab_u[:]), reads=[Rab], writes=[Rab])
            S.op("dve", lambda e: e.reduce_sum(gs[:], ge[:], axis=AX.X), reads=[Rge], writes=[Rge])
            yield
            S.op("dve", lambda e: e.reciprocal(gs[:], gs[:]), reads=[Rge], writes=[Rge])
            yield
            S.op("dve", lambda e: e.tensor_tensor(Rt[:, 2, :].rearrange("p (h k) -> p h k", h=8), ge[:],
                                                 gs[:].unsqueeze(2).broadcast_to([128, 8, 16]), op=ALU.mult),
                 reads=[Rge], writes=[RRt])
            io16 = iota_f[:, 0:16].unsqueeze(1).unsqueeze(1).broadcast_to([128, 8, 16, 16])
            for c in range(2):
                EE, REE = (B.E0, B.RE0) if c == 0 else (B.E1, B.RE1)
                S.op("dve", lambda e: e.tensor_tensor(EE[:], io16, ab_f[:, c, :, :].unsqueeze(3).broadcast_to([128, 8, 16, 16]), op=ALU.is_equal),
                     reads=[Rab, RC], writes=[REE])
                S.op("pool", lambda e: e.tensor_tensor(EE[:], EE[:], tif4[:, :, c, :].unsqueeze(2).broadcast_to([128, 8, 16, 16]), op=ALU.mult),
                     reads=[REE, Rtif], writes=[REE])
                yield
                S.op("dve", lambda e: e.reduce_sum(Rt[:, c, :].rearrange("p (h k) -> p h k", h=8), EE[:], axis=AX.X),
                     reads=[REE], writes=[RRt])
                yield
            for c in range(3):
                S.op("pe", lambda e: e.transpose(PS[:, 6, c * 128:(c + 1) * 128], Rt[:, c, :], ident_f[:]), reads=[RRt, RC], writes=[RB[6]])
            RT_, RRT, dRT = RTr.get()
            S.op("act", lambda e: e.copy(RT_[:], PS[:, 6, 0:384].rearrange("p (c t) -> p c t", c=3)), reads=[RB[6]], writes=[RRT])
            tb = t0 + a * 128
            S.dma(rS[:, :, tb:tb + 128].rearrange("c p t -> p c t"), RT_[:], reads=[RRT], dsem=dRT)

        def front(ti):
            t0, p0, pt = tiles[ti]
            qpT = qpTs[ti % 2]
            RqpT = RqpTs[ti % 2]
            hT, RH, dH = hTr.get()
            for a in range(4):
                bi = ti * 4 + a
                xt_, Rx, dX = xr.get()
                S.dma(xt_[:], x2S[t0 + a * 128:t0 + (a + 1) * 128, :], writes=[Rx], dsem=dX)
                hb, Rhb, _ = hbr.get()
                col = bi % 8
                rms_block(xt_[:], Rx, g2b, Rg2, hb[:], Rhb, junk, Rjunk, ssq, Rssq[col], col)
                bank = 6
                ptb = PS[:, bank, :].bitcast(BF16)
                for kc in range(8):
                    S.op("pe", lambda e: e.transpose(ptb[:, kc * 128:(kc + 1) * 128], hb[:, kc * 128:(kc + 1) * 128], ident_bf[:]),
                         reads=[Rhb], writes=[RB[bank]])
                S.op("act", lambda e: e.copy(hT[:, :, a * 128:(a + 1) * 128], ptb.rearrange("p (k t) -> p k t", k=8)),
                     reads=[RB[bank]], writes=[RH])
            S.dma(h2S[:, t0:t0 + 512].rearrange("(k p) t -> p k t", p=128), hT[:], reads=[RH], dsem=dH)
            for hc in range(16):
                bk = 4 + (hc % 2)
                for kc in range(8):
                    mm(PS[:, bk, :], wq[:, kc, hc * 128:(hc + 1) * 128], hT[:, kc, :], kc == 0, kc == 7, [Rwq, RH], [RB[bk]])
                S.op("act", lambda e: e.copy(qpT[:, hc, :], PS[:, bk, :]), reads=[RB[bk]], writes=[RqpT])

        def run_gens(gens):
            alive = [True] * len(gens)
            while any(alive):
                for gi in range(len(gens)):
                    if alive[gi]:
                        try:
                            next(gens[gi])
                        except StopIteration:
                            alive[gi] = False

        front(0)
        NTP_STEPS = (NI + len(tiles) - 1) // len(tiles)
        for ti, (t0, p0, pt) in enumerate(tiles):
            qa, Rqa = qpTs[ti % 2], RqpTs[ti % 2]
            run_gens([routing(0, BSETS[0], t0, qa, Rqa), routing(1, BSETS[1], t0, qa, Rqa)])
            g23 = [routing(2, BSETS[0], t0, qa, Rqa), routing(3, BSETS[1], t0, qa, Rqa)]
            for g_ in g23:
                next(g_)
            if ti + 1 < len(tiles):
                front(ti + 1)
            for _ in range(NTP_STEPS):
                tp_step()
            run_gens(g23)
        while tp_done[0] < NI:
            tp_step()
    barrier()

    TT = 256
    with ExitStack() as st:
        gfb, Rgf = load_small(st, "gfb", gfb_in, [128, D])
        WTs = [sb(st, "WT%d" % i, [128, TT, 128], BF16) for i in range(2)]
        RWTs = [Res("WT%d" % i) for i in range(2)]
        rtr = Rot(S, st, nc, "rt4", [128, 3, TT], F32, 1)
        h2r = Rot(S, st, nc, "h2T4", [128, 8, TT], BF16, 2)
        x2r = Rot(S, st, nc, "x24", [128, D], F32, 1)
        rtr_n = 2
        TB = 8
        Lr = Rot(S, st, nc, "Lb", [128, TB, 128], BF16, 2)
        Rr = Rot(S, st, nc, "Rb", [128, TB, 128], BF16, 2)
        Ubr = Rot(S, st, nc, "Ub", [128, 2, D], BF16, 3)
        Vbr = Rot(S, st, nc, "Vb", [128, 2, D], BF16, 4)
        Asb = Rot(S, st, nc, "Asb", [128, 2, TT], BF16, 2)
        WAr = Rot(S, st, nc, "WA", [128, 2, TT], BF16, 3)
        yo = Rot(S, st, nc, "yo", [128, D], F32, 1)
        ssq = sb(st, "ssq4", [128, 8], F32)
        Rssq = [Res("ssq4_%d" % i) for i in range(8)]
        ntile = NT // TT
        NP = 64
        NTB = TT // TB
        PER = NP // NTB
        pend = {}

        def p4_load(ti):
            tb = ti * TT
            r_ = rtr.get()
            S.dma(r_[0][:], rS[:, :, tb:tb + TT].rearrange("c p t -> p c t"), writes=[r_[1]], dsem=r_[2])
            h_ = h2r.get()
            S.dma(h_[0][:], h2S[:, tb:tb + TT].rearrange("(k p) t -> p k t", p=128), writes=[h_[1]], dsem=h_[2])
            pend[ti] = (r_, h_)

        chunkq = {}
        cnext = [0]
        total = ntile * NP

        def p4_chunk_prefetch(upto):
            while cnext[0] <= upto and cnext[0] < total:
                ip = cnext[0] % NP
                u_ = Ubr.get()
                v_ = Vbr.get()
                S.dma(u_[0][:], UT[2 * ip:2 * ip + 2].rearrange("i p f -> p i f"), writes=[u_[1]], dsem=u_[2])
                S.dma(v_[0][:], VB[2 * ip * 128:(2 * ip + 2) * 128, :].rearrange("(i p) f -> p i f", p=128), writes=[v_[1]], dsem=v_[2])
                chunkq[cnext[0]] = (u_, v_)
                cnext[0] += 1

        iota_b = sb(st, "iota_b", [128, 128], BF16)
        S.op("dve", lambda e: e.tensor_copy(iota_b[:], iota_f[:]), reads=[RC], writes=[RC])
        io = iota_b[:].unsqueeze(1).broadcast_to([128, TB, 128])
        rtb = Rot(S, st, nc, "rtb", [128, 3, TT], BF16, 1)
        rtb_of = {}
        evn = [0]

        def onehot(ti, tb):
            if ti not in rtb_of:
                rt32, Rrt32, _ = pend[ti][0]
                rb_ = rtb.get()
                S.op("dve", lambda e: e.tensor_copy(rb_[0][:], rt32[:]), reads=[Rrt32], writes=[rb_[1]])
                rtb_of.clear()
                rtb_of[ti] = rb_
            rt_, Rrt, _ = rtb_of[ti]
            Lb, RL, _ = Lr.get()
            Rb, RR, _ = Rr.get()
            S.op("dve", lambda e: e.tensor_tensor(Lb[:], io, rt_[:, 0, tb * TB:(tb + 1) * TB].unsqueeze(2).broadcast_to([128, TB, 128]), op=ALU.is_equal),
                 reads=[Rrt, RC], writes=[RL])
            S.op("dve", lambda e: e.tensor_tensor(Lb[:], Lb[:], rt_[:, 2, tb * TB:(tb + 1) * TB].unsqueeze(2).broadcast_to([128, TB, 128]), op=ALU.mult),
                 reads=[Rrt, RL], writes=[RL])
            S.op("dve", lambda e: e.tensor_tensor(Rb[:], io, rt_[:, 1, tb * TB:(tb + 1) * TB].unsqueeze(2).broadcast_to([128, TB, 128]), op=ALU.is_equal),
                 reads=[Rrt, RC], writes=[RR])
            return (Lb, RL, Rb, RR)

        def wmm(ti, tb, LR):
            Lb, RL, Rb, RR = LR
            WT = WTs[ti % 2]
            RWT = RWTs[ti % 2]
            for q4 in range(TB // 4):
                bk = 6 + (evn[0] % 2)
                evn[0] += 1
                pw = PS[:, bk, :].rearrange("p (t i) -> p t i", t=4)
                for tq in range(4):
                    tl = q4 * 4 + tq
                    mm(pw[:, tq, :], Rb[:, tl, :], Lb[:, tl, :], True, True, [RL, RR], [RB[bk]])
                tg = tb * TB + q4 * 4
                S.op("act", lambda e: e.copy(WT[:, tg:tg + 4, :], pw), reads=[RB[bk]], writes=[RWT])

        def u_mm(ci):
            ti, ip = divmod(ci, NP)
            h2T, Rh2, _ = pend[ti][1]
            (Ub, RU, _), _v = chunkq[ci]
            bk = 4 + (ci % 2)
            pa = PS[:, bk, :].rearrange("p (i t) -> p i t", i=2)
            for ii in range(2):
                for kc in range(8):
                    mm(pa[:, ii, :], Ub[:, ii, kc * 128:(kc + 1) * 128], h2T[:, kc, :], kc == 0, kc == 7, [RU, Rh2], [RB[bk]])

        xo = Rot(S, st, nc, "xo4", [128, D], F32, 1)

        def v_mm(ci, WA, RWA):
            ti, ip = divmod(ci, NP)
            _u, (Vb, RVb, _) = chunkq.pop(ci)
            for ii in range(2):
                for th in range(2):
                    for n in range(2):
                        bo = th * 2 + n
                        mm(PS[:, bo, :], WA[:, ii, th * 128:(th + 1) * 128], Vb[:, ii, n * 512:(n + 1) * 512],
                           ip == 0 and ii == 0, ip == NP - 1 and ii == 1, [RWA, RVb], [RB[bo]])

        def finalize(ti):
            for th in range(2):
                tb_ = ti * TT + th * 128
                xo_, Rxo, _ = xo.get()
                S.op("act", lambda e: e.copy(xo_[:].rearrange("p (n f) -> p n f", n=2), PS[:, 2 * th:2 * th + 2, :]),
                     reads=[RB[2 * th], RB[2 * th + 1]], writes=[Rxo])
                o, Ro, dX = x2r.get()
                S.dma(o[:], x2S[tb_:tb_ + 128, :], writes=[Ro], dsem=dX)
                S.op("pool", lambda e: e.tensor_tensor(o[:], o[:], xo_[:], op=ALU.add), reads=[Rxo, Ro], writes=[Ro])
                y_, Ry, dY = yo.get()
                col = (ti * 2 + th) % 8
                S.op("act", lambda e: e.activation(out=y_[:], in_=o[:], func=AF.Square, accum_out=ssq[:, col:col + 1]),
                     reads=[Ro], writes=[Ry, Rssq[col]])
                S.op("act", lambda e: e.activation(out=ssq[:, col:col + 1], in_=ssq[:, col:col + 1], func=AF.Sqrt, bias=epsc[:], scale=1.0 / D),
                     reads=[Rssq[col], RC], writes=[Rssq[col]])
                S.op("dve", lambda e: e.reciprocal(ssq[:, col:col + 1], ssq[:, col:col + 1]), reads=[Rssq[col]], writes=[Rssq[col]])
                S.op("dve", lambda e: e.scalar_tensor_tensor(y_[:], o[:], ssq[:, col:col + 1], gfb[:], op0=ALU.mult, op1=ALU.mult),
                     reads=[Ro, Rssq[col], Rgf], writes=[Ry])
                S.dma(y_out[tb_:tb_ + 128, :], y_[:], reads=[Ry], dsem=dY)

        p4_load(0)
        p4_chunk_prefetch(2)
        for tb in range(NTB):
            wmm(0, tb, onehot(0, tb))
        u_mm(0)
        pendLR = None
        pendV = None
        for ti in range(ntile):
            if ti + 1 < ntile:
                p4_load(ti + 1)
            WT = WTs[ti % 2]
            RWT = RWTs[ti % 2]
            for ip in range(NP):
                ci = ti * NP + ip
                p4_chunk_prefetch(ci + 2)
                if ti + 1 < ntile and ip % PER == 0:
                    if pendLR is not None:
                        wmm(ti + 1, ip // PER - 1, pendLR)
                    pendLR = onehot(ti + 1, ip // PER)
                if ci + 1 < total:
                    u_mm(ci + 1)
                bk = 4 + (ci % 2)
                pa = PS[:, bk, :].rearrange("p (i t) -> p i t", i=2)
                A_, RA_, _ = Asb.get()
                S.op("act", lambda e: e.activation(out=A_[:], in_=pa, func=AF.Gelu), reads=[RB[bk]], writes=[RA_])
                WA, RWA, _ = WAr.get()
                S.op("pool", lambda e: e.tensor_tensor(WA[:], A_[:], WT[:, :, 2 * ip:2 * ip + 2].rearrange("p t i -> p i t"), op=ALU.mult),
                     reads=[RA_, RWT], writes=[RWA])
                if pendV is not None:
                    pci, pWA, pRWA = pendV
                    v_mm(pci, pWA, pRWA)
                    if pci % NP == NP - 1:
                        finalize(pci // NP)
                        pend.pop(pci // NP)
                pendV = (ci, WA, RWA)
            if pendLR is not None:
                wmm(ti + 1, NTB - 1, pendLR)
                pendLR = None
        pci, pWA, pRWA = pendV
        v_mm(pci, pWA, pRWA)
        finalize(pci // NP)
    barrier()
    top.close()
    return nc


def rope_tables(smax=8192):
    half = 8
    inv = (np.float32(500000.0) ** (-np.arange(half, dtype=np.float32) * np.float32(2.0) / np.float32(16))).astype(np.float32)
    pos = np.arange(smax, dtype=np.float32)
    ang = (pos[:, None] * inv[None, :]).astype(np.float32)
    cos = np.cos(ang).astype(np.float32).T
    sin = np.sin(ang).astype(np.float32).T
    C = np.ones((128, smax), np.float32)
    Sn = np.zeros((128, smax), np.float32)
    for hh in range(2):
        b = hh * 64
        C[b:b + 8] = cos
        C[b + 8:b + 16] = cos
        Sn[b:b + 8] = sin
        Sn[b + 8:b + 16] = sin
    return C, Sn


def make_shared(inp):
    f = lambda a: np.ascontiguousarray(np.asarray(a, dtype=np.float32))
    C, Sn = rope_tables()
    sh = {
        "w_in": f(inp["w_in"][0]),
        "g1b": f(np.broadcast_to(inp["norm1_g"][0][None, :], (128, D))),
        "g2b": f(np.broadcast_to(inp["norm2_g"][0][None, :], (128, D))),
        "gfb": f(np.broadcast_to(np.asarray(inp["final_g"])[None, :], (128, D))),
        "bgate": f(np.asarray(inp["b_gate"][0]).reshape(16, 128).T),
        "w_up": f(inp["w_attn_up"][0]),
        "dww": f(np.asarray(inp["conv_dw_w"][0]).reshape(31, 4, 128).transpose(2, 1, 0)),
        "cpar": f(np.stack([np.asarray(inp["conv_dw_b"][0]).reshape(4, 128).T,
                            np.asarray(inp["conv_ln_g"][0]).reshape(4, 128).T,
                            np.asarray(inp["conv_ln_b"][0]).reshape(4, 128).T], axis=1)),
        "pw_w": f(inp["conv_pw_w"][0]),
        "pwb": f(np.asarray(inp["conv_pw_b"][0]).reshape(8, 128).T),
        "w_out": f(inp["w_out"][0]),
        "wq": f(inp["peer_wq"][0]),
        "keysT": f(np.asarray(inp["peer_keys"][0]).reshape(16, 128, 128).transpose(2, 0, 1)),
        "peer_u": f(inp["peer_u"][0]),
        "peer_v": f(inp["peer_v"][0]),
        "rope_c": C,
        "rope_s": Sn,
        "dummy8": np.zeros((1, 8), np.float32),
    }
    return sh


def kernel(**inputs):
    xp = np.asarray(inputs["x_prompt"], dtype=np.float32)
    xs = np.asarray(inputs["x_sample"], dtype=np.float32)
    seqs = [xp.shape[1], xs.shape[1], xs.shape[1]]
    nc = build_program(seqs)
    sh = make_shared(inputs)
    in_maps = []
    for c in range(N_CORES):
        m = dict(sh)
        m["x"] = np.ascontiguousarray(np.concatenate([xp[c], xs[2 * c], xs[2 * c + 1]], axis=0))
        in_maps.append(m)
    res = run_bass_kernel_spmd(nc, in_maps, core_ids=list(range(N_CORES)))
    yp = np.empty_like(xp)
    ys = np.empty_like(xs)
    Lp, Ls = seqs[0], seqs[1]
    for c in range(N_CORES):
        y = np.asarray(res.results[c]["y"])
        yp[c] = y[0:Lp]
        ys[2 * c] = y[Lp:Lp + Ls]
        ys[2 * c + 1] = y[Lp + Ls:Lp + 2 * Ls]
    return (yp, ys)
```
